# Optimizing a Trainium2 kernel written in Bass

```python
import math
import jax, jax.numpy as jnp
from jax import lax
import numpy as np

D_MODEL = 1024
BATCH = 8
SEQ = 4096
DEPTH = 2

GDN_HEADS = 4
GDN_DK = 128
GDN_DV = 128
GDN_CONV = 4
RET_HEADS = 4
RET_DK = 64
RET_DV = 64
RWKV_HEADS = 4
RWKV_N = 64
RWKV_W_LORA = 32
RWKV_A_LORA = 32
RWKV_V_LORA = 32
RWKV_G_LORA = 64
D_FF = 4 * D_MODEL
CHUNK = 64
ROPE_BASE = 10000.0
NORM_EPS = 1e-6
L2_EPS = 1e-6
RET_GN_EPS = 1e-6
RWKV_GN_EPS = 64e-5

GDN_QK = GDN_HEADS * GDN_DK
GDN_V = GDN_HEADS * GDN_DV
GDN_CONV_DIM = 2 * GDN_QK + GDN_V
RET_QK = RET_HEADS * RET_DK
RET_V = RET_HEADS * RET_DV
RWKV_C = RWKV_HEADS * RWKV_N
D_MIX = GDN_V + RET_V + RWKV_C
GDN_IN = 2 * GDN_QK + 2 * GDN_V + 2 * GDN_HEADS
RET_IN = 2 * RET_QK + 2 * RET_V
RWKV_IN = 3 * RWKV_C + RWKV_W_LORA + RWKV_A_LORA + RWKV_G_LORA
D_IN = GDN_IN + RET_IN + RWKV_IN

kernel_name = 'hymba_style_gdn_retnet_rwkv7_adaln'


def _split(x, sizes):
    idx = [int(i) for i in np.cumsum(sizes)[:-1]]
    return jnp.split(x, idx, axis=-1)


def _rmsnorm(x, w):
    xf = x.astype(jnp.float32)
    y = xf * lax.rsqrt(jnp.mean(xf * xf, axis=-1, keepdims=True) + NORM_EPS)
    return (y * w.astype(jnp.float32)).astype(x.dtype)


def _l2norm(x):
    return x * lax.rsqrt(jnp.sum(x * x, axis=-1, keepdims=True) + L2_EPS)


def _head_layernorm(x, eps):
    mu = jnp.mean(x, axis=-1, keepdims=True)
    xc = x - mu
    return xc * lax.rsqrt(jnp.mean(xc * xc, axis=-1, keepdims=True) + eps)


def _causal_depthwise_conv(x, w):
    K, C = w.shape
    return lax.conv_general_dilated(
        x, w[:, None, :].astype(x.dtype), window_strides=(1,), padding=[(K - 1, 0)],
        dimension_numbers=('NWC', 'WIO', 'NWC'), feature_group_count=C)


def _rotary(x, cos, sin):
    half = x.shape[-1] // 2
    x1, x2 = x[..., :half], x[..., half:]
    return jnp.concatenate([x1 * cos - x2 * sin, x2 * cos + x1 * sin], axis=-1)


def _to_chunks(x):
    B, T, H, D = x.shape
    return x.reshape(B, T // CHUNK, CHUNK, H, D).transpose(0, 3, 1, 2, 4)


def _from_chunks(x):
    B, H, N, C, D = x.shape
    return x.transpose(0, 2, 3, 1, 4).reshape(B, N * C, H, D)


def _gated_delta_rule(q, k, v, g, beta):
    B, T, H, DK = q.shape
    DV = v.shape[-1]
    N = T // CHUNK
    q = _to_chunks(q * DK ** -0.5)
    k = _to_chunks(k)
    v = _to_chunks(v)
    g = g.reshape(B, N, CHUNK, H).transpose(0, 3, 1, 2)
    beta = beta.reshape(B, N, CHUNK, H).transpose(0, 3, 1, 2)[..., None]
    gc = jnp.cumsum(g, axis=-1)
    idx = jnp.arange(CHUNK)
    incl = idx[:, None] >= idx[None, :]
    strict = idx[:, None] > idx[None, :]
    diff = gc[..., :, None] - gc[..., None, :]
    decay = jnp.where(incl, jnp.exp(jnp.where(incl, diff, 0.0)), 0.0)
    k_beta = k * beta
    lower = jnp.einsum('bhncd,bhnsd->bhncs', k_beta, k) * jnp.where(strict, decay, 0.0)
    rhs = jnp.concatenate([v * beta, k_beta * jnp.exp(gc)[..., None]], axis=-1)
    sol = lax.linalg.triangular_solve(lower, rhs, left_side=True, lower=True, unit_diagonal=True)
    u, w = sol[..., :DV], sol[..., DV:]
    attn = jnp.einsum('bhncd,bhnsd->bhncs', q, k) * decay
    q_dec = q * jnp.exp(gc)[..., None]
    g_last = gc[..., -1]
    k_dec = k * jnp.exp(g_last[..., None] - gc)[..., None]
    xs = (jnp.moveaxis(u, 2, 0), jnp.moveaxis(w, 2, 0), jnp.moveaxis(q_dec, 2, 0),
          jnp.moveaxis(attn, 2, 0), jnp.moveaxis(k_dec, 2, 0), jnp.moveaxis(jnp.exp(g_last), 2, 0))

    def step(S, inp):
        u_n, w_n, q_n, a_n, k_n, d_n = inp
        v_new = u_n - jnp.einsum('bhcd,bhde->bhce', w_n, S)
        o = jnp.einsum('bhcd,bhde->bhce', q_n, S) + jnp.einsum('bhcs,bhse->bhce', a_n, v_new)
        S = S * d_n[..., None, None] + jnp.einsum('bhcd,bhce->bhde', k_n, v_new)
        return S, o

    S0 = jnp.zeros((B, H, DK, DV), jnp.float32)
    _, o = lax.scan(step, S0, xs)
    return _from_chunks(jnp.moveaxis(o, 0, 2))


def _retention(q, k, v, log_gamma):
    B, T, H, DK = q.shape
    DV = v.shape[-1]
    q = _to_chunks(q)
    k = _to_chunks(k)
    v = _to_chunks(v)
    idx = jnp.arange(CHUNK, dtype=jnp.float32)
    rel = idx[:, None] - idx[None, :]
    dmask = jnp.where(rel >= 0, jnp.exp(jnp.maximum(rel, 0.0) * log_gamma[:, None, None]), 0.0)
    scores = jnp.einsum('bhncd,bhnsd->bhncs', q, k) * dmask[:, None]
    inner = jnp.einsum('bhncs,bhnse->bhnce', scores, v)
    q_dec = q * jnp.exp((idx + 1.0)[None, :] * log_gamma[:, None])[:, None, :, None]
    k_dec = k * jnp.exp((CHUNK - 1.0 - idx)[None, :] * log_gamma[:, None])[:, None, :, None]
    chunk_decay = jnp.exp(CHUNK * log_gamma)[:, None, None]

    def step(R, inp):
        q_n, k_n, v_n = inp
        o = jnp.einsum('bhcd,bhde->bhce', q_n, R)
        R = R * chunk_decay + jnp.einsum('bhcd,bhce->bhde', k_n, v_n)
        return R, o

    R0 = jnp.zeros((B, H, DK, DV), jnp.float32)
    _, cross = lax.scan(step, R0, (jnp.moveaxis(q_dec, 2, 0), jnp.moveaxis(k_dec, 2, 0), jnp.moveaxis(v, 2, 0)))
    return _from_chunks(inner + jnp.moveaxis(cross, 0, 2))


def _rwkv7_recurrence(r, w, k, v, a, b):
    B, T, H, N = r.shape
    xs = (jnp.moveaxis(r, 1, 0), jnp.moveaxis(w, 1, 0), jnp.moveaxis(k, 1, 0),
          jnp.moveaxis(v, 1, 0), jnp.moveaxis(a, 1, 0), jnp.moveaxis(b, 1, 0))

    def step(S, inp):
        r_t, w_t, k_t, v_t, a_t, b_t = inp
        sa = jnp.einsum('bhvk,bhk->bhv', S, a_t)
        S = S * w_t[:, :, None, :] + sa[..., None] * b_t[:, :, None, :] + v_t[..., None] * k_t[:, :, None, :]
        return S, jnp.einsum('bhvk,bhk->bhv', S, r_t)

    S0 = jnp.zeros((B, H, N, N), jnp.float32)
    _, y = lax.scan(step, S0, xs)
    return jnp.moveaxis(y, 0, 1)


def _gdn_mixer(p, conv_w, a_log, dt_bias, norm_w):
    B, T, _ = p.shape
    qkv, z, b, a = _split(p, (GDN_CONV_DIM, GDN_V, GDN_HEADS, GDN_HEADS))
    qkv = jax.nn.silu(_causal_depthwise_conv(qkv, conv_w)).astype(jnp.float32)
    q, k, v = _split(qkv, (GDN_QK, GDN_QK, GDN_V))
    q = _l2norm(q.reshape(B, T, GDN_HEADS, GDN_DK))
    k = _l2norm(k.reshape(B, T, GDN_HEADS, GDN_DK))
    v = v.reshape(B, T, GDN_HEADS, GDN_DV)
    beta = jax.nn.sigmoid(b.astype(jnp.float32))
    g = -jnp.exp(a_log.astype(jnp.float32)) * jax.nn.softplus(a.astype(jnp.float32) + dt_bias.astype(jnp.float32))
    o = _gated_delta_rule(q, k, v, g, beta)
    o = _rmsnorm(o, norm_w) * jax.nn.silu(z.astype(jnp.float32).reshape(B, T, GDN_HEADS, GDN_DV))
    return o.reshape(B, T, GDN_V)


def _retention_mixer(p, rope_cos, rope_sin):
    B, T, _ = p.shape
    q, k, v, gate = _split(p.astype(jnp.float32), (RET_QK, RET_QK, RET_V, RET_V))
    q = _rotary(q.reshape(B, T, RET_HEADS, RET_DK), rope_cos, rope_sin)
    k = _rotary(k.reshape(B, T, RET_HEADS, RET_DK), rope_cos, rope_sin) * RET_DK ** -0.5
    log_gamma = jnp.log(1.0 - jnp.power(2.0, -5.0 - jnp.arange(RET_HEADS, dtype=jnp.float32)))
    o = _retention(q, k, v.reshape(B, T, RET_HEADS, RET_DV), log_gamma)
    o = _head_layernorm(o, RET_GN_EPS) * jax.nn.silu(gate.reshape(B, T, RET_HEADS, RET_DV))
    return o.reshape(B, T, RET_V)


def _rwkv7_mixer(p, v_first, mu, w0, w2, a0, a2, g2, k_k, k_a, r_k, ln_w, ln_b, v0, v1, v2):
    B, T, _ = p.shape
    p = p.astype(jnp.float32)
    p_prev = jnp.pad(p, ((0, 0), (1, 0), (0, 0)))[:, :-1]
    p = p + (p_prev - p) * mu
    r, k, v, wd, ad, gd = _split(p, (RWKV_C, RWKV_C, RWKV_C, RWKV_W_LORA, RWKV_A_LORA, RWKV_G_LORA))
    w_log = -jax.nn.softplus(-(w0 + jnp.tanh(wd) @ w2)) - 0.5
    decay = jnp.exp(-jnp.exp(w_log))
    a = jax.nn.sigmoid(a0 + ad @ a2)
    g = jax.nn.sigmoid(gd) @ g2
    if v_first is None:
        v_first = v
    else:
        v = v + (v_first - v) * jax.nn.sigmoid(v0 + (v @ v1) @ v2)
    hs = lambda t: t.reshape(B, T, RWKV_HEADS, RWKV_N)
    kk = _l2norm(hs(k * k_k))
    k = k * (1.0 + (a - 1.0) * k_a)
    r_h, k_h, v_h, a_h = hs(r), hs(k), hs(v), hs(a)
    y = _rwkv7_recurrence(r_h, hs(decay), k_h, v_h, -kk, kk * a_h)
    y = _head_layernorm(y, RWKV_GN_EPS).reshape(B, T, RWKV_C) * ln_w + ln_b
    bonus = jnp.sum(r_h * k_h * r_k, axis=-1, keepdims=True) * v_h
    out = (y + bonus.reshape(B, T, RWKV_C)) * g
    return out, v_first


def setup_inputs(seed: int = 0) -> dict:
    key = jax.random.key(seed)
    ks = jax.random.split(key, 32)
    f32 = jnp.float32
    L = DEPTH

    def nrm(k, shape, scale):
        return jax.random.normal(k, shape, f32) * scale

    def gain(k, shape):
        return 1.0 + 0.02 * jax.random.normal(k, shape, f32)

    x = nrm(ks[0], (BATCH, SEQ, D_MODEL), 1.0)
    c = nrm(ks[1], (BATCH, D_MODEL), 1.0)
    ada_w = nrm(ks[2], (L, D_MODEL, 6 * D_MODEL), 0.5 * D_MODEL ** -0.5)
    ada_b = nrm(ks[3], (L, 6 * D_MODEL), 0.01)
    norm1_w = gain(ks[4], (L, D_MODEL))
    norm2_w = gain(ks[5], (L, D_MODEL))
    w_in = nrm(ks[6], (L, D_MODEL, D_IN), D_MODEL ** -0.5)
    gdn_conv_w = nrm(ks[7], (L, GDN_CONV, GDN_CONV_DIM), GDN_CONV ** -0.5)
    gdn_a_log = jnp.log(jax.random.uniform(ks[8], (L, GDN_HEADS), f32, minval=1.0, maxval=16.0))
    dt = jnp.exp(jax.random.uniform(ks[9], (L, GDN_HEADS), f32, minval=math.log(1e-3), maxval=math.log(1e-1)))
    gdn_dt_bias = dt + jnp.log(-jnp.expm1(-dt))
    gdn_norm_w = gain(ks[10], (L, GDN_DV))
    rwkv_mu = jax.random.uniform(ks[11], (L, RWKV_IN), f32)
    rwkv_w0 = jax.random.uniform(ks[12], (L, RWKV_C), f32, minval=-6.5, maxval=-1.5)
    rwkv_w2 = nrm(ks[13], (L, RWKV_W_LORA, RWKV_C), 0.1 * RWKV_W_LORA ** -0.5)
    rwkv_a0 = nrm(ks[14], (L, RWKV_C), 0.1)
    rwkv_a2 = nrm(ks[15], (L, RWKV_A_LORA, RWKV_C), 0.5 * RWKV_A_LORA ** -0.5)
    rwkv_g2 = nrm(ks[16], (L, RWKV_G_LORA, RWKV_C), RWKV_G_LORA ** -0.5)
    rwkv_k_k = 0.85 + nrm(ks[17], (L, RWKV_C), 0.05)
    rwkv_k_a = 1.0 + nrm(ks[18], (L, RWKV_C), 0.05)
    rwkv_r_k = nrm(ks[19], (L, RWKV_HEADS, RWKV_N), 0.1)
    rwkv_ln_w = gain(ks[20], (L, RWKV_C))
    rwkv_ln_b = nrm(ks[21], (L, RWKV_C), 0.01)
    rwkv_v0 = 1.0 + nrm(ks[22], (L - 1, RWKV_C), 0.05)
    rwkv_v1 = nrm(ks[23], (L - 1, RWKV_C, RWKV_V_LORA), 0.5 * RWKV_C ** -0.5)
    rwkv_v2 = nrm(ks[24], (L - 1, RWKV_V_LORA, RWKV_C), 0.5 * RWKV_V_LORA ** -0.5)
    w_out = nrm(ks[25], (L, D_MIX, D_MODEL), D_MIX ** -0.5)
    w_up = nrm(ks[26], (L, D_MODEL, D_FF), D_MODEL ** -0.5)
    w_down = nrm(ks[27], (L, D_FF, D_MODEL), D_FF ** -0.5)
    final_norm_w = gain(ks[28], (D_MODEL,))
    return {'x': x, 'c': c, 'ada_w': ada_w, 'ada_b': ada_b, 'norm1_w': norm1_w, 'norm2_w': norm2_w,
            'w_in': w_in, 'gdn_conv_w': gdn_conv_w, 'gdn_a_log': gdn_a_log, 'gdn_dt_bias': gdn_dt_bias,
            'gdn_norm_w': gdn_norm_w, 'rwkv_mu': rwkv_mu, 'rwkv_w0': rwkv_w0, 'rwkv_w2': rwkv_w2,
            'rwkv_a0': rwkv_a0, 'rwkv_a2': rwkv_a2, 'rwkv_g2': rwkv_g2, 'rwkv_k_k': rwkv_k_k,
            'rwkv_k_a': rwkv_k_a, 'rwkv_r_k': rwkv_r_k, 'rwkv_ln_w': rwkv_ln_w, 'rwkv_ln_b': rwkv_ln_b,
            'rwkv_v0': rwkv_v0, 'rwkv_v1': rwkv_v1, 'rwkv_v2': rwkv_v2, 'w_out': w_out,
            'w_up': w_up, 'w_down': w_down, 'final_norm_w': final_norm_w}


def reference(x, c, ada_w, ada_b, norm1_w, norm2_w, w_in, gdn_conv_w, gdn_a_log, gdn_dt_bias,
              gdn_norm_w, rwkv_mu, rwkv_w0, rwkv_w2, rwkv_a0, rwkv_a2, rwkv_g2, rwkv_k_k,
              rwkv_k_a, rwkv_r_k, rwkv_ln_w, rwkv_ln_b, rwkv_v0, rwkv_v1, rwkv_v2, w_out,
              w_up, w_down, final_norm_w):
    T = x.shape[1]
    pos = jnp.arange(T, dtype=jnp.float32)
    inv_freq = ROPE_BASE ** (-jnp.arange(RET_DK // 2, dtype=jnp.float32) / (RET_DK // 2))
    ang = pos[:, None] * inv_freq[None, :]
    rope_cos = jnp.cos(ang)[:, None, :]
    rope_sin = jnp.sin(ang)[:, None, :]
    cond = jax.nn.silu(c)
    v_first = None
    for l in range(DEPTH):
        mod = cond @ ada_w[l] + ada_b[l]
        sh1, sc1, g1, sh2, sc2, g2 = jnp.split(mod[:, None, :], 6, axis=-1)
        h = _rmsnorm(x, norm1_w[l]) * (1.0 + sc1) + sh1
        p = h @ w_in[l]
        p_gdn, p_ret, p_rwkv = _split(p, (GDN_IN, RET_IN, RWKV_IN))
        o_a = _gdn_mixer(p_gdn, gdn_conv_w[l], gdn_a_log[l], gdn_dt_bias[l], gdn_norm_w[l])
        o_b = _retention_mixer(p_ret, rope_cos, rope_sin)
        if l == 0:
            o_c, v_first = _rwkv7_mixer(p_rwkv, None, rwkv_mu[l], rwkv_w0[l], rwkv_w2[l], rwkv_a0[l],
                                        rwkv_a2[l], rwkv_g2[l], rwkv_k_k[l], rwkv_k_a[l], rwkv_r_k[l],
                                        rwkv_ln_w[l], rwkv_ln_b[l], None, None, None)
        else:
            o_c, v_first = _rwkv7_mixer(p_rwkv, v_first, rwkv_mu[l], rwkv_w0[l], rwkv_w2[l], rwkv_a0[l],
                                        rwkv_a2[l], rwkv_g2[l], rwkv_k_k[l], rwkv_k_a[l], rwkv_r_k[l],
                                        rwkv_ln_w[l], rwkv_ln_b[l], rwkv_v0[l - 1], rwkv_v1[l - 1],
                                        rwkv_v2[l - 1])
        mix = jnp.concatenate([o_a, o_b, o_c], axis=-1).astype(x.dtype)
        x = x + g1 * (mix @ w_out[l])
        h = _rmsnorm(x, norm2_w[l]) * (1.0 + sc2) + sh2
        x = x + g2 * (jnp.square(jax.nn.relu(h @ w_up[l])) @ w_down[l])
    return _rmsnorm(x, final_norm_w)
```

```python
import numpy as np
import concourse.bass as bass
import concourse.mybir as mybir
from concourse.bass_utils import run_bass_kernel_spmd

F32 = mybir.dt.float32
BF16 = mybir.dt.bfloat16
F32R = mybir.dt.float32r
AF = mybir.ActivationFunctionType
ALU = mybir.AluOpType
AX = mybir.AxisListType

T = 4096
D = 1024
DIN = 3976
NCH = 32
NEGM = -100000.0
WSC = -0.6065306597126334
LG = [float(np.log(1.0 - 2.0 ** (-5.0 - h))) for h in range(4)]


SEG = 30000


class Buf:
    __slots__ = ("name", "last_w", "readers", "excl")

    def __init__(self, name):
        self.name = name
        self.excl = False
        self.last_w = None
        self.readers = []


class Op:
    __slots__ = ("stream", "clock", "fn", "deps", "signals", "count", "is_dma")

    def __init__(self, stream, clock, fn, is_dma):
        self.stream = stream
        self.clock = clock
        self.fn = fn
        self.deps = []
        self.signals = is_dma
        self.count = None
        self.is_dma = is_dma


STREAMS = ("pe", "act", "dve", "pool", "sp")


class Prog:
    def __init__(self, nc):
        self.nc = nc
        self.ops = {s: [] for s in STREAMS}
        self.all_ops = []
        self.dma_r = {"sp": 12, "pool": 8, "act": 4}
        self.clocks = ["pe", "act", "dve", "pool"]
        for q, r in self.dma_r.items():
            self.clocks += [f"dma_{q}_{i}" for i in range(r)]
        self.dma_hist = {q: [] for q in self.dma_r}
        self.bufs = []

    def buf(self, name):
        b = Buf(name)
        self.bufs.append(b)
        return b

    def _add(self, stream, clock, fn, reads, writes, is_dma):
        op = Op(stream, clock, fn, is_dma)
        deps = []
        for b in reads:
            if b.last_w is not None:
                deps.append(b.last_w)
            if b.excl:
                for rd in b.readers:
                    if rd.stream != stream:
                        deps.append(rd)
        for b in writes:
            if b.last_w is not None:
                deps.append(b.last_w)
            deps.extend(b.readers)
        seen = set()
        for d in deps:
            if d is op or id(d) in seen:
                continue
            seen.add(id(d))
            if d.stream == "pe" and stream == "pe" and not d.is_dma and not is_dma:
                continue
            op.deps.append(d)
            d.signals = True
        for b in reads:
            b.readers.append(op)
        for b in writes:
            b.last_w = op
            b.readers = []
        self.ops[stream].append(op)
        self.all_ops.append(op)
        return op

    def op(self, eng, fn, reads=(), writes=()):
        return self._add(eng, eng, fn, reads, writes, False)

    def dma(self, q, out, in_, reads=(), writes=(), **kw):
        def fn(e):
            return e.dma_start(out=out, in_=in_, **kw)
        hist = self.dma_hist[q]
        i = len(hist)
        r = self.dma_r[q]
        op = self._add(q, f"dma_{q}_{i % r}", fn, reads, writes, True)
        if i >= r:
            prev = hist[i - r]
            if prev not in op.deps:
                op.deps.append(prev)
        hist.append(op)
        return op

    def barrier(self):
        last = {}
        for op in self.all_ops:
            if op.fn != "nop":
                last[op.clock] = op
        b = self.buf("barrier")
        for c, op in last.items():
            op.signals = True
        self._barrier_deps = list(last.values())
        for s in STREAMS:
            o = Op(s, s, None, False)
            o.deps = [d for d in last.values()]
            o.fn = "nop"
            self.ops[s].append(o)
            self.all_ops.append(o)
        for bb in self.bufs:
            bb.last_w = None
            bb.readers = []

    def emit(self, final_wait_stream="sp"):
        nc = self.nc
        counts = {c: 0 for c in self.clocks}
        for op in self.all_ops:
            if op.fn == "nop":
                continue
            if op.signals:
                counts[op.clock] += 16 if op.is_dma else 1
                op.count = counts[op.clock]
        fin = Op(final_wait_stream, final_wait_stream, "nop", False)
        lastdma = {}
        for op in self.all_ops:
            if op.is_dma:
                lastdma[op.clock] = op
        fin.deps = list(lastdma.values())
        self.ops[final_wait_stream].append(fin)
        nseg = {c: (counts[c] // SEG) + 1 for c in self.clocks}
        sem_ctx = []
        sems = {}
        for c in self.clocks:
            if counts[c] == 0:
                continue
            sems[c] = []
            for k in range(nseg[c]):
                cm = nc.semaphore(f"s_{c}_{k}")
                sems[c].append(cm.__enter__())
                sem_ctx.append(cm)
        self.nsems = sum(len(v) for v in sems.values())

        def seg_total(c, k):
            tot = counts[c]
            hi = min(tot, (k + 1) * SEG - 1) if False else None
            return None

        def locate(c, n, inc):
            pre = n - inc
            k = pre // SEG
            return k, n - k * SEG

        segfinal = {c: [0] * nseg[c] for c in self.clocks}
        for op in self.all_ops:
            if op.fn != "nop" and op.signals:
                inc = 16 if op.is_dma else 1
                k, v = locate(op.clock, op.count, inc)
                segfinal[op.clock][k] = max(segfinal[op.clock][k], v)

        engs = {"pe": "tensor", "act": "scalar", "dve": "vector", "pool": "gpsimd", "sp": "sync"}
        prog = self

        def run_stream(s, e):
            seen = {c: (0, 0) for c in prog.clocks}
            for op in prog.ops[s]:
                for d in op.deps:
                    c = d.clock
                    inc = 16 if d.is_dma else 1
                    k, v = locate(c, d.count, inc)
                    sk, sv = seen[c]
                    if (k, v) <= (sk, sv):
                        continue
                    if d.is_dma and k > sk:
                        for j in range(sk, k):
                            e.wait_ge(sems[c][j], segfinal[c][j])
                    e.wait_ge(sems[c][k], v)
                    seen[c] = (k, v)
                if op.fn == "nop":
                    continue
                ins = op.fn(e)
                if op.signals:
                    inc = 16 if op.is_dma else 1
                    k, v = locate(op.clock, op.count, inc)
                    ins.then_inc(sems[op.clock][k], inc)

        with nc.Block() as block:
            for s in STREAMS:
                if not prog.ops[s]:
                    continue
                getattr(block, engs[s])(lambda e, s=s: run_stream(s, e))
        for cm in reversed(sem_ctx):
            cm.__exit__(None, None, None)


class TL:
    def __init__(self, t, b):
        self.t = t
        self.b = b

    def __getitem__(self, k):
        return self.t[k]


class KB:
    def __init__(self, nc):
        self.nc = nc
        self.P = Prog(nc)
        self.ctx = []
        self.pctx = []
        self.n = 0
        self.pstiles = []
        self.pi = 0

    def sb(self, shape, dt, name=None):
        self.n += 1
        nm = f"{name or 't'}_{self.n}"
        cm = self.nc.sbuf_tensor(nm, shape, dt)
        t = cm.__enter__()
        self.ctx.append(cm)
        return TL(t, self.P.buf(nm))

    def sb_persist(self, shape, dt, name=None):
        self.n += 1
        nm = f"{name or 'p'}_{self.n}"
        cm = self.nc.sbuf_tensor(nm, shape, dt)
        t = cm.__enter__()
        self.pctx.append(cm)
        return TL(t, self.P.buf(nm))

    def free_persist(self):
        for cm in reversed(self.pctx):
            cm.__exit__(None, None, None)
        self.pctx = []

    def psum_pool(self, n=8):
        self.pstiles = []
        for i in range(n):
            self.n += 1
            nm = f"ps_{self.n}"
            cm = self.nc.psum_tensor(nm, [128, 512], F32)
            t = cm.__enter__()
            self.ctx.append(cm)
            tl = TL(t, self.P.buf(nm))
            tl.b.excl = True
            self.pstiles.append(tl)
        self.pi = 0

    def pn(self):
        p = self.pstiles[self.pi % len(self.pstiles)]
        self.pi += 1
        return p

    def end_phase(self):
        self.P.barrier()
        for cm in reversed(self.ctx):
            cm.__exit__(None, None, None)
        self.ctx = []

    def op(self, eng, fn, r=(), w=()):
        return self.P.op(eng, fn, [x.b if isinstance(x, TL) else x for x in r],
                         [x.b if isinstance(x, TL) else x for x in w])

    def dma(self, q, out, in_, r=(), w=(), **kw):
        return self.P.dma(q, out, in_, [x.b if isinstance(x, TL) else x for x in r],
                          [x.b if isinstance(x, TL) else x for x in w], **kw)

    def mm(self, out_tl, out_ap, pairs, r):
        n = len(pairs)

        def fn(e):
            ins = None
            for i, (l, rh) in enumerate(pairs):
                ins = e.matmul(out_ap, lhsT=l, rhs=rh, start=(i == 0), stop=(i == n - 1))
            return ins
        return self.op("pe", fn, r=r, w=[out_tl])

    def tr(self, out_tl, out_ap, in_ap, ident_ap, r):
        return self.op("pe", lambda e: e.transpose(out_ap, in_ap, ident_ap), r=r, w=[out_tl])


def phase0(kb, A):
    kb.psum_pool(4)
    cT = kb.sb([128, 8], F32)
    condT = kb.sb([128, 8], F32)
    kb.dma("sp", cT[:], A["cT"], w=[cT])
    kb.op("act", lambda e: e.activation(out=condT[:], in_=cT[:], func=AF.Silu), r=[cT], w=[condT])
    wt = [kb.sb([128, 4096], F32, "adaw") for _ in range(2)]
    for l in range(2):
        adab = kb.sb([1, 6144], F32)
        n1 = kb.sb([1, 1024], F32)
        n2 = kb.sb([1, 1024], F32)
        modsb = kb.sb([1, 6144], F32)
        rows = kb.sb([1, 6144], F32)
        kb.dma("sp", adab[:], A["adab"][l], w=[adab])
        kb.dma("sp", n1[:], A["n1w"][l], w=[n1])
        kb.dma("sp", n2[:], A["n2w"][l], w=[n2])
        for g in range(12):
            w = wt[(l * 12 + g) % 2]
            kb.dma("sp", w[:], A["adaw"][l, g], w=[w])
            ps = kb.pn()
            kb.mm(ps, ps[0:1, 0:512], [(condT[:, k:k + 1], w[:, k * 512:(k + 1) * 512]) for k in range(8)], r=[condT, w])
            kb.op("dve", lambda e, g=g, ps=ps, modsb=modsb, adab=adab: e.tensor_tensor(out=modsb[0:1, g * 512:(g + 1) * 512], in0=ps[0:1, 0:512],
                                                               in1=adab[0:1, g * 512:(g + 1) * 512], op=ALU.add),
                  r=[ps, adab], w=[modsb])
        S = lambda i: slice(i * 1024, (i + 1) * 1024)
        kb.op("dve", lambda e, rows=rows, modsb=modsb, n1=n1, n2=n2: e.scalar_tensor_tensor(out=rows[0:1, S(0)], in0=modsb[0:1, S(1)], scalar=1.0, in1=n1[0:1, :], op0=ALU.add, op1=ALU.mult), r=[modsb, n1], w=[rows])
        kb.op("dve", lambda e, rows=rows, modsb=modsb, n1=n1, n2=n2: e.tensor_copy(out=rows[0:1, S(1)], in_=modsb[0:1, S(0)]), r=[modsb], w=[rows])
        kb.op("dve", lambda e, rows=rows, modsb=modsb, n1=n1, n2=n2: e.tensor_copy(out=rows[0:1, S(2)], in_=modsb[0:1, S(2)]), r=[modsb], w=[rows])
        kb.op("dve", lambda e, rows=rows, modsb=modsb, n1=n1, n2=n2: e.scalar_tensor_tensor(out=rows[0:1, S(3)], in0=modsb[0:1, S(4)], scalar=1.0, in1=n2[0:1, :], op0=ALU.add, op1=ALU.mult), r=[modsb, n2], w=[rows])
        kb.op("dve", lambda e, rows=rows, modsb=modsb, n1=n1, n2=n2: e.tensor_copy(out=rows[0:1, S(4)], in_=modsb[0:1, S(3)]), r=[modsb], w=[rows])
        kb.op("dve", lambda e, rows=rows, modsb=modsb, n1=n1, n2=n2: e.tensor_copy(out=rows[0:1, S(5)], in_=modsb[0:1, S(5)]), r=[modsb], w=[rows])
        kb.dma("sp", A["modrow"][l], rows[0:1, :], r=[rows])
    kb.end_phase()


def rms_rstd(kb, xt, junk, ss, rstd, eps, n):
    kb.op("act", lambda e: e.activation(out=junk[:], in_=xt[:], func=AF.Square, accum_out=ss[:]), r=[xt], w=[junk, ss])
    kb.op("act", lambda e: e.activation(out=rstd[:], in_=ss[:], func=AF.Sqrt, scale=1.0 / n, bias=kb.epsc[(eps)][:]), r=[ss], w=[rstd])
    kb.op("dve", lambda e: e.reciprocal(out=rstd[:], in_=rstd[:]), r=[rstd], w=[rstd])


def make_consts(kb, A, vals):
    kb.epsc = {}
    for v in vals:
        t = kb.sb([128, 1], F32, "eps")
        kb.op("pool", lambda e, t=t, v=v: e.memset(t[:], v), w=[t])
        kb.epsc[v] = t


class Pipe:
    def __init__(self, pools):
        self.pools = {k: list(v) for k, v in pools.items()}
        self.active = []

    def run(self, items):
        items = list(items)
        pos = 0
        while pos < len(items) or self.active:
            if pos < len(items):
                pool, fac = items[pos]
                if self.pools[pool]:
                    slot = self.pools[pool].pop(0)
                    self.active.append((fac(slot), pool, slot))
                    pos += 1
            nxt = []
            for g, pool, slot in self.active:
                try:
                    next(g)
                    nxt.append((g, pool, slot))
                except StopIteration:
                    self.pools[pool].append(slot)
            self.active = nxt


def phase1(kb, A, l, xsrc):
    kb.psum_pool(8)
    make_consts(kb, A, [1e-6])
    ident16 = kb.sb([128, 128], BF16)
    ident32 = kb.sb([128, 128], F32)
    ones32 = kb.sb([128, 128], F32)
    kb.dma("pool", ident16[:], A["ident"], w=[ident16])
    kb.dma("sp", ident32[:], A["ident"], w=[ident32])
    kb.op("pool", lambda e: e.memset(ones32[:], 1.0), w=[ones32])
    win = [kb.sb([128, DIN], BF16, "win") for _ in range(8)]
    for k in range(8):
        kb.dma("pool", win[k][:], A["win"][l, :, k * DIN:(k + 1) * DIN], w=[win[k]])
    bcW = kb.sb([128, 1024], F32)
    bcS = kb.sb([128, 1024], F32)
    kb.dma("sp", bcW[:], A["modrow"][l, 0:1, 0:1024].partition_broadcast(128), w=[bcW])
    kb.dma("sp", bcS[:], A["modrow"][l, 0:1, 1024:2048].partition_broadcast(128), w=[bcS])
    convw = kb.sb([128, 48], F32)
    kb.dma("sp", convw[:], A["convw"][l], w=[convw])
    mu = kb.sb([128, 7], F32)
    kb.dma("sp", mu[:], A["mu"][l], w=[mu])
    omu = kb.sb([128, 7], F32)
    kb.op("dve", lambda e: e.tensor_scalar(out=omu[:], in0=mu[:], scalar1=-1.0, scalar2=1.0, op0=ALU.mult, op1=ALU.add), r=[mu], w=[omu])
    halo = [kb.sb([128, 3], F32, "halo") for _ in range(12)]
    halor = [kb.sb([128, 1], F32, "halor") for _ in range(7)]
    for t in halo + halor:
        kb.op("pool", lambda e, t=t: e.memset(t[:], 0.0), w=[t])
    junk = kb.sb([128, 1024], BF16)
    hT = [kb.sb([128, 8 * 512], BF16, "hT") for _ in range(2)]
    NP, NF, NT = 2, 4, 2
    pslot = [dict(bank=4 + s, xt=kb.sb([128, 1024], F32, "xt"), tmp=kb.sb([128, 1024], F32, "tmp"), xn=kb.sb([128, 1024], BF16, "xn"),
                  ss=kb.sb([128, 1], F32), rstd=kb.sb([128, 1], F32)) for s in range(NP)]
    fslot = [dict(bank=s, fb=kb.sb([128, 515], F32, "fb"), acc=kb.sb([128, 512], F32, "acc"), sil=kb.sb([128, 512], F32, "sil"),
                  sq=kb.sb([128, 512], F32, "sq"), rs=kb.sb([128, 512], F32, "rs"), kn=kb.sb([128, 512], F32, "kn"),
                  o16=kb.sb([128, 512], BF16, "o16"), tok=kb.sb([128, 512], F32, "tok")) for s in range(NF)]
    tslot = [dict(bank=6 + s, tm=kb.sb([128, 1544], F32, "tmb")) for s in range(NT)]

    def tile_prep(S, tb, j):
        xt, tmp, xn, ss, rstd = S["xt"], S["tmp"], S["xn"], S["ss"], S["rstd"]
        ps = kb.pstiles[S["bank"]]
        h = hT[tb % 2]
        t0 = tb * 512 + j * 128
        kb.dma("sp", xt[:], xsrc[t0:t0 + 128, :], w=[xt])
        yield
        kb.op("act", lambda e: e.activation(out=junk[:], in_=xt[:], func=AF.Square, accum_out=ss[:]), r=[xt], w=[junk, ss])
        yield
        kb.op("act", lambda e: e.activation(out=rstd[:], in_=ss[:], func=AF.Sqrt, scale=1.0 / 1024, bias=kb.epsc[1e-6][:]), r=[ss], w=[rstd])
        yield
        kb.op("dve", lambda e: e.reciprocal(out=rstd[:], in_=rstd[:]), r=[rstd], w=[rstd])
        yield
        kb.op("dve", lambda e: e.scalar_tensor_tensor(out=tmp[:], in0=xt[:], scalar=rstd[:, 0:1], in1=bcW[:], op0=ALU.mult, op1=ALU.mult),
              r=[xt, rstd, bcW], w=[tmp])
        yield
        kb.op("pool", lambda e: e.tensor_tensor(out=xn[:], in0=tmp[:], in1=bcS[:], op=ALU.add), r=[tmp, bcS], w=[xn])
        yield
        psb = ps[:].bitcast(BF16)

        def trf(e):
            ins = None
            for k in range(8):
                ins = e.transpose(psb[:, k * 128:(k + 1) * 128], xn[:, k * 128:(k + 1) * 128], ident16[:])
            return ins
        kb.op("pe", trf, r=[xn, ident16], w=[ps])
        yield
        hv = h[:].rearrange("p (k t) -> p k t", k=8)[:, :, j * 128:(j + 1) * 128]
        pv = psb.rearrange("p (k t) -> p k t", k=8)
        kb.op("act", lambda e: e.copy(out=hv, in_=pv), r=[ps], w=[h])
        yield

    def fm_group(S, tb, c):
        h = hT[tb % 2]
        ps = kb.pstiles[S["bank"]]
        fb, a_, s_, q_, r_, kn, o_, tk = S["fb"], S["acc"], S["sil"], S["sq"], S["rs"], S["kn"], S["o16"], S["tok"]
        col0 = c * 128 if c < 12 else 3080 + (c - 12) * 128
        bsl = slice(tb * 512, (tb + 1) * 512)
        kb.mm(ps, ps[:, 0:512], [(win[k][:, col0:col0 + 128], h[:, k * 512:(k + 1) * 512]) for k in range(8)], r=[h] + win)
        yield
        kb.op("act", lambda e: e.copy(out=fb[:, 3:515], in_=ps[:, 0:512]), r=[ps], w=[fb])
        if c < 12:
            hl = halo[c]
            kb.op("pool", lambda e: e.tensor_copy(out=fb[:, 0:3], in_=hl[:]), r=[hl], w=[fb])
            yield
            kb.op("pool", lambda e: e.tensor_copy(out=hl[:], in_=fb[:, 512:515]), r=[fb], w=[hl])
            kb.op("act", lambda e: e.activation(out=a_[:], in_=fb[:, 0:512], func=AF.Copy, scale=convw[:, c * 4:c * 4 + 1]), r=[fb, convw], w=[a_])
            yield
            for k in range(1, 4):
                kb.op("dve", lambda e, k=k: e.scalar_tensor_tensor(out=a_[:], in0=fb[:, k:k + 512], scalar=convw[:, c * 4 + k:c * 4 + k + 1], in1=a_[:], op0=ALU.mult, op1=ALU.add),
                      r=[fb, convw, a_], w=[a_])
                yield
            kb.op("act", lambda e: e.activation(out=s_[:], in_=a_[:], func=AF.Silu), r=[a_], w=[s_])
            yield
            if c < 8:
                kb.op("pool", lambda e: e.tensor_tensor(out=q_[:], in0=s_[:], in1=s_[:], op=ALU.mult), r=[s_], w=[q_])
                yield
                kb.mm(ps, ps[:, 0:512], [(ones32[:], q_[:])], r=[ones32, q_])
                yield
                kb.op("act", lambda e: e.activation(out=r_[:], in_=ps[:, 0:512], func=AF.Sqrt, bias=kb.epsc[1e-6][:], scale=1.0), r=[ps], w=[r_])
                yield
                kb.op("dve", lambda e: e.reciprocal(out=r_[:], in_=r_[:]), r=[r_], w=[r_])
                yield
                hh = c % 4
                if c < 4:
                    kb.op("dve", lambda e: e.scalar_tensor_tensor(out=o_[:], in0=s_[:], scalar=128.0 ** -0.5, in1=r_[:], op0=ALU.mult, op1=ALU.mult), r=[s_, r_], w=[o_])
                    yield
                    kb.dma("sp", A["qT"][l, hh, :, bsl], o_[:], r=[o_])
                    return
                kb.op("dve", lambda e: e.tensor_tensor(out=kn[:], in0=s_[:], in1=r_[:], op=ALU.mult), r=[s_, r_], w=[kn])
                yield
                kb.op("pool", lambda e: e.tensor_copy(out=o_[:], in_=kn[:]), r=[kn], w=[o_])
                src = kn
            else:
                src = s_
                hh = c - 8

            def trf2(e):
                ins = None
                for j in range(4):
                    ins = e.transpose(ps[:, j * 128:(j + 1) * 128], src[:, j * 128:(j + 1) * 128], ident32[:])
                return ins
            kb.op("pe", trf2, r=[src, ident32], w=[ps])
            yield
            if c < 8:
                kb.dma("sp", A["kT"][l, hh, :, bsl], o_[:], r=[o_])
            kb.op("act", lambda e: e.copy(out=tk[:], in_=ps[:, 0:512]), r=[ps], w=[tk])
            yield
            dst = A["ktok"] if c < 8 else A["vtok"]
            dv = dst[l, bsl, hh * 128:(hh + 1) * 128].rearrange("(j p) d -> p j d", p=128)
            kb.dma("sp", dv, tk[:].rearrange("p (j d) -> p j d", j=4), r=[tk])
        else:
            jr = c - 12
            hl = halor[jr]
            kb.op("pool", lambda e: e.tensor_copy(out=fb[:, 2:3], in_=hl[:]), r=[hl], w=[fb])
            yield
            kb.op("pool", lambda e: e.tensor_copy(out=hl[:], in_=fb[:, 514:515]), r=[fb], w=[hl])
            kb.op("act", lambda e: e.activation(out=a_[:], in_=fb[:, 3:515], func=AF.Copy, scale=omu[:, jr:jr + 1]), r=[fb, omu], w=[a_])
            yield
            kb.op("dve", lambda e: e.scalar_tensor_tensor(out=s_[:], in0=fb[:, 2:514], scalar=mu[:, jr:jr + 1], in1=a_[:], op0=ALU.mult, op1=ALU.add),
                  r=[fb, a_, mu], w=[s_])
            yield
            kb.dma("sp", A["rwT"][l, jr * 128:(jr + 1) * 128, bsl], s_[:], r=[s_])

    def tm_group(S, tb, j):
        h = hT[tb % 2]
        ps = kb.pstiles[S["bank"]]
        tm = S["tm"]
        for gi, (c0, c1) in enumerate(((1536, 2048), (2048, 2056), (2056, 2568), (2568, 3080))):
            nw = c1 - c0
            kb.mm(ps, ps[:, 0:nw], [(h[:, k * 512 + j * 128:k * 512 + (j + 1) * 128], win[k][:, c0:c1]) for k in range(8)], r=[h] + win)
            yield
            if True:
                kb.op("act", lambda e, c0=c0, c1=c1, nw=nw: e.copy(out=tm[:, c0 - 1536:c1 - 1536], in_=ps[:, 0:nw]), r=[ps], w=[tm])
            else:
                kb.op("dve", lambda e, c0=c0, c1=c1, nw=nw: e.tensor_copy(out=tm[:, c0 - 1536:c1 - 1536], in_=ps[:, 0:nw]), r=[ps], w=[tm])
            yield
        t0 = tb * 512 + j * 128
        kb.dma("sp", A["tm"][l, t0:t0 + 128, :], tm[:], r=[tm])

    pipe = Pipe({"p": pslot, "f": fslot, "t": tslot})
    pipe.run([("p", (lambda S, j=j: tile_prep(S, 0, j))) for j in range(4)])
    items = []
    for tb in range(8):
        blk = []
        for c in range(19):
            blk.append(("f", (lambda S, tb=tb, c=c: fm_group(S, tb, c))))
            if tb < 7 and c in (2, 6, 10, 14):
                blk.append(("p", (lambda S, tb=tb, j=(c - 2) // 4: tile_prep(S, tb + 1, j))))
        for j in range(4):
            blk.append(("t", (lambda S, tb=tb, j=j: tm_group(S, tb, j))))
        items += blk
    pipe.run(items)
    kb.end_phase()


def interleave(gens, stagger=0):
    gens = list(gens)
    if stagger:
        for idx, g in enumerate(gens):
            for _ in range(idx * stagger):
                next(g)
    while gens:
        nxt = []
        for g in gens:
            try:
                next(g)
                nxt.append(g)
            except StopIteration:
                pass
        gens = nxt


def zip_gens(*gens):
    gens = list(gens)
    while gens:
        nxt = []
        for g in gens:
            try:
                next(g)
                nxt.append(g)
            except StopIteration:
                pass
        gens = nxt
        if gens:
            yield


class H2:
    def __init__(self, kb, n32, n16, nw, banks, nev=0, evq="act"):
        self.kb = kb
        self.rev = [kb.sb([128, 512], F32R, "rev") for _ in range(nev)]
        self.r32r = [kb.sb([128, 128], F32R, "r32r") for _ in range(2 if nev else 0)]
        self.i32r = 0
        self.iev = 0
        self.evq = evq
        self.r32 = [kb.sb([128, 128], F32, "r32") for _ in range(n32)]
        self.r16 = [kb.sb([128, 128], BF16, "r16") for _ in range(n16)]
        self.rw = [kb.sb([128, 256], F32, "rw") for _ in range(nw)]
        self.banks = banks
        self.i32 = self.i16 = self.iw = self.ip = 0

    def t32(self):
        self.i32 += 1
        return self.r32[self.i32 % len(self.r32)]

    def t16(self):
        self.i16 += 1
        return self.r16[self.i16 % len(self.r16)]

    def tr(self):
        self.i32r += 1
        return self.r32r[self.i32r % len(self.r32r)]

    def tev(self):
        self.iev += 1
        return self.rev[self.iev % len(self.rev)]

    def tw(self):
        self.iw += 1
        return self.rw[self.iw % len(self.rw)]

    def pn(self):
        self.ip += 1
        return self.kb.pstiles[self.banks[self.ip % len(self.banks)]]

    def TT(self, eng, out_tl, out, in0, in1, op, r):
        self.kb.op(eng, lambda e: e.tensor_tensor(out=out, in0=in0, in1=in1, op=op), r=r, w=[out_tl])

    def TS(self, eng, out_tl, out, in0, s1, op0, r, s2=None, op1=None):
        if eng == "act":
            assert op1 is None and op0 == ALU.mult
            self.kb.op("act", lambda e: e.activation(out=out, in_=in0, func=AF.Copy, scale=s1), r=r, w=[out_tl])
            return
        if op1 is None:
            self.kb.op(eng, lambda e: e.tensor_scalar(out=out, in0=in0, scalar1=s1, scalar2=None, op0=op0), r=r, w=[out_tl])
        else:
            self.kb.op(eng, lambda e: e.tensor_scalar(out=out, in0=in0, scalar1=s1, scalar2=s2, op0=op0, op1=op1), r=r, w=[out_tl])

    def STT(self, eng, out_tl, out, in0, sc, in1, op0, op1, r):
        self.kb.op("dve", lambda e: e.scalar_tensor_tensor(out=out, in0=in0, scalar=sc, in1=in1, op0=op0, op1=op1), r=r, w=[out_tl])

    def ACT(self, out_tl, out, in_, func, r, bias=0.0, scale=1.0):
        self.kb.op("act", lambda e: e.activation(out=out, in_=in_, func=func, bias=bias, scale=scale), r=r, w=[out_tl])

    def CP(self, eng, out_tl, out, in_, r):
        if eng == "act":
            self.kb.op("act", lambda e: e.copy(out=out, in_=in_), r=r, w=[out_tl])
        else:
            self.kb.op(eng, lambda e: e.tensor_copy(out=out, in_=in_), r=r, w=[out_tl])

    def MM(self, ps, out, pairs, r):
        self.kb.mm(ps, out, pairs, r=r)


def inverse(H, ev, LoffTn, I32):
    kb = H.kb
    bank = kb.pstiles[H.banks[0]]
    bv = bank[:, 0:512].rearrange("p (h b c) -> p h b c", h=2, b=2)
    v4 = lambda t: t[:].rearrange("p (h b c) -> p h b c", h=2, b=2)
    Ib = I32[:].rearrange("p (o c) -> p o c", o=1).to_broadcast([128, 2, 128])
    H.CP("pool", ev, v4(ev)[:, :, 1, :], Ib, [I32])
    for j in range(6):
        def fn(e, ev=ev):
            e.matmul(bank[:, 0:256], lhsT=ev[:, 256:384], rhs=ev[:, 0:256], start=True, stop=True)
            return e.matmul(bank[:, 256:512], lhsT=ev[:, 0:128], rhs=ev[:, 256:512], start=True, stop=True)
        kb.op("pe", fn, r=[ev], w=[bank])
        yield
        evn = H.tev()
        if j < 5:
            H.CP("act", evn, v4(evn)[:, :, 0, :], bv[:, :, 0, :], [bank])
        H.TT("dve", evn, v4(evn)[:, :, 1, :], bv[:, :, 1, :], v4(ev)[:, :, 1, :].bitcast(F32), ALU.add, [bank, ev])
        ev = evn
        yield
    H.MM(bank, bank[:, 0:256], [(LoffTn[:], ev[:, 128:384])], [LoffTn, ev])
    yield
    ImY = H.tr()
    H.TT("dve", ImY, ImY[:], bank[:, 0:128], I32[:], ALU.add, [bank, I32])
    yield
    H.MM(bank, bank[:, 256:512], [(ImY[:], ev[:, 256:512])], [ImY, ev])
    yield
    TTt = H.t32()
    H.CP("act", TTt, TTt[:], bank[:, 384:512], [bank])
    yield
    return TTt


def phase2(kb, A, l):
    kb.psum_pool(8)
    make_consts(kb, A, [1e-6, 64e-5, 1.0])
    Hg = [H2(kb, 6, 0, 1, [h], nev=2, evq=("act" if h % 2 == 0 else "dve")) for h in range(4)]
    Hh = [H2(kb, 4, 4, 0, [4 + h], nev=2, evq=("dve" if h % 2 == 0 else "act")) for h in range(4)]
    Hp = [H2(kb, 22, 8, 0, [2 * j, 2 * j + 1]) for j in range(2)]
    Hs = H2(kb, 1, 6, 0, [4])
    Ht = H2(kb, 3, 3, 2, [5])
    Hr = H2(kb, 3, 6, 4, [6, 7])
    Hx = H2(kb, 1, 1, 1, [0, 1, 2, 3, 4, 5, 6, 7])
    LoffD = [kb.sb([128, 128], F32R, "LoffD") for _ in range(8)]
    H = Hx
    C = {}
    C["ident32"] = kb.sb([128, 128], F32)
    C["ident16"] = kb.sb([128, 128], BF16)
    kb.dma("sp", C["ident32"][:], A["ident"], w=[C["ident32"]])
    kb.dma("pool", C["ident16"][:], A["ident"], w=[C["ident16"]])
    cm = []
    for i in range(9):
        t = kb.sb([128, 128], F32, "cm")
        kb.dma("sp", t[:], A["cmask"][i], w=[t])
        cm.append(t)
    triU, strictL, negmT, bd, offm, offT, strictT, ones32, negones = cm
    retc = kb.sb([128, 1026], F32)
    kb.dma("sp", retc[:], A["retc"], w=[retc])
    ba = kb.sb([128, NCH * 8], F32)
    for n_ in range(NCH):
        kb.dma("sp", ba[:, n_ * 8:(n_ + 1) * 8], A["tm"][l, n_ * 128:(n_ + 1) * 128, 512:520], w=[ba])
    alog = kb.sb([128, 4], F32)
    dtb = kb.sb([128, 4], F32)
    kb.dma("sp", alog[:], A["alog"][l].partition_broadcast(128), w=[alog])
    kb.dma("sp", dtb[:], A["dtb"][l].partition_broadcast(128), w=[dtb])
    negA = kb.sb([128, 4], F32)
    H.ACT(negA, negA[:], alog[:], AF.Exp, [alog])
    H.TS("dve", negA, negA[:], negA[:], -1.0, ALU.mult, [negA])
    bav = ba[:].rearrange("p (n c) -> p n c", c=8)
    v3 = lambda t: t[:].rearrange("p (n h) -> p n h", h=4)
    beta = kb.sb([128, 128], F32)
    H.ACT(beta, v3(beta), bav[:, :, 0:4], AF.Sigmoid, [ba])
    xg = kb.sb([128, 128], F32)
    dtb3 = dtb[:].rearrange("p (o h) -> p o h", o=1).to_broadcast([128, NCH, 4])
    negA3 = negA[:].rearrange("p (o h) -> p o h", o=1).to_broadcast([128, NCH, 4])
    H.TT("dve", xg, v3(xg), bav[:, :, 4:8], dtb3, ALU.add, [ba, dtb])
    axg = kb.sb([128, 128], F32)
    H.TS("dve", axg, axg[:], xg[:], -1.0, ALU.mult, [xg])
    H.TT("dve", axg, axg[:], axg[:], xg[:], ALU.max, [axg, xg])
    H.ACT(axg, axg[:], axg[:], AF.Exp, [axg], scale=-1.0)
    H.ACT(axg, axg[:], axg[:], AF.Ln, [axg], bias=kb.epsc[1.0][:])
    H.TS("dve", xg, xg[:], xg[:], 0.0, ALU.max, [xg])
    H.TT("dve", xg, xg[:], xg[:], axg[:], ALU.add, [xg, axg])
    g_all = kb.sb([128, 128], F32)
    H.TT("dve", g_all, v3(g_all), v3(xg), negA3, ALU.mult, [xg, negA])
    gc_all = kb.sb([128, 128], F32)
    egc = kb.sb([128, 128], F32)
    ekd = kb.sb([128, 128], F32)
    dch = kb.sb([128, 128], F32)
    negbeta = kb.sb([128, 128], F32)
    bg = kb.sb([128, 128], F32)
    ps = kb.pn()
    H.MM(ps, ps[:, 0:128], [(triU[:], g_all[:])], [triU, g_all])
    H.CP("dve", gc_all, gc_all[:], ps[:, 0:128], [ps])
    H.ACT(egc, egc[:], ps[:, 0:128], AF.Exp, [ps])
    ps = kb.pn()
    H.MM(ps, ps[:, 0:128], [(strictL[:], g_all[:])], [strictL, g_all])
    H.ACT(ekd, ekd[:], ps[:, 0:128], AF.Exp, [ps])
    ps = kb.pn()
    H.MM(ps, ps[:, 0:128], [(ones32[:], g_all[:])], [ones32, g_all])
    H.ACT(dch, dch[:], ps[:, 0:128], AF.Exp, [ps])
    H.TS("dve", negbeta, negbeta[:], beta[:], -1.0, ALU.mult, [beta])
    H.TT("dve", bg, bg[:], beta[:], egc[:], ALU.mult, [beta, egc])
    gnw = kb.sb([128, 128], F32)
    kb.dma("sp", gnw[:], A["gnw"][l].partition_broadcast(128), w=[gnw])
    lora = kb.sb([128, 256], F32)
    kb.dma("sp", lora[:], A["lora"][l], w=[lora])
    rwc = kb.sb([128, 12], F32)
    kb.dma("sp", rwc[:], A["rwcol"][l], w=[rwc])
    omka = kb.sb([128, 2], F32)
    H.TS("dve", omka, omka[:], rwc[:, 6:8], -1.0, ALU.mult, [rwc], s2=1.0, op1=ALU.add)
    lnw = kb.sb([128, 256], F32)
    lnb = kb.sb([128, 256], F32)
    kb.dma("sp", lnw[:], A["lnw"][l].partition_broadcast(128), w=[lnw])
    kb.dma("sp", lnb[:], A["lnb"][l].partition_broadcast(128), w=[lnb])
    sel = kb.sb([128, 2], F32)
    kb.op("pool", lambda e: e.memset(sel[:], 0.0), w=[sel])
    kb.op("pool", lambda e: e.memset(sel[0:64, 0:1], 1.0), w=[sel])
    kb.op("pool", lambda e: e.memset(sel[64:128, 1:2], 1.0), w=[sel])
    v12 = None
    if l == 1:
        v1 = kb.sb([128, 64], F32)
        v2 = kb.sb([32, 256], F32)
        kb.dma("sp", v1[:], A["v1"], w=[v1])
        kb.dma("sp", v2[:], A["v2"], w=[v2])
        v12 = (v1, v2)
    S32 = [kb.sb([128, 128], F32, "S32") for _ in range(4)]
    S16 = [kb.sb([128, 128], BF16, "S16") for _ in range(4)]
    R32 = [kb.sb([128, 128], F32, "R32") for _ in range(2)]
    R16 = [kb.sb([128, 128], BF16, "R16") for _ in range(2)]
    W32 = [kb.sb([128, 128], F32, "W32") for _ in range(2)]
    W16 = [kb.sb([128, 128], BF16, "W16") for _ in range(2)]
    for t in S32 + S16 + R32 + R16 + W32 + W16:
        kb.op("pool", lambda e, t=t: e.memset(t[:], 0.0), w=[t])
    NB = 2
    _ded = lambda dt, nm: [kb.sb([128, 128], dt, nm) for _ in range(4)]
    ded = lambda dt, nm: [_ded(dt, nm)] * NB
    attnT = ded(BF16, "attnT")
    qdecT = ded(BF16, "qdecT")
    wT16 = ded(BF16, "wT16")
    kdec16 = ded(BF16, "kdec")
    u32 = ded(F32, "u32")
    par = lambda shape, dt, nm: [kb.sb(shape, dt, nm) for _ in range(NB)]
    qTc = [kb.sb([128, 512], BF16, "qTc")] * NB
    kTc = [kb.sb([128, 512], BF16, "kTc")] * NB
    ktok = [kb.sb([128, 512], F32, "ktok")] * NB
    vtok = [kb.sb([128, 512], F32, "vtok")] * NB
    tmc = [kb.sb([128, 1544], F32, "tmc")] * NB
    rope = [kb.sb([128, 64], F32, "rope")] * NB
    rwt = [kb.sb([128, 7 * 128], F32, "rwt")] * NB
    vft = [kb.sb([128, 256], F32, "vft")] * NB
    osb = [kb.sb([128, 512], F32, "osb")] * NB
    obs = [kb.sb([128, 256], F32, "obs")] * NB
    ysb = [kb.sb([128, 256], F32, "ysb")] * NB
    mixc = par([128, 1024], BF16, "mixc")
    junk = kb.sb([128, 128], F32)
    kb.rot16 = [kb.sb([128, 512], BF16, "rot16")] * NB
    kb.rqd = [kb.sb([128, 256], BF16, "rqd")] * NB
    kb.rq = [kb.sb([128, 256], BF16, "rq")] * NB
    kb.rk = [kb.sb([128, 256], BF16, "rk")] * NB
    kb.rkd = [kb.sb([128, 256], BF16, "rkd")] * NB
    kb.rv = [kb.sb([128, 256], BF16, "rv")] * NB
    kb.rst = par([128, 12], F32, "rst")
    kb.gst = par([128, 8], F32, "gst")
    kb.gsz = [kb.sb([128, 512], F32, "gsz")] * NB
    kb.wst = par([128, 12], F32, "wst")
    kb.rwg = par([128, 256], F32, "rwg")
    kb.rwv32 = par([128, 256], F32, "rwv32")
    kb.rwgc = par([128, 2], F32, "rwgc")
    kb.rwbc = par([128, 2], F32, "rwbc")
    kb.rwbs = par([128, 4], F32, "rwbs")
    kb.rwtw = [kb.sb([128, 128], F32, "rwtw")] * NB
    kb.rwvl = [kb.sb([128, 128], F32, "rwvl")] * NB
    kb.rwtok = [[kb.sb([128, 512], BF16, "rwtok") for _ in range(2)] for _ in range(NB)]
    kb.rwpa = [[kb.sb([128, 128], BF16, "rwpa") for _ in range(2)] for _ in range(NB)]
    kb.rwqv = [[kb.sb([128, 128], F32, "rwqv") for _ in range(2)] for _ in range(NB)]
    kb.rwar = [[kb.sb([128, 256], BF16, "rwar") for _ in range(2)] for _ in range(NB)]
    _bk = [kb.sb([128, 256], BF16, "rwbk") for _ in range(2)]
    kb.rwbk = [_bk] * NB
    kb.rwA = [[kb.sb([128, 256], BF16, "rwA") for _ in range(4)] for _ in range(NB)]
    I32 = C["ident32"]
    def load_s1(n):
        ts = slice(n * 128, (n + 1) * 128)
        kb.dma("sp", qTc[0][:].rearrange("p (h t) -> p h t", h=4), A["qT"][l, :, :, ts].rearrange("h p t -> p h t"), w=[qTc[0]])
        kb.dma("sp", kTc[0][:].rearrange("p (h t) -> p h t", h=4), A["kT"][l, :, :, ts].rearrange("h p t -> p h t"), w=[kTc[0]])
        kb.dma("sp", ktok[0][:], A["ktok"][l, ts, :], w=[ktok[0]])
        kb.dma("sp", vtok[0][:], A["vtok"][l, ts, :], w=[vtok[0]])

    def load_s2(n):
        ts = slice(n * 128, (n + 1) * 128)
        kb.dma("sp", tmc[0][:], A["tm"][l, ts, :], w=[tmc[0]])
        kb.dma("sp", rope[0][:], A["rope"][ts, :], w=[rope[0]])
        kb.dma("sp", rwt[0][:].rearrange("p (j t) -> p j t", j=7), A["rwT"][l, :, ts].rearrange("(j p) t -> p j t", p=128), w=[rwt[0]])
        if l == 1:
            kb.dma("sp", vft[0][:].rearrange("p (j t) -> p j t", j=2), A["vfT"][:, ts].rearrange("(j p) t -> p j t", p=128), w=[vft[0]])

    def S1(n):
        i = n % NB
        return [gdn_prep(kb, Hg[h], LoffD[h], I32, n, h, qTc[0], kTc[0], ktok[0], vtok[0], cm, g_all, gc_all, ekd, negbeta, bg, beta,
                         attnT[i][h], qdecT[i][h], wT16[i][h], kdec16[i][h], u32[i][h]) for h in range(4)]

    def S2(n):
        i = n % NB
        return [ret_chunk(kb, Hr, C, i, tmc[0], rope[0], retc, bd, R32, R16, obs[i], mixc[i]),
                gdn_scan(kb, Hs, n, i, attnT[i], qdecT[i], wT16[i], kdec16[i], u32[i], S32, S16, dch, osb[i], tmc[0], gnw, mixc[i], junk)] + \
               [rwkv_pair(kb, Hp[j], C, A, l, n, i, j, rwt[0], vft[0], lora, rwc, omka, sel, cm, v12) for j in range(2)]

    def S3(n):
        i = n % NB
        return [rwkv_head(kb, Hh[2 * j + hh], LoffD[4 + 2 * j + hh], I32, i, j, hh, cm) for j in range(2) for hh in range(2)]

    def S4(n):
        i = n % NB
        return rwkv_tail(kb, Ht, A, i, slice(n * 128, (n + 1) * 128), bd, W32, W16, ysb[i], lnw, lnb, mixc[i])

    def mix2(a, b):
        out = []
        for k in range(max(len(a), len(b))):
            if k < len(a):
                out.append(a[k])
            if k < len(b):
                out.append(b[k])
        return out

    load_s1(0)
    interleave(S1(0), stagger=1)
    load_s1(1)
    load_s2(0)
    rwkv_pre(kb, Hx, l, 0, rwt[0], lora, v12)
    interleave(S2(0))
    for n in range(NCH):
        more = n + 1 < NCH
        if more:
            load_s2(n + 1)
        interleave(mix2(S3(n), S1(n + 1) if more else []), stagger=1)
        if n + 2 < NCH:
            load_s1(n + 2)
        if more:
            rwkv_pre(kb, Hx, l, (n + 1) % NB, rwt[0], lora, v12)
        interleave([S4(n)] + (S2(n + 1) if more else []))
    kb.end_phase()


def ret_chunk(kb, H, C, i, tm, rope, retc, bd, R32, R16, obs, mx):
    I16 = C["ident16"]
    psO = kb.pstiles[H.banks[0]]
    pother = kb.pstiles[H.banks[1]]
    qk = tm[:, 520:1032].rearrange("p (g d) -> p g d", d=64)
    x1, x2 = qk[:, :, 0:32], qk[:, :, 32:64]
    cosb = rope[:, 0:32].rearrange("p (o d) -> p o d", o=1).to_broadcast([128, 8, 32])
    sinb = rope[:, 32:64].rearrange("p (o d) -> p o d", o=1).to_broadcast([128, 8, 32])
    ta, tb_, tc, td = H.tw(), H.tw(), H.tw(), H.tw()
    v8 = lambda t: t[:].rearrange("p (g d) -> p g d", d=32)
    H.TT("dve", ta, v8(ta), x1, cosb, ALU.mult, [tm, rope])
    H.TT("pool", tb_, v8(tb_), x2, sinb, ALU.mult, [tm, rope])
    yield
    H.TT("dve", tc, v8(tc), x2, cosb, ALU.mult, [tm, rope])
    H.TT("pool", td, v8(td), x1, sinb, ALU.mult, [tm, rope])
    yield
    rot = kb.rot16[i]
    rv = rot[:].rearrange("p (g d) -> p g d", d=64)
    H.TT("dve", rot, rv[:, :, 0:32], v8(ta), v8(tb_), ALU.subtract, [ta, tb_])
    H.TT("pool", rot, rv[:, :, 32:64], v8(tc), v8(td), ALU.add, [tc, td])
    yield
    ps = pother
    psb = ps[:].bitcast(BF16)

    def trf(e):
        ins = None
        for j in range(4):
            ins = e.transpose(psb[:, j * 128:(j + 1) * 128], rot[:, j * 128:(j + 1) * 128], I16[:])
        return ins
    kb.op("pe", trf, r=[rot, I16], w=[ps])
    yield
    qdT, qT, kT = kb.rqd[i], kb.rq[i], kb.rk[i]
    H.TT("dve", qdT, qdT[:], psb[:, 0:256], retc[:, 512:768], ALU.mult, [ps, retc])
    yield
    H.CP("act", qT, qT[:], psb[:, 0:256], [ps])
    H.ACT(kT, kT[:], psb[:, 256:512], AF.Copy, [ps], scale=0.125)
    kdec = kb.rkd[i]
    H.TT("pool", kdec, kdec[:], rot[:, 256:512], retc[:, 768:1024], ALU.mult, [rot, retc])
    v16 = kb.rv[i]
    H.CP("pool", v16, v16[:], tm[:, 1032:1288], [tm])
    yield
    hc = lambda h: slice(h * 128, (h + 1) * 128)

    sbank = lambda h: (pother if h % 2 == 0 else psO)
    scol = lambda h: slice((h // 2) * 128, (h // 2 + 1) * 128)

    def fS(e):
        ins = None
        for h in range(4):
            pr, pb = h // 2, (h % 2) * 64
            ins = e.matmul(sbank(h)[:, scol(h)], lhsT=kT[pb:pb + 64, pr * 128:(pr + 1) * 128], rhs=qT[pb:pb + 64, pr * 128:(pr + 1) * 128], start=True, stop=True)
        return ins
    kb.op("pe", fS, r=[kT, qT], w=[pother, psO])
    yield
    scs = []
    for h in range(4):
        sc = H.t16()
        H.TT("dve", sc, sc[:], sbank(h)[:, scol(h)], retc[:, hc(h)], ALU.mult, [sbank(h), retc])
        scs.append(sc)
    yield

    def fO(e):
        ins = None
        for h in range(4):
            pr = h // 2
            e.matmul(psO[:, h * 64:(h + 1) * 64], lhsT=qdT[:, pr * 128:(pr + 1) * 128], rhs=R16[pr][:, h % 2 * 64:(h % 2 + 1) * 64], start=True, stop=False)
            ins = e.matmul(psO[:, h * 64:(h + 1) * 64], lhsT=scs[h][:], rhs=v16[:, h * 64:(h + 1) * 64], start=False, stop=True)
        return ins
    kb.op("pe", fO, r=[qdT, v16] + R16 + scs, w=[psO])
    yield
    H.CP("act", obs, obs[:], psO[:, 0:256], [psO])
    yield

    def r_update():
        def fR(e):
            ins = None
            for pr in range(2):
                ins = e.matmul(pother[:, hc(pr)], lhsT=kdec[:, pr * 128:(pr + 1) * 128], rhs=v16[:, pr * 128:(pr + 1) * 128], start=True, stop=True)
            return ins
        kb.op("pe", fR, r=[kdec, v16], w=[pother])
        yield
        trs = []
        for pr in range(2):
            tr_ = H.t32()
            H.TT("dve", tr_, tr_[:], pother[:, hc(pr)], bd[:], ALU.mult, [pother, bd])
            trs.append(tr_)
        yield
        for pr in range(2):
            H.STT("dve", R32[pr], R32[pr][:], R32[pr][:], retc[:, 1024 + pr:1025 + pr], trs[pr][:], ALU.mult, ALU.add, [R32[pr], trs[pr], retc])
        yield
        for pr in range(2):
            H.CP("act", R16[pr], R16[pr][:], R32[pr][:], [R32[pr]])
        yield

    def ln_gate():
        st = kb.rst[i]
        ov = obs[:].rearrange("p (h d) -> p h d", d=64)
        kb.op("dve", lambda e: e.tensor_reduce(out=st[:, 0:4], in_=ov, axis=AX.X, op=ALU.add), r=[obs], w=[st])
        sq = H.tw()
        H.TT("pool", sq, sq[:], obs[:], obs[:], ALU.mult, [obs])
        yield
        kb.op("dve", lambda e: e.tensor_reduce(out=st[:, 4:8], in_=sq[:].rearrange("p (h d) -> p h d", d=64), axis=AX.X, op=ALU.add), r=[sq], w=[st])
        yield
        H.TS("dve", st, st[:, 0:4], st[:, 0:4], 1.0 / 64, ALU.mult, [st])
        yield
        H.TT("dve", st, st[:, 8:12], st[:, 0:4], st[:, 0:4], ALU.mult, [st])
        yield
        H.STT("dve", st, st[:, 4:8], st[:, 4:8], 1.0 / 64, st[:, 8:12], ALU.mult, ALU.subtract, [st])
        yield
        H.ACT(st, st[:, 4:8], st[:, 4:8], AF.Ln, [st], bias=kb.epsc[1e-6][:])
        yield
        H.ACT(st, st[:, 4:8], st[:, 4:8], AF.Exp, [st], scale=-0.5)
        yield
        H.STT("dve", st, st[:, 8:12], st[:, 0:4], -1.0, st[:, 4:8], ALU.mult, ALU.mult, [st])
        sg = H.tw()
        H.ACT(sg, sg[:], tm[:, 1288:1544], AF.Silu, [tm])
        yield
        yn = H.tw()
        for h in range(4):
            H.TS("dve", yn, yn[:, h * 64:(h + 1) * 64], obs[:, h * 64:(h + 1) * 64], st[:, 4 + h:5 + h], ALU.mult, [obs, st], s2=st[:, 8 + h:9 + h], op1=ALU.add)
        yield
        H.TT("pool", mx, mx[:, 512:768], yn[:], sg[:], ALU.mult, [yn, sg])
        yield
    yield from zip_gens(r_update(), ln_gate())


def gdn_prep(kb, H, LoffTn, I32, n, h, qTc, kTc, ktok, vtok, cm, g_all, gc_all, ekd, negbeta, bg, beta,
             attnT, qdecT, wT16, kdec16, u32):
    triU, strictL, negmT, bd, offm, offT, strictT, ones32, negones = cm
    col = n * 4 + h
    cs = slice(col, col + 1)
    hs = slice(h * 128, (h + 1) * 128)
    Gtri = H.t32()
    H.TS("act", Gtri, Gtri[:], triU[:], g_all[:, cs], ALU.mult, [triU, g_all])
    H.TS("act", kdec16, kdec16[:], ktok[:, hs], ekd[:, cs], ALU.mult, [ktok, ekd])
    rhs = H.tw()
    H.TS("act", rhs, rhs[:, 0:128], vtok[:, hs], beta[:, cs], ALU.mult, [vtok, beta])
    H.TS("act", rhs, rhs[:, 128:256], ktok[:, hs], bg[:, cs], ALU.mult, [ktok, bg])
    yield
    psG = H.pn()
    H.MM(psG, psG[:, 0:128], [(ones32[:], Gtri[:])], [ones32, Gtri])
    psK = H.pn()
    H.MM(psK, psK[:, 128:256], [(kTc[:, hs], kTc[:, hs])], [kTc])
    yield
    tD = H.t32()
    H.STT("dve", tD, tD[:], psG[:, 0:128], gc_all[:, cs], negmT[:], ALU.subtract, ALU.add, [psG, gc_all, negmT])
    yield
    ebc = H.t32()
    H.ACT(ebc, ebc[:], psG[:, 0:128], AF.Exp, [psG])
    DTi = H.t32()
    H.ACT(DTi, DTi[:], tD[:], AF.Exp, [tD])
    yield
    H.TT("dve", qdecT, qdecT[:], qTc[:, hs], ebc[:], ALU.mult, [qTc, ebc])
    psT = H.pn()
    kb.tr(psT, psT[:, 256:384], DTi[:], I32[:], r=[DTi, I32])
    yield
    Dst = H.t32()
    H.TT("dve", Dst, Dst[:], psT[:, 256:384], strictL[:], ALU.mult, [psT, strictL])
    yield
    Xf = H.t32()
    H.STT("dve", Xf, Xf[:], psK[:, 128:256], negbeta[:, cs], Dst[:], ALU.mult, ALU.mult, [psK, negbeta, Dst])
    yield
    psX = H.pn()
    kb.tr(psX, psX[:, 384:512], Xf[:], I32[:], r=[Xf, I32])
    ev0 = H.tev()
    H.TT("dve", ev0, ev0[:, 0:128], Xf[:], bd[:], ALU.mult, [Xf, bd])
    yield
    XfT = H.t32()
    H.CP("act", XfT, XfT[:], psX[:, 384:512], [psX])
    yield
    H.TT("dve", ev0, ev0[:, 256:384], XfT[:], bd[:], ALU.mult, [XfT, bd])
    H.TT("pool", LoffTn, LoffTn[:], XfT[:], offT[:], ALU.mult, [XfT, offT])
    psQ = H.pn()
    H.MM(psQ, psQ[:, 0:128], [(kTc[:, hs], qTc[:, hs])], [kTc, qTc])
    yield
    H.TT("dve", attnT, attnT[:], psQ[:, 0:128], DTi[:], ALU.mult, [psQ, DTi])
    yield
    TTt = yield from inverse(H, ev0, LoffTn, I32)
    psS = H.pn()
    H.MM(psS, psS[:, 0:256], [(TTt[:], rhs[:])], [TTt, rhs])
    yield
    H.CP("act", u32, u32[:], psS[:, 0:128], [psS])
    w32 = H.t32()
    H.CP("act", w32, w32[:], psS[:, 128:256], [psS])
    yield
    psW = H.pn()
    kb.tr(psW, psW[:, 256:384], w32[:], I32[:], r=[w32, I32])
    yield
    H.CP("dve", wT16, wT16[:], psW[:, 256:384], [psW])
    yield


def gdn_scan(kb, H, n, i, attnT, qdecT, wT16, kdec16, u32, S32, S16, dch, osb, tm, gnw, mx, junk):
    B = H.pn()
    hc = lambda h: slice(h * 128, (h + 1) * 128)

    def f1(e):
        ins = None
        for h in range(4):
            ins = e.matmul(B[:, hc(h)], lhsT=wT16[h][:], rhs=S16[h][:], start=True, stop=True)
        return ins
    kb.op("pe", f1, r=wT16 + S16, w=[B])
    yield
    vn = []
    for h in range(4):
        v_ = H.t16()
        H.TT("dve", v_, v_[:], u32[h][:], B[:, hc(h)], ALU.subtract, [u32[h], B])
        vn.append(v_)
    yield

    def f2(e):
        ins = None
        for h in range(4):
            e.matmul(B[:, hc(h)], lhsT=qdecT[h][:], rhs=S16[h][:], start=True, stop=False)
            ins = e.matmul(B[:, hc(h)], lhsT=attnT[h][:], rhs=vn[h][:], start=False, stop=True)
        return ins
    kb.op("pe", f2, r=qdecT + S16 + attnT + vn, w=[B])
    yield
    H.CP("act", osb, osb[:, 0:512], B[:, 0:512], [B])
    yield

    def f3(e):
        ins = None
        for h in range(4):
            ins = e.matmul(B[:, hc(h)], lhsT=kdec16[h][:], rhs=vn[h][:], start=True, stop=True)
        return ins
    kb.op("pe", f3, r=kdec16 + vn, w=[B])
    yield
    for h in range(4):
        col = n * 4 + h
        H.STT("dve", S32[h], S32[h][:], S32[h][:], dch[:, col:col + 1], B[:, hc(h)], ALU.mult, ALU.add, [S32[h], dch, B])
    yield
    for h in range(4):
        H.CP("act" if h % 2 == 0 else "pool", S16[h], S16[h][:], S32[h][:], [S32[h]])
    yield
    st = kb.gst[i]
    for h in range(4):
        kb.op("act", lambda e, h=h: e.activation(out=junk[:], in_=osb[:, h * 128:(h + 1) * 128], func=AF.Square, accum_out=st[:, h:h + 1]), r=[osb], w=[junk, st])
    yield
    H.ACT(st, st[:, 4:8], st[:, 0:4], AF.Ln, [st], bias=kb.epsc[1e-6][:], scale=1.0 / 128)
    yield
    H.ACT(st, st[:, 4:8], st[:, 4:8], AF.Exp, [st], scale=-0.5)
    sz = kb.gsz[i]
    H.ACT(sz, sz[:], tm[:, 0:512], AF.Silu, [tm])
    yield
    for h in range(4):
        H.TT("pool", sz, sz[:, h * 128:(h + 1) * 128], sz[:, h * 128:(h + 1) * 128], gnw[:], ALU.mult, [sz, gnw])
    yield
    for h in range(4):
        H.STT("dve", mx, mx[:, h * 128:(h + 1) * 128], osb[:, h * 128:(h + 1) * 128], st[:, 4 + h:5 + h], sz[:, h * 128:(h + 1) * 128], ALU.mult, ALU.mult, [osb, st, sz])
    yield


def rwkv_pre(kb, H, l, i, rwt, lora, v12):
    sl = lambda j: slice(j * 128, (j + 1) * 128)
    tw_ = kb.rwtw[i]
    H.ACT(tw_, tw_[0:32, :], rwt[0:32, sl(6)], AF.Tanh, [rwt])
    H.ACT(tw_, tw_[64:128, :], rwt[64:128, sl(6)], AF.Sigmoid, [rwt])
    gtok = kb.rwg[i]
    psg = kb.pn()
    H.MM(psg, psg[:, 0:256], [(tw_[64:128, :], lora[64:128, 0:256])], [tw_, lora])
    H.CP("act", gtok, gtok[:], psg[:, 0:256], [psg])
    if l == 1:
        v1, v2 = v12
        psv = kb.pn()
        H.MM(psv, psv[0:32, 0:128], [(v1[:, 0:32], rwt[:, sl(4)]), (v1[:, 32:64], rwt[:, sl(5)])], [v1, rwt])
        vl = kb.rwvl[i]
        H.CP("act", vl, vl[0:32, :], psv[0:32, 0:128], [psv])


def rwkv_pair(kb, H, C, A, l, n, i, j, rwt, vft, lora, rwc, omka, sel, cm, v12):
    triU, strictL, negmT, bd, offm, offT, strictT, ones32, negones = cm
    I32, I16 = C["ident32"], C["ident16"]
    ts = slice(n * 128, (n + 1) * 128)
    sl = lambda q: slice(q * 128, (q + 1) * 128)
    tw_ = kb.rwtw[i]
    vl = kb.rwvl[i]
    tk, ar, bk = kb.rwtok[i][j], kb.rwar[i][j], kb.rwbk[i][j]
    v32, gcol, bs, bcol = kb.rwv32[i], kb.rwgc[i], kb.rwbs[i], kb.rwbc[i]
    ch = sl(j)
    rT, kT_, vT = rwt[:, sl(j)], rwt[:, sl(2 + j)], rwt[:, sl(4 + j)]
    psw = H.pn()
    H.MM(psw, psw[:, 0:128], [(lora[0:32, ch], tw_[0:32, :])], [lora, tw_])
    psa = H.pn()
    H.MM(psa, psa[:, 0:128], [(lora[32:64, ch], rwt[32:64, sl(6)])], [lora, rwt])
    kk0 = H.t32()
    H.TS("act", kk0, kk0[:], kT_, rwc[:, 4 + j:5 + j], ALU.mult, [rwt, rwc])
    yield
    sgd = H.t32()
    H.ACT(sgd, sgd[:], psw[:, 0:128], AF.Sigmoid, [psw, rwc], bias=rwc[:, j:j + 1])
    sq = H.t32()
    H.TT("dve", sq, sq[:], kk0[:], kk0[:], ALU.mult, [kk0])
    yield
    aa = H.t32()
    H.ACT(aa, aa[:], psa[:, 0:128], AF.Sigmoid, [psa, rwc], bias=rwc[:, 2 + j:3 + j])
    yield
    pst = H.pn()
    kb.tr(pst, pst[:, 0:128], sgd[:], I32[:], r=[sgd, I32])
    yield
    sgtok = H.t32()
    H.CP("act", sgtok, sgtok[:], pst[:, 0:128], [pst])
    psq = H.pn()
    H.MM(psq, psq[:, 0:128], [(bd[:], sq[:])], [bd, sq])
    yield
    rs = H.t32()
    H.ACT(rs, rs[:], psq[:, 0:128], AF.Ln, [psq], bias=kb.epsc[1e-6][:])
    yield
    psc = H.pn()
    H.MM(psc, psc[:, 0:128], [(sgtok[:], triU[:])], [sgtok, triU])
    H.ACT(rs, rs[:], rs[:], AF.Exp, [rs], scale=-0.5)
    yield
    kk = H.t32()
    H.TT("dve", kk, kk[:], kk0[:], rs[:], ALU.mult, [kk0, rs])
    tka = H.t32()
    H.TS("dve", tka, tka[:], aa[:], rwc[:, 6 + j:7 + j], ALU.mult, [aa, rwc, omka], s2=omka[:, j:j + 1], op1=ALU.add)
    yield
    Scp = H.t32()
    H.CP("dve", Scp, Scp[:], psc[:, 0:128], [psc])
    yield
    E1, Einv, E1x, Ehat, Sx = H.t32(), H.t32(), H.t32(), H.t32(), H.t32()
    H.ACT(E1, E1[:], psc[:, 0:128], AF.Exp, [psc], scale=WSC)
    H.ACT(Einv, Einv[:], psc[:, 0:128], AF.Exp, [psc], scale=-WSC)
    H.TT("dve", Sx, Sx[:], Scp[:], sgd[:], ALU.subtract, [Scp, sgd])
    H.TS("dve", bcol, bcol[:, j:j + 1], Scp[:, 127:128], WSC, ALU.mult, [Scp])
    yield
    kmod = H.t32()
    H.TT("pool", kmod, kmod[:], kT_, tka[:], ALU.mult, [rwt, tka])
    bp = H.t32()
    H.TT("pool", bp, bp[:], kk[:], aa[:], ALU.mult, [kk, aa])
    H.ACT(E1x, E1x[:], Sx[:], AF.Exp, [Sx], scale=WSC)
    H.ACT(Ehat, Ehat[:], Scp[:], AF.Exp, [Scp, bcol], scale=-WSC, bias=bcol[:, j:j + 1])
    yield
    if l == 1:
        v1, v2 = v12
        psv2 = H.pn()
        H.MM(psv2, psv2[:, 0:128], [(v2[0:32, ch], vl[0:32, :])], [v2, vl])
        yield
        sgv = H.t32()
        H.ACT(sgv, sgv[:], psv2[:, 0:128], AF.Sigmoid, [psv2, rwc], bias=rwc[:, 10 + j:11 + j])
        dv = H.t32()
        H.TT("dve", dv, dv[:], vft[:, ch], vT, ALU.subtract, [vft, rwt])
        yield
        H.TT("pool", dv, dv[:], dv[:], sgv[:], ALU.mult, [dv, sgv])
        yield
        vu = H.t32()
        H.TT("pool", vu, vu[:], dv[:], vT, ALU.add, [dv, rwt])
        yield
        vuse, vub = vu[:], vu
    else:
        kb.dma("sp", A["vfT"][ch, ts], vT, r=[rwt])
        vuse, vub = vT, rwt
    H.CP("act", gcol, gcol[:, j:j + 1], E1[:, 127:128], [E1])
    H.STT("dve", ar, ar[:, 0:128], kk[:], -1.0, E1x[:], ALU.mult, ALU.mult, [kk, E1x])
    H.TT("pool", ar, ar[:, 128:256], rT, E1[:], ALU.mult, [rwt, E1])
    yield
    bhT, khT, v16T = H.t16(), H.t16(), H.t16()
    H.TT("dve", bk, bk[:, 0:128], bp[:], Einv[:], ALU.mult, [bp, Einv])
    H.TT("pool", bk, bk[:, 128:256], kmod[:], Einv[:], ALU.mult, [kmod, Einv])
    yield
    H.TT("dve", bhT, bhT[:], bp[:], Ehat[:], ALU.mult, [bp, Ehat])
    H.TT("pool", khT, khT[:], kmod[:], Ehat[:], ALU.mult, [kmod, Ehat])
    yield
    H.CP("act", v16T, v16T[:], vuse, [vub])
    rk_ = H.t32()
    H.STT("dve", rk_, rk_[:], rT, rwc[:, 8 + j:9 + j], kmod[:], ALU.mult, ALU.mult, [rwt, rwc, kmod])
    yield
    ps4 = H.pn()
    psb4 = ps4[:].bitcast(BF16)
    srcs = [bhT, khT, v16T]

    def trf(e):
        ins = None
        for q, s_ in enumerate(srcs):
            ins = e.transpose(psb4[:, q * 128:(q + 1) * 128], s_[:], I16[:])
        ins = e.transpose(psb4[:, 384:512], ar[:, 0:128], I16[:])
        return ins
    kb.op("pe", trf, r=srcs + [ar, I16], w=[ps4])
    yield
    H.CP("act", tk, tk[:], psb4[:, 0:512], [ps4])
    ps5 = H.pn()
    kb.tr(ps5, ps5[:, 0:128], vuse, I32[:], r=[vub, I32])
    yield
    H.CP("dve", v32, v32[:, ch], ps5[:, 0:128], [ps5])
    yield
    ps6 = H.pn()
    H.MM(ps6, ps6[:, 0:2], [(rk_[:], sel[:])], [rk_, sel])
    yield
    H.CP("dve", bs, bs[:, 2 * j:2 * j + 2], ps6[:, 0:2], [ps6])
    yield


def rwkv_head(kb, H, LoffTn, I32, i, j, hh, cm):
    triU, strictL, negmT, bd, offm, offT, strictT, ones32, negones = cm
    tk, ar, bk = kb.rwtok[i][j], kb.rwar[i][j], kb.rwbk[i][j]
    PaT, Qv = kb.rwpa[i][j], kb.rwqv[i][j]
    pb = hh * 64
    hg = 2 * j + hh
    At = kb.rwA[i][hg]
    btT = bk[pb:pb + 64, 0:128]
    ktT = bk[pb:pb + 64, 128:256]
    psA3 = H.pn()
    H.MM(psA3, psA3[:, 0:128], [(ar[pb:pb + 64, 0:128], btT)], [bk, ar])
    yield
    Xf = H.t32()
    H.TT("dve", Xf, Xf[:], psA3[:, 0:128], strictL[:], ALU.mult, [psA3, strictL])
    yield
    psA1 = H.pn()
    H.MM(psA1, psA1[:, 128:384], [(btT, ar[pb:pb + 64, 0:256])], [bk, ar])
    ev0 = H.tev()
    H.TT("dve", ev0, ev0[:, 0:128], Xf[:], bd[:], ALU.mult, [Xf, bd])
    yield
    XfT = H.t32()
    H.TT("dve", XfT, XfT[:], psA1[:, 128:256], strictT[:], ALU.mult, [psA1, strictT])
    H.TT("dve", At, At[:, 0:128], psA1[:, 256:384], triU[:], ALU.mult, [psA1, triU])
    yield
    psA2 = H.pn()
    H.MM(psA2, psA2[:, 0:256], [(ktT, ar[pb:pb + 64, 0:256])], [bk, ar])
    H.TT("dve", ev0, ev0[:, 256:384], XfT[:], bd[:], ALU.mult, [XfT, bd])
    H.TT("pool", LoffTn, LoffTn[:], XfT[:], offT[:], ALU.mult, [XfT, offT])
    yield
    H.TT("dve", At, At[:, 128:256], psA2[:, 128:256], triU[:], ALU.mult, [psA2, triU])
    AakT = H.t16()
    H.TT("dve", AakT, AakT[:], psA2[:, 0:128], strictT[:], ALU.mult, [psA2, strictT])
    yield
    TTt = yield from inverse(H, ev0, LoffTn, I32)
    psAV = H.pn()
    H.MM(psAV, psAV[:, 0:64], [(AakT[:], tk[:, 256 + pb:256 + pb + 64])], [AakT, tk])
    yield
    AVs = H.t32()
    H.CP("act", AVs, AVs[:, 0:64], psAV[:, 0:64], [psAV])
    yield
    TT16 = H.t16()
    H.CP("act", TT16, TT16[:], TTt[:], [TTt])
    psQv = H.pn()
    H.MM(psQv, psQv[:, 64:128], [(TTt[:], AVs[:, 0:64])], [TTt, AVs])
    yield
    H.CP("dve", Qv, Qv[:, pb:pb + 64], psQv[:, 64:128], [psQv])
    yield
    psP = H.pn()
    H.MM(psP, psP[:, 128:256], [(tk[:, 384:512], TT16[:])], [tk, TT16])
    yield
    H.CP("act", PaT, PaT[pb:pb + 64, :], psP[pb:pb + 64, 128:256], [psP])
    yield


def rwkv_tail(kb, H, A, i, ts, bd, W32, W16, ysb, lnw, lnb, mx):
    tokm, PaT, Qv, arT, AT, gcol = kb.rwtok[i], kb.rwpa[i], kb.rwqv[i], kb.rwar[i], kb.rwA[i], kb.rwgc[i]
    B = H.pn()
    jc = lambda j: slice(j * 128, (j + 1) * 128)

    def fU(e):
        ins = None
        for j in range(2):
            ins = e.matmul(B[:, jc(j)], lhsT=PaT[j][:], rhs=W16[j][:], start=True, stop=True)
        return ins
    kb.op("pe", fU, r=PaT + W16, w=[B])
    yield
    U16 = []
    for j in range(2):
        u_ = H.t16()
        H.TT("dve", u_, u_[:], B[:, jc(j)], Qv[j][:], ALU.add, [B, Qv[j]])
        U16.append(u_)
    yield

    def fY(e):
        ins = None
        for j in range(2):
            tk = tokm[j]
            for hh in range(2):
                hg = 2 * j + hh
                es = slice(hh * 64, (hh + 1) * 64)
                o = B[:, 256 + hg * 64:256 + (hg + 1) * 64]
                e.matmul(o, lhsT=arT[j][:, 128:256], rhs=W16[j][:, es], start=True, stop=False)
                e.matmul(o, lhsT=AT[hg][:, 0:128], rhs=U16[j][:, es], start=False, stop=False)
                ins = e.matmul(o, lhsT=AT[hg][:, 128:256], rhs=tk[:, 256 + hh * 64:256 + (hh + 1) * 64], start=False, stop=True)
        return ins
    kb.op("pe", fY, r=arT + W16 + AT + U16 + tokm, w=[B])
    yield
    H.CP("act", ysb, ysb[:], B[:, 256:512], [B])
    yield

    def w_update():
        def fH(e):
            ins = None
            for j in range(2):
                tk = tokm[j]
                e.matmul(B[:, jc(j)], lhsT=tk[:, 0:128], rhs=U16[j][:], start=True, stop=False)
                ins = e.matmul(B[:, jc(j)], lhsT=tk[:, 128:256], rhs=tk[:, 256:384], start=False, stop=True)
            return ins
        kb.op("pe", fH, r=tokm + U16, w=[B])
        yield
        tmps = []
        for j in range(2):
            tmpH = H.t32()
            H.TT("dve", tmpH, tmpH[:], B[:, jc(j)], bd[:], ALU.mult, [B, bd])
            tmps.append(tmpH)
        yield
        for j in range(2):
            H.STT("dve", W32[j], W32[j][:], W32[j][:], gcol[:, j:j + 1], tmps[j][:], ALU.mult, ALU.add, [W32[j], gcol, tmps[j]])
        yield
        for j in range(2):
            H.CP("act", W16[j], W16[j][:], W32[j][:], [W32[j]])
        yield

    def out_chain():
        st, v32, bs, gtok = kb.wst[i], kb.rwv32[i], kb.rwbs[i], kb.rwg[i]
        kb.op("dve", lambda e: e.tensor_reduce(out=st[:, 0:4], in_=ysb[:].rearrange("p (h d) -> p h d", d=64), axis=AX.X, op=ALU.add), r=[ysb], w=[st])
        sq2 = H.tw()
        H.TT("pool", sq2, sq2[:], ysb[:], ysb[:], ALU.mult, [ysb])
        yield
        kb.op("dve", lambda e: e.tensor_reduce(out=st[:, 4:8], in_=sq2[:].rearrange("p (h d) -> p h d", d=64), axis=AX.X, op=ALU.add), r=[sq2], w=[st])
        yield
        H.TS("dve", st, st[:, 0:4], st[:, 0:4], 1.0 / 64, ALU.mult, [st])
        yield
        H.TT("dve", st, st[:, 8:12], st[:, 0:4], st[:, 0:4], ALU.mult, [st])
        yield
        H.STT("dve", st, st[:, 4:8], st[:, 4:8], 1.0 / 64, st[:, 8:12], ALU.mult, ALU.subtract, [st])
        yield
        H.ACT(st, st[:, 4:8], st[:, 4:8], AF.Ln, [st], bias=kb.epsc[64e-5][:])
        yield
        H.ACT(st, st[:, 4:8], st[:, 4:8], AF.Exp, [st], scale=-0.5)
        yield
        H.STT("dve", st, st[:, 8:12], st[:, 0:4], -1.0, st[:, 4:8], ALU.mult, ALU.mult, [st])
        yield
        yn = H.tw()
        for h in range(4):
            hs = slice(h * 64, (h + 1) * 64)
            H.TS("dve", yn, yn[:, hs], ysb[:, hs], st[:, 4 + h:5 + h], ALU.mult, [ysb, st], s2=st[:, 8 + h:9 + h], op1=ALU.add)
        yield
        H.TT("pool", yn, yn[:], yn[:], lnw[:], ALU.mult, [yn, lnw])
        yield
        H.TT("pool", yn, yn[:], yn[:], lnb[:], ALU.add, [yn, lnb])
        yield
        for h in range(4):
            hs = slice(h * 64, (h + 1) * 64)
            H.STT("dve", yn, yn[:, hs], v32[:, hs], bs[:, h:h + 1], yn[:, hs], ALU.mult, ALU.add, [v32, bs, yn])
        yield
        H.TT("dve", mx, mx[:, 768:1024], yn[:], gtok[:], ALU.mult, [yn, gtok])
        yield
        kb.dma("sp", A["mix"][ts, :], mx[:], r=[mx])

    yield from zip_gens(w_update(), out_chain())


def phase3a(kb, A, l, xsrc):
    kb.psum_pool(8)
    make_consts(kb, A, [1e-6])
    ident16 = kb.sb([128, 128], BF16)
    kb.dma("pool", ident16[:], A["ident"], w=[ident16])
    wout = kb.sb([128, 8 * 1024], BF16)
    for k in range(8):
        kb.dma("pool", wout[:, k * 1024:(k + 1) * 1024], A["wout"][l, :, k * 1024:(k + 1) * 1024], w=[wout])
    bc = {}
    for nm, i in (("g1", 2), ("w2", 3), ("s2", 4)):
        bc[nm] = kb.sb([128, 1024], F32, nm)
        kb.dma("sp", bc[nm][:], A["modrow"][l, 0:1, i * 1024:(i + 1) * 1024].partition_broadcast(128), w=[bc[nm]])
    for k in range(8):
        kb.op("dve" if k % 2 == 0 else "pool", lambda e, k=k: e.tensor_tensor(out=wout[:, k * 1024:(k + 1) * 1024], in0=wout[:, k * 1024:(k + 1) * 1024],
                                                                            in1=bc["g1"][:], op=ALU.mult), r=[wout, bc["g1"]], w=[wout])
    junk = kb.sb([128, 1024], BF16)
    slots = [dict(b0=2 * s, b1=2 * s + 1, mx=kb.sb([128, 1024], BF16, "mx"), xt=kb.sb([128, 1024], F32, "xt"), mT=kb.sb([128, 1024], BF16, "mT"),
                  t1=kb.sb([128, 1024], F32, "t1"), xm=kb.sb([128, 1024], F32, "xm"), ss=kb.sb([128, 1], F32), rstd=kb.sb([128, 1], F32),
                  xn=kb.sb([128, 1024], BF16, "xn"), hT=kb.sb([128, 1024], BF16, "hT")) for s in range(4)]

    def tile(S, n):
        mx, xt, mT, t1, xm, ss, rstd, xn, hT = (S[k] for k in ("mx", "xt", "mT", "t1", "xm", "ss", "rstd", "xn", "hT"))
        pa, pb = kb.pstiles[S["b0"]], kb.pstiles[S["b1"]]
        t0 = n * 128
        kb.dma("sp", mx[:], A["mix"][t0:t0 + 128, :], w=[mx])
        kb.dma("sp", xt[:], xsrc[t0:t0 + 128, :], w=[xt])
        yield
        psb = pa[:].bitcast(BF16)

        def trf(e):
            ins = None
            for k in range(8):
                ins = e.transpose(psb[:, k * 128:(k + 1) * 128], mx[:, k * 128:(k + 1) * 128], ident16[:])
            return ins
        kb.op("pe", trf, r=[mx, ident16], w=[pa])
        yield
        kb.op("act", lambda e: e.copy(out=mT[:], in_=psb), r=[pa], w=[mT])
        yield
        for hf, ps2 in ((0, pb), (1, pa)):
            hs = slice(hf * 512, (hf + 1) * 512)
            kb.mm(ps2, ps2[:, 0:512], [(mT[:, k * 128:(k + 1) * 128], wout[:, k * 1024 + hf * 512:k * 1024 + (hf + 1) * 512]) for k in range(8)], r=[mT, wout])
            yield
            kb.op("dve", lambda e, ps2=ps2, hs=hs: e.tensor_tensor(out=xm[:, hs], in0=ps2[:, 0:512], in1=xt[:, hs], op=ALU.add), r=[ps2, xt], w=[xm])
            yield
        kb.dma("sp", A["xmid"][t0:t0 + 128, :], xm[:], r=[xm])
        kb.op("act", lambda e: e.activation(out=junk[:], in_=xm[:], func=AF.Square, accum_out=ss[:]), r=[xm], w=[junk, ss])
        yield
        kb.op("act", lambda e: e.activation(out=rstd[:], in_=ss[:], func=AF.Sqrt, scale=1.0 / 1024, bias=kb.epsc[1e-6][:]), r=[ss], w=[rstd])
        yield
        kb.op("dve", lambda e: e.reciprocal(out=rstd[:], in_=rstd[:]), r=[rstd], w=[rstd])
        yield
        kb.op("dve", lambda e: e.scalar_tensor_tensor(out=t1[:], in0=xm[:], scalar=rstd[:, 0:1], in1=bc["w2"][:], op0=ALU.mult, op1=ALU.mult),
              r=[xm, rstd, bc["w2"]], w=[t1])
        yield
        kb.op("pool", lambda e: e.tensor_tensor(out=xn[:], in0=t1[:], in1=bc["s2"][:], op=ALU.add), r=[t1, bc["s2"]], w=[xn])
        yield
        psb3 = pb[:].bitcast(BF16)

        def trf3(e):
            ins = None
            for k in range(8):
                ins = e.transpose(psb3[:, k * 128:(k + 1) * 128], xn[:, k * 128:(k + 1) * 128], ident16[:])
            return ins
        kb.op("pe", trf3, r=[xn, ident16], w=[pb])
        yield
        kb.op("act", lambda e: e.copy(out=hT[:], in_=psb3), r=[pb], w=[hT])
        yield
        kb.dma("sp", A["h2T"][:, :, t0:t0 + 128].rearrange("k p t -> p k t"), hT[:].rearrange("p (k t) -> p k t", k=8), r=[hT])

    Pipe({"s": slots}).run([("s", (lambda S, n=n: tile(S, n))) for n in range(NCH)])
    kb.end_phase()


def phase3b(kb, A, l, xdst, final):
    kb.psum_pool(8)
    make_consts(kb, A, [1e-6])
    wup = [kb.sb([128, 4096], BF16, "wup") for _ in range(8)]
    for k in range(8):
        kb.dma("pool", wup[k][:], A["wup"][l, :, k * 4096:(k + 1) * 4096], w=[wup[k]])
    wdn = [kb.sb([128, 4096], BF16, "wdn") for _ in range(8)]
    for k in range(8):
        kb.dma("pool", wdn[k][:], A["wdn"][l, :, k * 4096:(k + 1) * 4096], w=[wdn[k]])
    g2 = kb.sb([128, 1024], F32)
    kb.dma("sp", g2[:], A["modrow"][l, 0:1, 5 * 1024:6 * 1024].partition_broadcast(128), w=[g2])
    if final:
        fnw = kb.sb([128, 1024], F32)
        kb.dma("sp", fnw[:], A["fnw"][0:1, :].partition_broadcast(128), w=[fnw])
    TB = 512
    hT = kb.sb([128, 8 * TB], BF16, "hT")
    uT = kb.sb([128, 32 * TB], BF16, "uT")
    rl = [kb.sb([128, TB], F32, "rl") for _ in range(2)]
    xm = [kb.sb([128, 1024], F32, "xm") for _ in range(2)]
    t1 = [kb.sb([128, 512], F32, "t1") for _ in range(2)]
    xo = [kb.sb([128, 1024], F32, "xo") for _ in range(2)]
    junk = kb.sb([128, 1024], BF16)
    ss = [kb.sb([128, 1], F32) for _ in range(2)]
    rstd = [kb.sb([128, 1], F32) for _ in range(2)]
    ri = 0
    ti = 0
    h, u = hT, uT
    NBLK = T // TB

    def load_h(blk):
        kb.dma("sp", h[:].rearrange("p (k t) -> p k t", k=8), A["h2T"][:, :, blk * TB:(blk + 1) * TB].rearrange("k p t -> p k t"), w=[h])
    load_h(0)
    for blk in range(NBLK):
        t0 = blk * TB
        for f in range(32):
            ps = kb.pn()
            kb.mm(ps, ps[:, 0:TB], [(wup[k][:, f * 128:(f + 1) * 128], h[:, k * TB:(k + 1) * TB]) for k in range(8)], r=[h] + wup)
            r_ = rl[ri % 2]
            ri += 1
            kb.op("act", lambda e, ps=ps, r_=r_: e.activation(out=r_[:], in_=ps[:, 0:TB], func=AF.Relu), r=[ps], w=[r_])
            eng = "pool" if f % 3 == 0 else "dve"
            kb.op(eng, lambda e, r_=r_, f=f: e.tensor_tensor(out=u[:, f * TB:(f + 1) * TB], in0=r_[:], in1=r_[:], op=ALU.mult), r=[r_], w=[u])
        if blk + 1 < NBLK:
            load_h(blk + 1)
        for j in range(TB // 128):
            i = ti % 2
            ti += 1
            tt = t0 + j * 128
            kb.dma("sp", xm[i][:], A["xmid"][tt:tt + 128, :], w=[xm[i]])
            for hf in range(2):
                ps2 = kb.pn()
                kb.mm(ps2, ps2[:, 0:512], [(u[:, f * TB + j * 128:f * TB + (j + 1) * 128], wdn[f // 4][:, (f % 4) * 1024 + hf * 512:(f % 4) * 1024 + (hf + 1) * 512]) for f in range(32)], r=[u] + wdn)
                kb.op("dve", lambda e, i=i, ps2=ps2, hf=hf: e.tensor_tensor(out=t1[i][:], in0=ps2[:, 0:512], in1=g2[:, hf * 512:(hf + 1) * 512], op=ALU.mult), r=[ps2, g2], w=[t1[i]])
                kb.op("pool", lambda e, i=i, hf=hf: e.tensor_tensor(out=xo[i][:, hf * 512:(hf + 1) * 512], in0=t1[i][:], in1=xm[i][:, hf * 512:(hf + 1) * 512], op=ALU.add), r=[t1[i], xm[i]], w=[xo[i]])
            if not final:
                kb.dma("sp", xdst[tt:tt + 128, :], xo[i][:], r=[xo[i]])
            else:
                rms_rstd(kb, xo[i], junk, ss[i], rstd[i], 1e-6, 1024)
                kb.op("dve", lambda e, i=i: e.scalar_tensor_tensor(out=xm[i][:], in0=xo[i][:], scalar=rstd[i][:, 0:1], in1=fnw[:], op0=ALU.mult, op1=ALU.mult),
                      r=[xo[i], rstd[i], fnw], w=[xm[i]])
                kb.dma("sp", xdst[tt:tt + 128, :], xm[i][:], r=[xm[i]])
    kb.end_phase()


SHARED = ["adaw", "adab", "n1w", "n2w", "fnw", "win", "convw", "alog", "dtb", "gnw", "mu", "lora", "rwcol",
          "lnw", "lnb", "v1", "v2", "wout", "wup", "wdn", "ident", "rope", "cmask", "retc"]


def declare(nc, dbg=()):
    A = {}

    def inp(name, shape, dt=F32):
        A[name] = nc.dram_tensor(name, list(shape), dt, kind="ExternalInput").ap()

    def scr(name, shape, dt=F32):
        kind = "ExternalOutput" if name in dbg else "Internal"
        A[name] = nc.dram_tensor(name, list(shape), dt, kind=kind).ap()
    inp("x", [T, D])
    inp("cT", [128, 8])
    inp("adaw", [2, 12, 128, 4096])
    inp("adab", [2, 1, 6144])
    inp("n1w", [2, 1, 1024])
    inp("n2w", [2, 1, 1024])
    inp("fnw", [1, 1024])
    inp("win", [2, 128, 8 * DIN])
    inp("convw", [2, 128, 48])
    inp("alog", [2, 1, 4])
    inp("dtb", [2, 1, 4])
    inp("gnw", [2, 1, 128])
    inp("mu", [2, 128, 7])
    inp("lora", [2, 128, 256])
    inp("rwcol", [2, 128, 12])
    inp("lnw", [2, 1, 256])
    inp("lnb", [2, 1, 256])
    inp("v1", [128, 64])
    inp("v2", [32, 256])
    inp("wout", [2, 128, 8 * 1024])
    inp("wup", [2, 128, 8 * 4096])
    inp("wdn", [2, 128, 32 * 1024])
    inp("ident", [128, 128])
    inp("rope", [T, 64])
    inp("cmask", [10, 128, 128])
    inp("retc", [128, 1026])
    A["out"] = nc.dram_tensor("out", [T, D], F32, kind="ExternalOutput").ap()
    scr("modrow", [2, 1, 6144])
    scr("qT", [2, 4, 128, T], BF16)
    scr("kT", [2, 4, 128, T], BF16)
    scr("ktok", [2, T, 512])
    scr("vtok", [2, T, 512])
    scr("rwT", [2, 896, T])
    scr("tm", [2, T, 1544])
    scr("vfT", [256, T])
    scr("mix", [T, 1024], BF16)
    scr("xmid", [T, D])
    scr("h2T", [8, 128, T], BF16)
    scr("xres", [T, D])
    return A


def build(upto=99, dbg=()):
    nc = bass.Bass("TRN2", target_bir_lowering=False)
    A = declare(nc, dbg)
    kb = KB(nc)
    step = 0

    def go():
        nonlocal step
        step += 1
        return step <= upto
    if go():
        phase0(kb, A)
    for l in range(2):
        xsrc = A["x"] if l == 0 else A["xres"]
        if go():
            phase1(kb, A, l, xsrc)
        if go():
            phase2(kb, A, l)
        if go():
            phase3a(kb, A, l, xsrc)
        if go():
            phase3b(kb, A, l, A["xres"] if l == 0 else A["out"], l == 1)
    kb.P.emit()
    return nc


def fm(v, nj):
    return np.ascontiguousarray(v.reshape(nj, 128).T)


def prep_shared(I):
    f = np.float32
    S = {}
    aw = I["ada_w"].reshape(2, 8, 128, 12, 512)
    S["adaw"] = np.ascontiguousarray(aw.transpose(0, 3, 2, 1, 4)).reshape(2, 12, 128, 4096)
    S["adab"] = I["ada_b"].reshape(2, 1, 6144)
    S["n1w"] = I["norm1_w"].reshape(2, 1, 1024)
    S["n2w"] = I["norm2_w"].reshape(2, 1, 1024)
    S["fnw"] = I["final_norm_w"].reshape(1, 1024)

    def kmaj(w, nk):
        L, _, N = w.shape
        return np.ascontiguousarray(w.reshape(L, nk, 128, N).transpose(0, 2, 1, 3)).reshape(L, 128, nk * N)
    S["win"] = kmaj(I["w_in"], 8)
    S["wout"] = kmaj(I["w_out"], 8)
    S["wup"] = kmaj(I["w_up"], 8)
    S["wdn"] = kmaj(I["w_down"], 32)
    cw = I["gdn_conv_w"]
    S["convw"] = np.ascontiguousarray(cw.reshape(2, 4, 12, 128).transpose(0, 3, 2, 1)).reshape(2, 128, 48)
    S["alog"] = I["gdn_a_log"].reshape(2, 1, 4)
    S["dtb"] = I["gdn_dt_bias"].reshape(2, 1, 4)
    S["gnw"] = I["gdn_norm_w"].reshape(2, 1, 128)
    S["mu"] = np.stack([fm(I["rwkv_mu"][l], 7) for l in range(2)])
    S["lora"] = np.ascontiguousarray(np.concatenate([I["rwkv_w2"], I["rwkv_a2"], I["rwkv_g2"]], axis=1))
    v0 = np.concatenate([np.zeros((1, 256), f), I["rwkv_v0"]], axis=0)
    S["rwcol"] = np.stack([np.concatenate([fm(I["rwkv_w0"][l], 2), fm(I["rwkv_a0"][l], 2), fm(I["rwkv_k_k"][l], 2),
                                           fm(I["rwkv_k_a"][l], 2), fm(I["rwkv_r_k"][l].reshape(256), 2), fm(v0[l], 2)], axis=1)
                           for l in range(2)])
    S["lnw"] = I["rwkv_ln_w"].reshape(2, 1, 256)
    S["lnb"] = I["rwkv_ln_b"].reshape(2, 1, 256)
    S["v1"] = np.ascontiguousarray(I["rwkv_v1"][0].reshape(2, 128, 32).transpose(1, 0, 2)).reshape(128, 64)
    S["v2"] = np.ascontiguousarray(I["rwkv_v2"][0])
    S["ident"] = np.eye(128, dtype=f)
    pos = np.arange(T, dtype=f)
    inv_freq = (np.float32(10000.0) ** (-np.arange(32, dtype=f) / np.float32(32))).astype(f)
    ang = (pos[:, None] * inv_freq[None, :]).astype(f)
    S["rope"] = np.concatenate([np.cos(ang), np.sin(ang)], axis=1).astype(f)
    i = np.arange(128)
    cm = np.zeros((10, 128, 128), f)
    cm[0] = (i[:, None] <= i[None, :])
    cm[1] = (i[:, None] > i[None, :])
    cm[2] = np.where(i[None, :] >= i[:, None], 0.0, NEGM)
    cm[3] = ((i[:, None] // 64) == (i[None, :] // 64))
    cm[4] = ((i[:, None] >= 64) & (i[None, :] < 64))
    cm[5] = cm[4].T
    cm[6] = (i[:, None] < i[None, :])
    cm[7] = 1.0
    cm[8] = -1.0
    S["cmask"] = cm
    rc = np.zeros((128, 1026), np.float64)
    for h in range(4):
        d = i[None, :] - i[:, None]
        rc[:, h * 128:(h + 1) * 128] = np.where(d >= 0, np.exp(np.maximum(d, 0) * LG[h]), 0.0)
    for pr in range(2):
        for p in range(128):
            hh = pr * 2 + p // 64
            rc[p, 512 + pr * 128:512 + (pr + 1) * 128] = np.exp((i + 1.0) * LG[hh])
    for h in range(4):
        rc[:, 768 + h * 64:768 + (h + 1) * 64] = (0.125 * np.exp((127.0 - i) * LG[h]))[:, None]
    for pr in range(2):
        for p in range(128):
            rc[p, 1024 + pr] = np.exp(128.0 * LG[pr * 2 + p // 64])
    S["retc"] = rc.astype(f)
    return {k: np.ascontiguousarray(v, dtype=f) for k, v in S.items()}


_NC = None


def kernel(**inputs):
    global _NC
    I = {k: np.asarray(v) for k, v in inputs.items()}
    S = prep_shared(I)
    if _NC is None:
        _NC = build()
    in_maps = []
    for b in range(8):
        m = dict(S)
        m["x"] = np.ascontiguousarray(I["x"][b])
        m["cT"] = fm(I["c"][b], 8)
        in_maps.append(m)
    res = run_bass_kernel_spmd(_NC, in_maps, core_ids=list(range(8)))
    return np.stack([np.asarray(r["out"]) for r in res.results]).astype(np.float32)
```

```python
import numpy as np
import concourse.bass as bass
import concourse.mybir as mybir
from concourse.bass_utils import run_bass_kernel_spmd

F32 = mybir.dt.float32
BF16 = mybir.dt.bfloat16
F32R = mybir.dt.float32r
AF = mybir.ActivationFunctionType
ALU = mybir.AluOpType
AX = mybir.AxisListType

T = 4096
D = 1024
DIN = 3976
NCH = 32
NEGM = -100000.0
WSC = -0.6065306597126334
LG = [float(np.log(1.0 - 2.0 ** (-5.0 - h))) for h in range(4)]


SEG = 30000


class Buf:
    __slots__ = ("name", "last_w", "readers", "excl")

    def __init__(self, name):
        self.name = name
        self.excl = False
        self.last_w = None
        self.readers = []


class Op:
    __slots__ = ("stream", "clock", "fn", "deps", "signals", "count", "is_dma")

    def __init__(self, stream, clock, fn, is_dma):
        self.stream = stream
        self.clock = clock
        self.fn = fn
        self.deps = []
        self.signals = is_dma
        self.count = None
        self.is_dma = is_dma


STREAMS = ("pe", "act", "dve", "pool", "sp")


class Prog:
    def __init__(self, nc):
        self.nc = nc
        self.ops = {s: [] for s in STREAMS}
        self.all_ops = []
        self.dma_r = {"sp": 12, "pool": 8, "act": 4}
        self.clocks = ["pe", "act", "dve", "pool"]
        for q, r in self.dma_r.items():
            self.clocks += [f"dma_{q}_{i}" for i in range(r)]
        self.dma_hist = {q: [] for q in self.dma_r}
        self.bufs = []

    def buf(self, name):
        b = Buf(name)
        self.bufs.append(b)
        return b

    def _add(self, stream, clock, fn, reads, writes, is_dma):
        op = Op(stream, clock, fn, is_dma)
        deps = []
        for b in reads:
            if b.last_w is not None:
                deps.append(b.last_w)
            if b.excl:
                for rd in b.readers:
                    if rd.stream != stream:
                        deps.append(rd)
        for b in writes:
            if b.last_w is not None:
                deps.append(b.last_w)
            deps.extend(b.readers)
        seen = set()
        for d in deps:
            if d is op or id(d) in seen:
                continue
            seen.add(id(d))
            if d.stream == "pe" and stream == "pe" and not d.is_dma and not is_dma:
                continue
            op.deps.append(d)
            d.signals = True
        for b in reads:
            b.readers.append(op)
        for b in writes:
            b.last_w = op
            b.readers = []
        self.ops[stream].append(op)
        self.all_ops.append(op)
        return op

    def op(self, eng, fn, reads=(), writes=()):
        return self._add(eng, eng, fn, reads, writes, False)

    def dma(self, q, out, in_, reads=(), writes=(), **kw):
        def fn(e):
            return e.dma_start(out=out, in_=in_, **kw)
        hist = self.dma_hist[q]
        i = len(hist)
        r = self.dma_r[q]
        op = self._add(q, f"dma_{q}_{i % r}", fn, reads, writes, True)
        if i >= r:
            prev = hist[i - r]
            if prev not in op.deps:
                op.deps.append(prev)
        hist.append(op)
        return op

    def barrier(self):
        last = {}
        for op in self.all_ops:
            if op.fn != "nop":
                last[op.clock] = op
        b = self.buf("barrier")
        for c, op in last.items():
            op.signals = True
        self._barrier_deps = list(last.values())
        for s in STREAMS:
            o = Op(s, s, None, False)
            o.deps = [d for d in last.values()]
            o.fn = "nop"
            self.ops[s].append(o)
            self.all_ops.append(o)
        for bb in self.bufs:
            bb.last_w = None
            bb.readers = []

    def emit(self, final_wait_stream="sp"):
        nc = self.nc
        counts = {c: 0 for c in self.clocks}
        for op in self.all_ops:
            if op.fn == "nop":
                continue
            if op.signals:
                counts[op.clock] += 16 if op.is_dma else 1
                op.count = counts[op.clock]
        fin = Op(final_wait_stream, final_wait_stream, "nop", False)
        lastdma = {}
        for op in self.all_ops:
            if op.is_dma:
                lastdma[op.clock] = op
        fin.deps = list(lastdma.values())
        self.ops[final_wait_stream].append(fin)
        nseg = {c: (counts[c] // SEG) + 1 for c in self.clocks}
        sem_ctx = []
        sems = {}
        for c in self.clocks:
            if counts[c] == 0:
                continue
            sems[c] = []
            for k in range(nseg[c]):
                cm = nc.semaphore(f"s_{c}_{k}")
                sems[c].append(cm.__enter__())
                sem_ctx.append(cm)
        self.nsems = sum(len(v) for v in sems.values())

        def seg_total(c, k):
            tot = counts[c]
            hi = min(tot, (k + 1) * SEG - 1) if False else None
            return None

        def locate(c, n, inc):
            pre = n - inc
            k = pre // SEG
            return k, n - k * SEG

        segfinal = {c: [0] * nseg[c] for c in self.clocks}
        for op in self.all_ops:
            if op.fn != "nop" and op.signals:
                inc = 16 if op.is_dma else 1
                k, v = locate(op.clock, op.count, inc)
                segfinal[op.clock][k] = max(segfinal[op.clock][k], v)

        engs = {"pe": "tensor", "act": "scalar", "dve": "vector", "pool": "gpsimd", "sp": "sync"}
        prog = self

        def run_stream(s, e):
            seen = {c: (0, 0) for c in prog.clocks}
            for op in prog.ops[s]:
                for d in op.deps:
                    c = d.clock
                    inc = 16 if d.is_dma else 1
                    k, v = locate(c, d.count, inc)
                    sk, sv = seen[c]
                    if (k, v) <= (sk, sv):
                        continue
                    if d.is_dma and k > sk:
                        for j in range(sk, k):
                            e.wait_ge(sems[c][j], segfinal[c][j])
                    e.wait_ge(sems[c][k], v)
                    seen[c] = (k, v)
                if op.fn == "nop":
                    continue
                ins = op.fn(e)
                if op.signals:
                    inc = 16 if op.is_dma else 1
                    k, v = locate(op.clock, op.count, inc)
                    ins.then_inc(sems[op.clock][k], inc)

        with nc.Block() as block:
            for s in STREAMS:
                if not prog.ops[s]:
                    continue
                getattr(block, engs[s])(lambda e, s=s: run_stream(s, e))
        for cm in reversed(sem_ctx):
            cm.__exit__(None, None, None)


class TL:
    def __init__(self, t, b):
        self.t = t
        self.b = b

    def __getitem__(self, k):
        return self.t[k]


class KB:
    def __init__(self, nc):
        self.nc = nc
        self.P = Prog(nc)
        self.ctx = []
        self.pctx = []
        self.n = 0
        self.pstiles = []
        self.pi = 0

    def sb(self, shape, dt, name=None):
        self.n += 1
        nm = f"{name or 't'}_{self.n}"
        cm = self.nc.sbuf_tensor(nm, shape, dt)
        t = cm.__enter__()
        self.ctx.append(cm)
        return TL(t, self.P.buf(nm))

    def sb_persist(self, shape, dt, name=None):
        self.n += 1
        nm = f"{name or 'p'}_{self.n}"
        cm = self.nc.sbuf_tensor(nm, shape, dt)
        t = cm.__enter__()
        self.pctx.append(cm)
        return TL(t, self.P.buf(nm))

    def free_persist(self):
        for cm in reversed(self.pctx):
            cm.__exit__(None, None, None)
        self.pctx = []

    def psum_pool(self, n=8):
        self.pstiles = []
        for i in range(n):
            self.n += 1
            nm = f"ps_{self.n}"
            cm = self.nc.psum_tensor(nm, [128, 512], F32)
            t = cm.__enter__()
            self.ctx.append(cm)
            tl = TL(t, self.P.buf(nm))
            tl.b.excl = True
            self.pstiles.append(tl)
        self.pi = 0

    def pn(self):
        p = self.pstiles[self.pi % len(self.pstiles)]
        self.pi += 1
        return p

    def end_phase(self):
        self.P.barrier()
        for cm in reversed(self.ctx):
            cm.__exit__(None, None, None)
        self.ctx = []

    def op(self, eng, fn, r=(), w=()):
        return self.P.op(eng, fn, [x.b if isinstance(x, TL) else x for x in r],
                         [x.b if isinstance(x, TL) else x for x in w])

    def dma(self, q, out, in_, r=(), w=(), **kw):
        return self.P.dma(q, out, in_, [x.b if isinstance(x, TL) else x for x in r],
                          [x.b if isinstance(x, TL) else x for x in w], **kw)

    def mm(self, out_tl, out_ap, pairs, r):
        n = len(pairs)

        def fn(e):
            ins = None
            for i, (l, rh) in enumerate(pairs):
                ins = e.matmul(out_ap, lhsT=l, rhs=rh, start=(i == 0), stop=(i == n - 1))
            return ins
        return self.op("pe", fn, r=r, w=[out_tl])

    def tr(self, out_tl, out_ap, in_ap, ident_ap, r):
        return self.op("pe", lambda e: e.transpose(out_ap, in_ap, ident_ap), r=r, w=[out_tl])


def phase0(kb, A):
    kb.psum_pool(4)
    cT = kb.sb([128, 8], F32)
    condT = kb.sb([128, 8], F32)
    kb.dma("sp", cT[:], A["cT"], w=[cT])
    kb.op("act", lambda e: e.activation(out=condT[:], in_=cT[:], func=AF.Silu), r=[cT], w=[condT])
    wt = [kb.sb([128, 4096], F32, "adaw") for _ in range(2)]
    for l in range(2):
        adab = kb.sb([1, 6144], F32)
        n1 = kb.sb([1, 1024], F32)
        n2 = kb.sb([1, 1024], F32)
        modsb = kb.sb([1, 6144], F32)
        rows = kb.sb([1, 6144], F32)
        kb.dma("sp", adab[:], A["adab"][l], w=[adab])
        kb.dma("sp", n1[:], A["n1w"][l], w=[n1])
        kb.dma("sp", n2[:], A["n2w"][l], w=[n2])
        for g in range(12):
            w = wt[(l * 12 + g) % 2]
            kb.dma("sp", w[:], A["adaw"][l, g], w=[w])
            ps = kb.pn()
            kb.mm(ps, ps[0:1, 0:512], [(condT[:, k:k + 1], w[:, k * 512:(k + 1) * 512]) for k in range(8)], r=[condT, w])
            kb.op("dve", lambda e, g=g, ps=ps, modsb=modsb, adab=adab: e.tensor_tensor(out=modsb[0:1, g * 512:(g + 1) * 512], in0=ps[0:1, 0:512],
                                                               in1=adab[0:1, g * 512:(g + 1) * 512], op=ALU.add),
                  r=[ps, adab], w=[modsb])
        S = lambda i: slice(i * 1024, (i + 1) * 1024)
        kb.op("dve", lambda e, rows=rows, modsb=modsb, n1=n1, n2=n2: e.scalar_tensor_tensor(out=rows[0:1, S(0)], in0=modsb[0:1, S(1)], scalar=1.0, in1=n1[0:1, :], op0=ALU.add, op1=ALU.mult), r=[modsb, n1], w=[rows])
        kb.op("dve", lambda e, rows=rows, modsb=modsb, n1=n1, n2=n2: e.tensor_copy(out=rows[0:1, S(1)], in_=modsb[0:1, S(0)]), r=[modsb], w=[rows])
        kb.op("dve", lambda e, rows=rows, modsb=modsb, n1=n1, n2=n2: e.tensor_copy(out=rows[0:1, S(2)], in_=modsb[0:1, S(2)]), r=[modsb], w=[rows])
        kb.op("dve", lambda e, rows=rows, modsb=modsb, n1=n1, n2=n2: e.scalar_tensor_tensor(out=rows[0:1, S(3)], in0=modsb[0:1, S(4)], scalar=1.0, in1=n2[0:1, :], op0=ALU.add, op1=ALU.mult), r=[modsb, n2], w=[rows])
        kb.op("dve", lambda e, rows=rows, modsb=modsb, n1=n1, n2=n2: e.tensor_copy(out=rows[0:1, S(4)], in_=modsb[0:1, S(3)]), r=[modsb], w=[rows])
        kb.op("dve", lambda e, rows=rows, modsb=modsb, n1=n1, n2=n2: e.tensor_copy(out=rows[0:1, S(5)], in_=modsb[0:1, S(5)]), r=[modsb], w=[rows])
        kb.dma("sp", A["modrow"][l], rows[0:1, :], r=[rows])
    kb.end_phase()


def rms_rstd(kb, xt, junk, ss, rstd, eps, n):
    kb.op("act", lambda e: e.activation(out=junk[:], in_=xt[:], func=AF.Square, accum_out=ss[:]), r=[xt], w=[junk, ss])
    kb.op("act", lambda e: e.activation(out=rstd[:], in_=ss[:], func=AF.Sqrt, scale=1.0 / n, bias=kb.epsc[(eps)][:]), r=[ss], w=[rstd])
    kb.op("dve", lambda e: e.reciprocal(out=rstd[:], in_=rstd[:]), r=[rstd], w=[rstd])


def make_consts(kb, A, vals):
    kb.epsc = {}
    for v in vals:
        t = kb.sb([128, 1], F32, "eps")
        kb.op("pool", lambda e, t=t, v=v: e.memset(t[:], v), w=[t])
        kb.epsc[v] = t


class Pipe:
    def __init__(self, pools):
        self.pools = {k: list(v) for k, v in pools.items()}
        self.active = []

    def run(self, items):
        items = list(items)
        pos = 0
        while pos < len(items) or self.active:
            if pos < len(items):
                pool, fac = items[pos]
                if self.pools[pool]:
                    slot = self.pools[pool].pop(0)
                    self.active.append((fac(slot), pool, slot))
                    pos += 1
            nxt = []
            for g, pool, slot in self.active:
                try:
                    next(g)
                    nxt.append((g, pool, slot))
                except StopIteration:
                    self.pools[pool].append(slot)
            self.active = nxt


def phase1(kb, A, l, xsrc):
    kb.psum_pool(8)
    make_consts(kb, A, [1e-6])
    ident16 = kb.sb([128, 128], BF16)
    ident32 = kb.sb([128, 128], F32)
    ones32 = kb.sb([128, 128], F32)
    kb.dma("pool", ident16[:], A["ident"], w=[ident16])
    kb.dma("sp", ident32[:], A["ident"], w=[ident32])
    kb.op("pool", lambda e: e.memset(ones32[:], 1.0), w=[ones32])
    win = [kb.sb([128, DIN], BF16, "win") for _ in range(8)]
    for k in range(8):
        kb.dma("pool", win[k][:], A["win"][l, :, k * DIN:(k + 1) * DIN], w=[win[k]])
    bcW = kb.sb([128, 1024], F32)
    bcS = kb.sb([128, 1024], F32)
    kb.dma("sp", bcW[:], A["modrow"][l, 0:1, 0:1024].partition_broadcast(128), w=[bcW])
    kb.dma("sp", bcS[:], A["modrow"][l, 0:1, 1024:2048].partition_broadcast(128), w=[bcS])
    convw = kb.sb([128, 48], F32)
    kb.dma("sp", convw[:], A["convw"][l], w=[convw])
    mu = kb.sb([128, 7], F32)
    kb.dma("sp", mu[:], A["mu"][l], w=[mu])
    omu = kb.sb([128, 7], F32)
    kb.op("dve", lambda e: e.tensor_scalar(out=omu[:], in0=mu[:], scalar1=-1.0, scalar2=1.0, op0=ALU.mult, op1=ALU.add), r=[mu], w=[omu])
    halo = [kb.sb([128, 3], F32, "halo") for _ in range(12)]
    halor = [kb.sb([128, 1], F32, "halor") for _ in range(7)]
    for t in halo + halor:
        kb.op("pool", lambda e, t=t: e.memset(t[:], 0.0), w=[t])
    junk = kb.sb([128, 1024], BF16)
    hT = [kb.sb([128, 8 * 512], BF16, "hT") for _ in range(2)]
    NP, NF, NT = 2, 4, 2
    pslot = [dict(bank=4 + s, xt=kb.sb([128, 1024], F32, "xt"), tmp=kb.sb([128, 1024], F32, "tmp"), xn=kb.sb([128, 1024], BF16, "xn"),
                  ss=kb.sb([128, 1], F32), rstd=kb.sb([128, 1], F32)) for s in range(NP)]
    fslot = [dict(bank=s, fb=kb.sb([128, 515], F32, "fb"), acc=kb.sb([128, 512], F32, "acc"), sil=kb.sb([128, 512], F32, "sil"),
                  sq=kb.sb([128, 512], F32, "sq"), rs=kb.sb([128, 512], F32, "rs"), kn=kb.sb([128, 512], F32, "kn"),
                  o16=kb.sb([128, 512], BF16, "o16"), tok=kb.sb([128, 512], F32, "tok")) for s in range(NF)]
    tslot = [dict(bank=6 + s, tm=kb.sb([128, 1544], F32, "tmb")) for s in range(NT)]

    def tile_prep(S, tb, j):
        xt, tmp, xn, ss, rstd = S["xt"], S["tmp"], S["xn"], S["ss"], S["rstd"]
        ps = kb.pstiles[S["bank"]]
        h = hT[tb % 2]
        t0 = tb * 512 + j * 128
        kb.dma("sp", xt[:], xsrc[t0:t0 + 128, :], w=[xt])
        yield
        kb.op("act", lambda e: e.activation(out=junk[:], in_=xt[:], func=AF.Square, accum_out=ss[:]), r=[xt], w=[junk, ss])
        yield
        kb.op("act", lambda e: e.activation(out=rstd[:], in_=ss[:], func=AF.Sqrt, scale=1.0 / 1024, bias=kb.epsc[1e-6][:]), r=[ss], w=[rstd])
        yield
        kb.op("dve", lambda e: e.reciprocal(out=rstd[:], in_=rstd[:]), r=[rstd], w=[rstd])
        yield
        kb.op("dve", lambda e: e.scalar_tensor_tensor(out=tmp[:], in0=xt[:], scalar=rstd[:, 0:1], in1=bcW[:], op0=ALU.mult, op1=ALU.mult),
              r=[xt, rstd, bcW], w=[tmp])
        yield
        kb.op("pool", lambda e: e.tensor_tensor(out=xn[:], in0=tmp[:], in1=bcS[:], op=ALU.add), r=[tmp, bcS], w=[xn])
        yield
        psb = ps[:].bitcast(BF16)

        def trf(e):
            ins = None
            for k in range(8):
                ins = e.transpose(psb[:, k * 128:(k + 1) * 128], xn[:, k * 128:(k + 1) * 128], ident16[:])
            return ins
        kb.op("pe", trf, r=[xn, ident16], w=[ps])
        yield
        hv = h[:].rearrange("p (k t) -> p k t", k=8)[:, :, j * 128:(j + 1) * 128]
        pv = psb.rearrange("p (k t) -> p k t", k=8)
        kb.op("act", lambda e: e.copy(out=hv, in_=pv), r=[ps], w=[h])
        yield

    def fm_group(S, tb, c):
        h = hT[tb % 2]
        ps = kb.pstiles[S["bank"]]
        fb, a_, s_, q_, r_, kn, o_, tk = S["fb"], S["acc"], S["sil"], S["sq"], S["rs"], S["kn"], S["o16"], S["tok"]
        col0 = c * 128 if c < 12 else 3080 + (c - 12) * 128
        bsl = slice(tb * 512, (tb + 1) * 512)
        kb.mm(ps, ps[:, 0:512], [(win[k][:, col0:col0 + 128], h[:, k * 512:(k + 1) * 512]) for k in range(8)], r=[h] + win)
        yield
        kb.op("act", lambda e: e.copy(out=fb[:, 3:515], in_=ps[:, 0:512]), r=[ps], w=[fb])
        if c < 12:
            hl = halo[c]
            kb.op("pool", lambda e: e.tensor_copy(out=fb[:, 0:3], in_=hl[:]), r=[hl], w=[fb])
            yield
            kb.op("pool", lambda e: e.tensor_copy(out=hl[:], in_=fb[:, 512:515]), r=[fb], w=[hl])
            kb.op("act", lambda e: e.activation(out=a_[:], in_=fb[:, 0:512], func=AF.Copy, scale=convw[:, c * 4:c * 4 + 1]), r=[fb, convw], w=[a_])
            yield
            for k in range(1, 4):
                kb.op("dve", lambda e, k=k: e.scalar_tensor_tensor(out=a_[:], in0=fb[:, k:k + 512], scalar=convw[:, c * 4 + k:c * 4 + k + 1], in1=a_[:], op0=ALU.mult, op1=ALU.add),
                      r=[fb, convw, a_], w=[a_])
                yield
            kb.op("act", lambda e: e.activation(out=s_[:], in_=a_[:], func=AF.Silu), r=[a_], w=[s_])
            yield
            if c < 8:
                kb.op("pool", lambda e: e.tensor_tensor(out=q_[:], in0=s_[:], in1=s_[:], op=ALU.mult), r=[s_], w=[q_])
                yield
                kb.mm(ps, ps[:, 0:512], [(ones32[:], q_[:])], r=[ones32, q_])
                yield
                kb.op("act", lambda e: e.activation(out=r_[:], in_=ps[:, 0:512], func=AF.Sqrt, bias=kb.epsc[1e-6][:], scale=1.0), r=[ps], w=[r_])
                yield
                kb.op("dve", lambda e: e.reciprocal(out=r_[:], in_=r_[:]), r=[r_], w=[r_])
                yield
                hh = c % 4
                if c < 4:
                    kb.op("dve", lambda e: e.scalar_tensor_tensor(out=o_[:], in0=s_[:], scalar=128.0 ** -0.5, in1=r_[:], op0=ALU.mult, op1=ALU.mult), r=[s_, r_], w=[o_])
                    yield
                    kb.dma("sp", A["qT"][l, hh, :, bsl], o_[:], r=[o_])
                    return
                kb.op("dve", lambda e: e.tensor_tensor(out=kn[:], in0=s_[:], in1=r_[:], op=ALU.mult), r=[s_, r_], w=[kn])
                yield
                kb.op("pool", lambda e: e.tensor_copy(out=o_[:], in_=kn[:]), r=[kn], w=[o_])
                src = kn
            else:
                src = s_
                hh = c - 8

            def trf2(e):
                ins = None
                for j in range(4):
                    ins = e.transpose(ps[:, j * 128:(j + 1) * 128], src[:, j * 128:(j + 1) * 128], ident32[:])
                return ins
            kb.op("pe", trf2, r=[src, ident32], w=[ps])
            yield
            if c < 8:
                kb.dma("sp", A["kT"][l, hh, :, bsl], o_[:], r=[o_])
            kb.op("act", lambda e: e.copy(out=tk[:], in_=ps[:, 0:512]), r=[ps], w=[tk])
            yield
            dst = A["ktok"] if c < 8 else A["vtok"]
            dv = dst[l, bsl, hh * 128:(hh + 1) * 128].rearrange("(j p) d -> p j d", p=128)
            kb.dma("sp", dv, tk[:].rearrange("p (j d) -> p j d", j=4), r=[tk])
        else:
            jr = c - 12
            hl = halor[jr]
            kb.op("pool", lambda e: e.tensor_copy(out=fb[:, 2:3], in_=hl[:]), r=[hl], w=[fb])
            yield
            kb.op("pool", lambda e: e.tensor_copy(out=hl[:], in_=fb[:, 514:515]), r=[fb], w=[hl])
            kb.op("act", lambda e: e.activation(out=a_[:], in_=fb[:, 3:515], func=AF.Copy, scale=omu[:, jr:jr + 1]), r=[fb, omu], w=[a_])
            yield
            kb.op("dve", lambda e: e.scalar_tensor_tensor(out=s_[:], in0=fb[:, 2:514], scalar=mu[:, jr:jr + 1], in1=a_[:], op0=ALU.mult, op1=ALU.add),
                  r=[fb, a_, mu], w=[s_])
            yield
            kb.dma("sp", A["rwT"][l, jr * 128:(jr + 1) * 128, bsl], s_[:], r=[s_])

    def tm_group(S, tb, j):
        h = hT[tb % 2]
        ps = kb.pstiles[S["bank"]]
        tm = S["tm"]
        for gi, (c0, c1) in enumerate(((1536, 2048), (2048, 2056), (2056, 2568), (2568, 3080))):
            nw = c1 - c0
            kb.mm(ps, ps[:, 0:nw], [(h[:, k * 512 + j * 128:k * 512 + (j + 1) * 128], win[k][:, c0:c1]) for k in range(8)], r=[h] + win)
            yield
            if True:
                kb.op("act", lambda e, c0=c0, c1=c1, nw=nw: e.copy(out=tm[:, c0 - 1536:c1 - 1536], in_=ps[:, 0:nw]), r=[ps], w=[tm])
            else:
                kb.op("dve", lambda e, c0=c0, c1=c1, nw=nw: e.tensor_copy(out=tm[:, c0 - 1536:c1 - 1536], in_=ps[:, 0:nw]), r=[ps], w=[tm])
            yield
        t0 = tb * 512 + j * 128
        kb.dma("sp", A["tm"][l, t0:t0 + 128, :], tm[:], r=[tm])

    pipe = Pipe({"p": pslot, "f": fslot, "t": tslot})
    pipe.run([("p", (lambda S, j=j: tile_prep(S, 0, j))) for j in range(4)])
    items = []
    for tb in range(8):
        blk = []
        for c in range(19):
            blk.append(("f", (lambda S, tb=tb, c=c: fm_group(S, tb, c))))
            if tb < 7 and c in (2, 6, 10, 14):
                blk.append(("p", (lambda S, tb=tb, j=(c - 2) // 4: tile_prep(S, tb + 1, j))))
        for j in range(4):
            blk.append(("t", (lambda S, tb=tb, j=j: tm_group(S, tb, j))))
        items += blk
    pipe.run(items)
    kb.end_phase()


def interleave(gens, stagger=0):
    gens = list(gens)
    if stagger:
        for idx, g in enumerate(gens):
            for _ in range(idx * stagger):
                next(g)
    while gens:
        nxt = []
        for g in gens:
            try:
                next(g)
                nxt.append(g)
            except StopIteration:
                pass
        gens = nxt


def zip_gens(*gens):
    gens = list(gens)
    while gens:
        nxt = []
        for g in gens:
            try:
                next(g)
                nxt.append(g)
            except StopIteration:
                pass
        gens = nxt
        if gens:
            yield


class H2:
    def __init__(self, kb, n32, n16, nw, banks, nev=0, evq="act"):
        self.kb = kb
        self.rev = [kb.sb([128, 512], F32R, "rev") for _ in range(nev)]
        self.r32r = [kb.sb([128, 128], F32R, "r32r") for _ in range(2 if nev else 0)]
        self.i32r = 0
        self.iev = 0
        self.evq = evq
        self.r32 = [kb.sb([128, 128], F32, "r32") for _ in range(n32)]
        self.r16 = [kb.sb([128, 128], BF16, "r16") for _ in range(n16)]
        self.rw = [kb.sb([128, 256], F32, "rw") for _ in range(nw)]
        self.banks = banks
        self.i32 = self.i16 = self.iw = self.ip = 0

    def t32(self):
        self.i32 += 1
        return self.r32[self.i32 % len(self.r32)]

    def t16(self):
        self.i16 += 1
        return self.r16[self.i16 % len(self.r16)]

    def tr(self):
        self.i32r += 1
        return self.r32r[self.i32r % len(self.r32r)]

    def tev(self):
        self.iev += 1
        return self.rev[self.iev % len(self.rev)]

    def tw(self):
        self.iw += 1
        return self.rw[self.iw % len(self.rw)]

    def pn(self):
        self.ip += 1
        return self.kb.pstiles[self.banks[self.ip % len(self.banks)]]

    def TT(self, eng, out_tl, out, in0, in1, op, r):
        self.kb.op(eng, lambda e: e.tensor_tensor(out=out, in0=in0, in1=in1, op=op), r=r, w=[out_tl])

    def TS(self, eng, out_tl, out, in0, s1, op0, r, s2=None, op1=None):
        if eng == "act":
            assert op1 is None and op0 == ALU.mult
            self.kb.op("act", lambda e: e.activation(out=out, in_=in0, func=AF.Copy, scale=s1), r=r, w=[out_tl])
            return
        if op1 is None:
            self.kb.op(eng, lambda e: e.tensor_scalar(out=out, in0=in0, scalar1=s1, scalar2=None, op0=op0), r=r, w=[out_tl])
        else:
            self.kb.op(eng, lambda e: e.tensor_scalar(out=out, in0=in0, scalar1=s1, scalar2=s2, op0=op0, op1=op1), r=r, w=[out_tl])

    def STT(self, eng, out_tl, out, in0, sc, in1, op0, op1, r):
        self.kb.op("dve", lambda e: e.scalar_tensor_tensor(out=out, in0=in0, scalar=sc, in1=in1, op0=op0, op1=op1), r=r, w=[out_tl])

    def ACT(self, out_tl, out, in_, func, r, bias=0.0, scale=1.0):
        self.kb.op("act", lambda e: e.activation(out=out, in_=in_, func=func, bias=bias, scale=scale), r=r, w=[out_tl])

    def CP(self, eng, out_tl, out, in_, r):
        if eng == "act":
            self.kb.op("act", lambda e: e.copy(out=out, in_=in_), r=r, w=[out_tl])
        else:
            self.kb.op(eng, lambda e: e.tensor_copy(out=out, in_=in_), r=r, w=[out_tl])

    def MM(self, ps, out, pairs, r):
        self.kb.mm(ps, out, pairs, r=r)


def inverse(H, ev, LoffTn, I32):
    kb = H.kb
    bank = kb.pstiles[H.banks[0]]
    bv = bank[:, 0:512].rearrange("p (h b c) -> p h b c", h=2, b=2)
    v4 = lambda t: t[:].rearrange("p (h b c) -> p h b c", h=2, b=2)
    Ib = I32[:].rearrange("p (o c) -> p o c", o=1).to_broadcast([128, 2, 128])
    H.CP("pool", ev, v4(ev)[:, :, 1, :], Ib, [I32])
    for j in range(6):
        def fn(e, ev=ev):
            e.matmul(bank[:, 0:256], lhsT=ev[:, 256:384], rhs=ev[:, 0:256], start=True, stop=True)
            return e.matmul(bank[:, 256:512], lhsT=ev[:, 0:128], rhs=ev[:, 256:512], start=True, stop=True)
        kb.op("pe", fn, r=[ev], w=[bank])
        yield
        evn = H.tev()
        if j < 5:
            H.CP("act", evn, v4(evn)[:, :, 0, :], bv[:, :, 0, :], [bank])
        H.TT("dve", evn, v4(evn)[:, :, 1, :], bv[:, :, 1, :], v4(ev)[:, :, 1, :].bitcast(F32), ALU.add, [bank, ev])
        ev = evn
        yield
    H.MM(bank, bank[:, 0:256], [(LoffTn[:], ev[:, 128:384])], [LoffTn, ev])
    yield
    ImY = H.tr()
    H.TT("dve", ImY, ImY[:], bank[:, 0:128], I32[:], ALU.add, [bank, I32])
    yield
    H.MM(bank, bank[:, 256:512], [(ImY[:], ev[:, 256:512])], [ImY, ev])
    yield
    TTt = H.t32()
    H.CP("act", TTt, TTt[:], bank[:, 384:512], [bank])
    yield
    return TTt


def phase2(kb, A, l):
    kb.psum_pool(8)
    make_consts(kb, A, [1e-6, 64e-5, 1.0])
    Hg = [H2(kb, 6, 0, 1, [h], nev=2, evq=("act" if h % 2 == 0 else "dve")) for h in range(4)]
    Hh = [H2(kb, 4, 4, 0, [4 + h], nev=2, evq=("dve" if h % 2 == 0 else "act")) for h in range(4)]
    Hp = [H2(kb, 22, 8, 0, [2 * j, 2 * j + 1]) for j in range(2)]
    Hs = H2(kb, 1, 6, 0, [4])
    Ht = H2(kb, 3, 3, 2, [5])
    Hr = H2(kb, 3, 6, 4, [6, 7])
    Hx = H2(kb, 1, 1, 1, [0, 1, 2, 3, 4, 5, 6, 7])
    LoffD = [kb.sb([128, 128], F32R, "LoffD") for _ in range(8)]
    H = Hx
    C = {}
    C["ident32"] = kb.sb([128, 128], F32)
    C["ident16"] = kb.sb([128, 128], BF16)
    kb.dma("sp", C["ident32"][:], A["ident"], w=[C["ident32"]])
    kb.dma("pool", C["ident16"][:], A["ident"], w=[C["ident16"]])
    cm = []
    for i in range(9):
        t = kb.sb([128, 128], F32, "cm")
        kb.dma("sp", t[:], A["cmask"][i], w=[t])
        cm.append(t)
    triU, strictL, negmT, bd, offm, offT, strictT, ones32, negones = cm
    retc = kb.sb([128, 1026], F32)
    kb.dma("sp", retc[:], A["retc"], w=[retc])
    ba = kb.sb([128, NCH * 8], F32)
    for n_ in range(NCH):
        kb.dma("sp", ba[:, n_ * 8:(n_ + 1) * 8], A["tm"][l, n_ * 128:(n_ + 1) * 128, 512:520], w=[ba])
    alog = kb.sb([128, 4], F32)
    dtb = kb.sb([128, 4], F32)
    kb.dma("sp", alog[:], A["alog"][l].partition_broadcast(128), w=[alog])
    kb.dma("sp", dtb[:], A["dtb"][l].partition_broadcast(128), w=[dtb])
    negA = kb.sb([128, 4], F32)
    H.ACT(negA, negA[:], alog[:], AF.Exp, [alog])
    H.TS("dve", negA, negA[:], negA[:], -1.0, ALU.mult, [negA])
    bav = ba[:].rearrange("p (n c) -> p n c", c=8)
    v3 = lambda t: t[:].rearrange("p (n h) -> p n h", h=4)
    beta = kb.sb([128, 128], F32)
    H.ACT(beta, v3(beta), bav[:, :, 0:4], AF.Sigmoid, [ba])
    xg = kb.sb([128, 128], F32)
    dtb3 = dtb[:].rearrange("p (o h) -> p o h", o=1).to_broadcast([128, NCH, 4])
    negA3 = negA[:].rearrange("p (o h) -> p o h", o=1).to_broadcast([128, NCH, 4])
    H.TT("dve", xg, v3(xg), bav[:, :, 4:8], dtb3, ALU.add, [ba, dtb])
    axg = kb.sb([128, 128], F32)
    H.TS("dve", axg, axg[:], xg[:], -1.0, ALU.mult, [xg])
    H.TT("dve", axg, axg[:], axg[:], xg[:], ALU.max, [axg, xg])
    H.ACT(axg, axg[:], axg[:], AF.Exp, [axg], scale=-1.0)
    H.ACT(axg, axg[:], axg[:], AF.Ln, [axg], bias=kb.epsc[1.0][:])
    H.TS("dve", xg, xg[:], xg[:], 0.0, ALU.max, [xg])
    H.TT("dve", xg, xg[:], xg[:], axg[:], ALU.add, [xg, axg])
    g_all = kb.sb([128, 128], F32)
    H.TT("dve", g_all, v3(g_all), v3(xg), negA3, ALU.mult, [xg, negA])
    gc_all = kb.sb([128, 128], F32)
    egc = kb.sb([128, 128], F32)
    ekd = kb.sb([128, 128], F32)
    dch = kb.sb([128, 128], F32)
    negbeta = kb.sb([128, 128], F32)
    bg = kb.sb([128, 128], F32)
    ps = kb.pn()
    H.MM(ps, ps[:, 0:128], [(triU[:], g_all[:])], [triU, g_all])
    H.CP("dve", gc_all, gc_all[:], ps[:, 0:128], [ps])
    H.ACT(egc, egc[:], ps[:, 0:128], AF.Exp, [ps])
    ps = kb.pn()
    H.MM(ps, ps[:, 0:128], [(strictL[:], g_all[:])], [strictL, g_all])
    H.ACT(ekd, ekd[:], ps[:, 0:128], AF.Exp, [ps])
    ps = kb.pn()
    H.MM(ps, ps[:, 0:128], [(ones32[:], g_all[:])], [ones32, g_all])
    H.ACT(dch, dch[:], ps[:, 0:128], AF.Exp, [ps])
    H.TS("dve", negbeta, negbeta[:], beta[:], -1.0, ALU.mult, [beta])
    H.TT("dve", bg, bg[:], beta[:], egc[:], ALU.mult, [beta, egc])
    gnw = kb.sb([128, 128], F32)
    kb.dma("sp", gnw[:], A["gnw"][l].partition_broadcast(128), w=[gnw])
    lora = kb.sb([128, 256], F32)
    kb.dma("sp", lora[:], A["lora"][l], w=[lora])
    rwc = kb.sb([128, 12], F32)
    kb.dma("sp", rwc[:], A["rwcol"][l], w=[rwc])
    omka = kb.sb([128, 2], F32)
    H.TS("dve", omka, omka[:], rwc[:, 6:8], -1.0, ALU.mult, [rwc], s2=1.0, op1=ALU.add)
    lnw = kb.sb([128, 256], F32)
    lnb = kb.sb([128, 256], F32)
    kb.dma("sp", lnw[:], A["lnw"][l].partition_broadcast(128), w=[lnw])
    kb.dma("sp", lnb[:], A["lnb"][l].partition_broadcast(128), w=[lnb])
    sel = kb.sb([128, 2], F32)
    kb.op("pool", lambda e: e.memset(sel[:], 0.0), w=[sel])
    kb.op("pool", lambda e: e.memset(sel[0:64, 0:1], 1.0), w=[sel])
    kb.op("pool", lambda e: e.memset(sel[64:128, 1:2], 1.0), w=[sel])
    v12 = None
    if l == 1:
        v1 = kb.sb([128, 64], F32)
        v2 = kb.sb([32, 256], F32)
        kb.dma("sp", v1[:], A["v1"], w=[v1])
        kb.dma("sp", v2[:], A["v2"], w=[v2])
        v12 = (v1, v2)
    S32 = [kb.sb([128, 128], F32, "S32") for _ in range(4)]
    S16 = [kb.sb([128, 128], BF16, "S16") for _ in range(4)]
    R32 = [kb.sb([128, 128], F32, "R32") for _ in range(2)]
    R16 = [kb.sb([128, 128], BF16, "R16") for _ in range(2)]
    W32 = [kb.sb([128, 128], F32, "W32") for _ in range(2)]
    W16 = [kb.sb([128, 128], BF16, "W16") for _ in range(2)]
    for t in S32 + S16 + R32 + R16 + W32 + W16:
        kb.op("pool", lambda e, t=t: e.memset(t[:], 0.0), w=[t])
    NB = 2
    _ded = lambda dt, nm: [kb.sb([128, 128], dt, nm) for _ in range(4)]
    ded = lambda dt, nm: [_ded(dt, nm)] * NB
    attnT = ded(BF16, "attnT")
    qdecT = ded(BF16, "qdecT")
    wT16 = ded(BF16, "wT16")
    kdec16 = ded(BF16, "kdec")
    u32 = ded(F32, "u32")
    par = lambda shape, dt, nm: [kb.sb(shape, dt, nm) for _ in range(NB)]
    qTc = [kb.sb([128, 512], BF16, "qTc")] * NB
    kTc = [kb.sb([128, 512], BF16, "kTc")] * NB
    ktok = [kb.sb([128, 512], F32, "ktok")] * NB
    vtok = [kb.sb([128, 512], F32, "vtok")] * NB
    tmc = [kb.sb([128, 1544], F32, "tmc")] * NB
    rope = [kb.sb([128, 64], F32, "rope")] * NB
    rwt = [kb.sb([128, 7 * 128], F32, "rwt")] * NB
    vft = [kb.sb([128, 256], F32, "vft")] * NB
    osb = [kb.sb([128, 512], F32, "osb")] * NB
    obs = [kb.sb([128, 256], F32, "obs")] * NB
    ysb = [kb.sb([128, 256], F32, "ysb")] * NB
    mixc = par([128, 1024], BF16, "mixc")
    junk = kb.sb([128, 128], F32)
    kb.rot16 = [kb.sb([128, 512], BF16, "rot16")] * NB
    kb.rqd = [kb.sb([128, 256], BF16, "rqd")] * NB
    kb.rq = [kb.sb([128, 256], BF16, "rq")] * NB
    kb.rk = [kb.sb([128, 256], BF16, "rk")] * NB
    kb.rkd = [kb.sb([128, 256], BF16, "rkd")] * NB
    kb.rv = [kb.sb([128, 256], BF16, "rv")] * NB
    kb.rst = par([128, 12], F32, "rst")
    kb.gst = par([128, 8], F32, "gst")
    kb.gsz = [kb.sb([128, 512], F32, "gsz")] * NB
    kb.wst = par([128, 12], F32, "wst")
    kb.rwg = par([128, 256], F32, "rwg")
    kb.rwv32 = par([128, 256], F32, "rwv32")
    kb.rwgc = par([128, 2], F32, "rwgc")
    kb.rwbc = par([128, 2], F32, "rwbc")
    kb.rwbs = par([128, 4], F32, "rwbs")
    kb.rwtw = [kb.sb([128, 128], F32, "rwtw")] * NB
    kb.rwvl = [kb.sb([128, 128], F32, "rwvl")] * NB
    kb.rwtok = [[kb.sb([128, 512], BF16, "rwtok") for _ in range(2)] for _ in range(NB)]
    kb.rwpa = [[kb.sb([128, 128], BF16, "rwpa") for _ in range(2)] for _ in range(NB)]
    kb.rwqv = [[kb.sb([128, 128], F32, "rwqv") for _ in range(2)] for _ in range(NB)]
    kb.rwar = [[kb.sb([128, 256], BF16, "rwar") for _ in range(2)] for _ in range(NB)]
    _bk = [kb.sb([128, 256], BF16, "rwbk") for _ in range(2)]
    kb.rwbk = [_bk] * NB
    kb.rwA = [[kb.sb([128, 256], BF16, "rwA") for _ in range(4)] for _ in range(NB)]
    I32 = C["ident32"]
    def load_s1(n):
        ts = slice(n * 128, (n + 1) * 128)
        kb.dma("sp", qTc[0][:].rearrange("p (h t) -> p h t", h=4), A["qT"][l, :, :, ts].rearrange("h p t -> p h t"), w=[qTc[0]])
        kb.dma("sp", kTc[0][:].rearrange("p (h t) -> p h t", h=4), A["kT"][l, :, :, ts].rearrange("h p t -> p h t"), w=[kTc[0]])
        kb.dma("sp", ktok[0][:], A["ktok"][l, ts, :], w=[ktok[0]])
        kb.dma("sp", vtok[0][:], A["vtok"][l, ts, :], w=[vtok[0]])

    def load_s2(n):
        ts = slice(n * 128, (n + 1) * 128)
        kb.dma("sp", tmc[0][:], A["tm"][l, ts, :], w=[tmc[0]])
        kb.dma("sp", rope[0][:], A["rope"][ts, :], w=[rope[0]])
        kb.dma("sp", rwt[0][:].rearrange("p (j t) -> p j t", j=7), A["rwT"][l, :, ts].rearrange("(j p) t -> p j t", p=128), w=[rwt[0]])
        if l == 1:
            kb.dma("sp", vft[0][:].rearrange("p (j t) -> p j t", j=2), A["vfT"][:, ts].rearrange("(j p) t -> p j t", p=128), w=[vft[0]])

    def S1(n):
        i = n % NB
        return [gdn_prep(kb, Hg[h], LoffD[h], I32, n, h, qTc[0], kTc[0], ktok[0], vtok[0], cm, g_all, gc_all, ekd, negbeta, bg, beta,
                         attnT[i][h], qdecT[i][h], wT16[i][h], kdec16[i][h], u32[i][h]) for h in range(4)]

    def S2(n):
        i = n % NB
        return [ret_chunk(kb, Hr, C, i, tmc[0], rope[0], retc, bd, R32, R16, obs[i], mixc[i]),
                gdn_scan(kb, Hs, n, i, attnT[i], qdecT[i], wT16[i], kdec16[i], u32[i], S32, S16, dch, osb[i], tmc[0], gnw, mixc[i], junk)] + \
               [rwkv_pair(kb, Hp[j], C, A, l, n, i, j, rwt[0], vft[0], lora, rwc, omka, sel, cm, v12) for j in range(2)]

    def S3(n):
        i = n % NB
        return [rwkv_head(kb, Hh[2 * j + hh], LoffD[4 + 2 * j + hh], I32, i, j, hh, cm) for j in range(2) for hh in range(2)]

    def S4(n):
        i = n % NB
        return rwkv_tail(kb, Ht, A, i, slice(n * 128, (n + 1) * 128), bd, W32, W16, ysb[i], lnw, lnb, mixc[i])

    def mix2(a, b):
        out = []
        for k in range(max(len(a), len(b))):
            if k < len(a):
                out.append(a[k])
            if k < len(b):
                out.append(b[k])
        return out

    load_s1(0)
    interleave(S1(0), stagger=1)
    load_s1(1)
    load_s2(0)
    rwkv_pre(kb, Hx, l, 0, rwt[0], lora, v12)
    interleave(S2(0))
    for n in range(NCH):
        more = n + 1 < NCH
        if more:
            load_s2(n + 1)
        interleave(mix2(S3(n), S1(n + 1) if more else []), stagger=0)
        if n + 2 < NCH:
            load_s1(n + 2)
        if more:
            rwkv_pre(kb, Hx, l, (n + 1) % NB, rwt[0], lora, v12)
        interleave([S4(n)] + (S2(n + 1) if more else []))
    kb.end_phase()


def ret_chunk(kb, H, C, i, tm, rope, retc, bd, R32, R16, obs, mx):
    I16 = C["ident16"]
    psO = kb.pstiles[H.banks[0]]
    pother = kb.pstiles[H.banks[1]]
    qk = tm[:, 520:1032].rearrange("p (g d) -> p g d", d=64)
    x1, x2 = qk[:, :, 0:32], qk[:, :, 32:64]
    cosb = rope[:, 0:32].rearrange("p (o d) -> p o d", o=1).to_broadcast([128, 8, 32])
    sinb = rope[:, 32:64].rearrange("p (o d) -> p o d", o=1).to_broadcast([128, 8, 32])
    ta, tb_, tc, td = H.tw(), H.tw(), H.tw(), H.tw()
    v8 = lambda t: t[:].rearrange("p (g d) -> p g d", d=32)
    H.TT("dve", ta, v8(ta), x1, cosb, ALU.mult, [tm, rope])
    H.TT("pool", tb_, v8(tb_), x2, sinb, ALU.mult, [tm, rope])
    yield
    H.TT("dve", tc, v8(tc), x2, cosb, ALU.mult, [tm, rope])
    H.TT("pool", td, v8(td), x1, sinb, ALU.mult, [tm, rope])
    yield
    rot = kb.rot16[i]
    rv = rot[:].rearrange("p (g d) -> p g d", d=64)
    H.TT("dve", rot, rv[:, :, 0:32], v8(ta), v8(tb_), ALU.subtract, [ta, tb_])
    H.TT("pool", rot, rv[:, :, 32:64], v8(tc), v8(td), ALU.add, [tc, td])
    yield
    ps = pother
    psb = ps[:].bitcast(BF16)

    def trf(e):
        ins = None
        for j in range(4):
            ins = e.transpose(psb[:, j * 128:(j + 1) * 128], rot[:, j * 128:(j + 1) * 128], I16[:])
        return ins
    kb.op("pe", trf, r=[rot, I16], w=[ps])
    yield
    qdT, qT, kT = kb.rqd[i], kb.rq[i], kb.rk[i]
    H.TT("dve", qdT, qdT[:], psb[:, 0:256], retc[:, 512:768], ALU.mult, [ps, retc])
    yield
    H.CP("act", qT, qT[:], psb[:, 0:256], [ps])
    H.ACT(kT, kT[:], psb[:, 256:512], AF.Copy, [ps], scale=0.125)
    kdec = kb.rkd[i]
    H.TT("pool", kdec, kdec[:], rot[:, 256:512], retc[:, 768:1024], ALU.mult, [rot, retc])
    v16 = kb.rv[i]
    H.CP("pool", v16, v16[:], tm[:, 1032:1288], [tm])
    yield
    hc = lambda h: slice(h * 128, (h + 1) * 128)

    sbank = lambda h: (pother if h % 2 == 0 else psO)
    scol = lambda h: slice((h // 2) * 128, (h // 2 + 1) * 128)

    def fS(e):
        ins = None
        for h in range(4):
            pr, pb = h // 2, (h % 2) * 64
            ins = e.matmul(sbank(h)[:, scol(h)], lhsT=kT[pb:pb + 64, pr * 128:(pr + 1) * 128], rhs=qT[pb:pb + 64, pr * 128:(pr + 1) * 128], start=True, stop=True)
        return ins
    kb.op("pe", fS, r=[kT, qT], w=[pother, psO])
    yield
    scs = []
    for h in range(4):
        sc = H.t16()
        H.TT("dve", sc, sc[:], sbank(h)[:, scol(h)], retc[:, hc(h)], ALU.mult, [sbank(h), retc])
        scs.append(sc)
    yield

    def fO(e):
        ins = None
        for h in range(4):
            pr = h // 2
            e.matmul(psO[:, h * 64:(h + 1) * 64], lhsT=qdT[:, pr * 128:(pr + 1) * 128], rhs=R16[pr][:, h % 2 * 64:(h % 2 + 1) * 64], start=True, stop=False)
            ins = e.matmul(psO[:, h * 64:(h + 1) * 64], lhsT=scs[h][:], rhs=v16[:, h * 64:(h + 1) * 64], start=False, stop=True)
        return ins
    kb.op("pe", fO, r=[qdT, v16] + R16 + scs, w=[psO])
    yield
    H.CP("act", obs, obs[:], psO[:, 0:256], [psO])
    yield

    def r_update():
        def fR(e):
            ins = None
            for pr in range(2):
                ins = e.matmul(pother[:, hc(pr)], lhsT=kdec[:, pr * 128:(pr + 1) * 128], rhs=v16[:, pr * 128:(pr + 1) * 128], start=True, stop=True)
            return ins
        kb.op("pe", fR, r=[kdec, v16], w=[pother])
        yield
        trs = []
        for pr in range(2):
            tr_ = H.t32()
            H.TT("dve", tr_, tr_[:], pother[:, hc(pr)], bd[:], ALU.mult, [pother, bd])
            trs.append(tr_)
        yield
        for pr in range(2):
            H.STT("dve", R32[pr], R32[pr][:], R32[pr][:], retc[:, 1024 + pr:1025 + pr], trs[pr][:], ALU.mult, ALU.add, [R32[pr], trs[pr], retc])
        yield
        for pr in range(2):
            H.CP("act", R16[pr], R16[pr][:], R32[pr][:], [R32[pr]])
        yield

    def ln_gate():
        st = kb.rst[i]
        ov = obs[:].rearrange("p (h d) -> p h d", d=64)
        kb.op("dve", lambda e: e.tensor_reduce(out=st[:, 0:4], in_=ov, axis=AX.X, op=ALU.add), r=[obs], w=[st])
        sq = H.tw()
        H.TT("pool", sq, sq[:], obs[:], obs[:], ALU.mult, [obs])
        yield
        kb.op("dve", lambda e: e.tensor_reduce(out=st[:, 4:8], in_=sq[:].rearrange("p (h d) -> p h d", d=64), axis=AX.X, op=ALU.add), r=[sq], w=[st])
        yield
        H.TS("dve", st, st[:, 0:4], st[:, 0:4], 1.0 / 64, ALU.mult, [st])
        yield
        H.TT("dve", st, st[:, 8:12], st[:, 0:4], st[:, 0:4], ALU.mult, [st])
        yield
        H.STT("dve", st, st[:, 4:8], st[:, 4:8], 1.0 / 64, st[:, 8:12], ALU.mult, ALU.subtract, [st])
        yield
        H.ACT(st, st[:, 4:8], st[:, 4:8], AF.Ln, [st], bias=kb.epsc[1e-6][:])
        yield
        H.ACT(st, st[:, 4:8], st[:, 4:8], AF.Exp, [st], scale=-0.5)
        yield
        H.STT("dve", st, st[:, 8:12], st[:, 0:4], -1.0, st[:, 4:8], ALU.mult, ALU.mult, [st])
        sg = H.tw()
        H.ACT(sg, sg[:], tm[:, 1288:1544], AF.Silu, [tm])
        yield
        yn = H.tw()
        for h in range(4):
            H.TS("dve", yn, yn[:, h * 64:(h + 1) * 64], obs[:, h * 64:(h + 1) * 64], st[:, 4 + h:5 + h], ALU.mult, [obs, st], s2=st[:, 8 + h:9 + h], op1=ALU.add)
        yield
        H.TT("pool", mx, mx[:, 512:768], yn[:], sg[:], ALU.mult, [yn, sg])
        yield
    yield from zip_gens(r_update(), ln_gate())


def gdn_prep(kb, H, LoffTn, I32, n, h, qTc, kTc, ktok, vtok, cm, g_all, gc_all, ekd, negbeta, bg, beta,
             attnT, qdecT, wT16, kdec16, u32):
    triU, strictL, negmT, bd, offm, offT, strictT, ones32, negones = cm
    col = n * 4 + h
    cs = slice(col, col + 1)
    hs = slice(h * 128, (h + 1) * 128)
    Gtri = H.t32()
    H.TS("act", Gtri, Gtri[:], triU[:], g_all[:, cs], ALU.mult, [triU, g_all])
    H.TS("act", kdec16, kdec16[:], ktok[:, hs], ekd[:, cs], ALU.mult, [ktok, ekd])
    rhs = H.tw()
    H.TS("act", rhs, rhs[:, 0:128], vtok[:, hs], beta[:, cs], ALU.mult, [vtok, beta])
    H.TS("act", rhs, rhs[:, 128:256], ktok[:, hs], bg[:, cs], ALU.mult, [ktok, bg])
    yield
    psG = H.pn()
    H.MM(psG, psG[:, 0:128], [(ones32[:], Gtri[:])], [ones32, Gtri])
    psK = H.pn()
    H.MM(psK, psK[:, 128:256], [(kTc[:, hs], kTc[:, hs])], [kTc])
    yield
    tD = H.t32()
    H.STT("dve", tD, tD[:], psG[:, 0:128], gc_all[:, cs], negmT[:], ALU.subtract, ALU.add, [psG, gc_all, negmT])
    yield
    ebc = H.t32()
    H.ACT(ebc, ebc[:], psG[:, 0:128], AF.Exp, [psG])
    DTi = H.t32()
    H.ACT(DTi, DTi[:], tD[:], AF.Exp, [tD])
    yield
    H.TT("dve", qdecT, qdecT[:], qTc[:, hs], ebc[:], ALU.mult, [qTc, ebc])
    psT = H.pn()
    kb.tr(psT, psT[:, 256:384], DTi[:], I32[:], r=[DTi, I32])
    yield
    Dst = H.t32()
    H.TT("dve", Dst, Dst[:], psT[:, 256:384], strictL[:], ALU.mult, [psT, strictL])
    yield
    Xf = H.t32()
    H.STT("dve", Xf, Xf[:], psK[:, 128:256], negbeta[:, cs], Dst[:], ALU.mult, ALU.mult, [psK, negbeta, Dst])
    yield
    psX = H.pn()
    kb.tr(psX, psX[:, 384:512], Xf[:], I32[:], r=[Xf, I32])
    ev0 = H.tev()
    H.TT("dve", ev0, ev0[:, 0:128], Xf[:], bd[:], ALU.mult, [Xf, bd])
    yield
    XfT = H.t32()
    H.CP("act", XfT, XfT[:], psX[:, 384:512], [psX])
    yield
    H.TT("dve", ev0, ev0[:, 256:384], XfT[:], bd[:], ALU.mult, [XfT, bd])
    H.TT("pool", LoffTn, LoffTn[:], XfT[:], offT[:], ALU.mult, [XfT, offT])
    psQ = H.pn()
    H.MM(psQ, psQ[:, 0:128], [(kTc[:, hs], qTc[:, hs])], [kTc, qTc])
    yield
    H.TT("dve", attnT, attnT[:], psQ[:, 0:128], DTi[:], ALU.mult, [psQ, DTi])
    yield
    TTt = yield from inverse(H, ev0, LoffTn, I32)
    psS = H.pn()

    def fUW(e):
        e.matmul(psS[:, 0:128], lhsT=TTt[:], rhs=rhs[:, 0:128], start=True, stop=True)
        return e.matmul(psS[:, 256:384], lhsT=rhs[:, 128:256], rhs=TTt[:], start=True, stop=True)
    kb.op("pe", fUW, r=[TTt, rhs], w=[psS])
    yield
    H.CP("act", u32, u32[:], psS[:, 0:128], [psS])
    H.CP("dve", wT16, wT16[:], psS[:, 256:384], [psS])
    yield


def gdn_scan(kb, H, n, i, attnT, qdecT, wT16, kdec16, u32, S32, S16, dch, osb, tm, gnw, mx, junk):
    B = H.pn()
    hc = lambda h: slice(h * 128, (h + 1) * 128)

    def f1(e):
        ins = None
        for h in range(4):
            ins = e.matmul(B[:, hc(h)], lhsT=wT16[h][:], rhs=S16[h][:], start=True, stop=True)
        return ins
    kb.op("pe", f1, r=wT16 + S16, w=[B])
    yield
    vn = []
    for h in range(4):
        v_ = H.t16()
        H.TT("dve", v_, v_[:], u32[h][:], B[:, hc(h)], ALU.subtract, [u32[h], B])
        vn.append(v_)
    yield

    def f2(e):
        ins = None
        for h in range(4):
            e.matmul(B[:, hc(h)], lhsT=qdecT[h][:], rhs=S16[h][:], start=True, stop=False)
            ins = e.matmul(B[:, hc(h)], lhsT=attnT[h][:], rhs=vn[h][:], start=False, stop=True)
        return ins
    kb.op("pe", f2, r=qdecT + S16 + attnT + vn, w=[B])
    yield
    H.CP("act", osb, osb[:, 0:512], B[:, 0:512], [B])
    yield

    def f3(e):
        ins = None
        for h in range(4):
            ins = e.matmul(B[:, hc(h)], lhsT=kdec16[h][:], rhs=vn[h][:], start=True, stop=True)
        return ins
    kb.op("pe", f3, r=kdec16 + vn, w=[B])
    yield
    for h in range(4):
        col = n * 4 + h
        H.STT("dve", S32[h], S32[h][:], S32[h][:], dch[:, col:col + 1], B[:, hc(h)], ALU.mult, ALU.add, [S32[h], dch, B])
    yield
    for h in range(4):
        H.CP("act" if h % 2 == 0 else "pool", S16[h], S16[h][:], S32[h][:], [S32[h]])
    yield
    st = kb.gst[i]
    for h in range(4):
        kb.op("act", lambda e, h=h: e.activation(out=junk[:], in_=osb[:, h * 128:(h + 1) * 128], func=AF.Square, accum_out=st[:, h:h + 1]), r=[osb], w=[junk, st])
    yield
    H.ACT(st, st[:, 4:8], st[:, 0:4], AF.Ln, [st], bias=kb.epsc[1e-6][:], scale=1.0 / 128)
    yield
    H.ACT(st, st[:, 4:8], st[:, 4:8], AF.Exp, [st], scale=-0.5)
    sz = kb.gsz[i]
    H.ACT(sz, sz[:], tm[:, 0:512], AF.Silu, [tm])
    yield
    for h in range(4):
        H.TT("pool", sz, sz[:, h * 128:(h + 1) * 128], sz[:, h * 128:(h + 1) * 128], gnw[:], ALU.mult, [sz, gnw])
    yield
    for h in range(4):
        H.STT("dve", mx, mx[:, h * 128:(h + 1) * 128], osb[:, h * 128:(h + 1) * 128], st[:, 4 + h:5 + h], sz[:, h * 128:(h + 1) * 128], ALU.mult, ALU.mult, [osb, st, sz])
    yield


def rwkv_pre(kb, H, l, i, rwt, lora, v12):
    sl = lambda j: slice(j * 128, (j + 1) * 128)
    tw_ = kb.rwtw[i]
    H.ACT(tw_, tw_[0:32, :], rwt[0:32, sl(6)], AF.Tanh, [rwt])
    H.ACT(tw_, tw_[64:128, :], rwt[64:128, sl(6)], AF.Sigmoid, [rwt])
    gtok = kb.rwg[i]
    psg = kb.pn()
    H.MM(psg, psg[:, 0:256], [(tw_[64:128, :], lora[64:128, 0:256])], [tw_, lora])
    H.CP("act", gtok, gtok[:], psg[:, 0:256], [psg])
    if l == 1:
        v1, v2 = v12
        psv = kb.pn()
        H.MM(psv, psv[0:32, 0:128], [(v1[:, 0:32], rwt[:, sl(4)]), (v1[:, 32:64], rwt[:, sl(5)])], [v1, rwt])
        vl = kb.rwvl[i]
        H.CP("act", vl, vl[0:32, :], psv[0:32, 0:128], [psv])


def rwkv_pair(kb, H, C, A, l, n, i, j, rwt, vft, lora, rwc, omka, sel, cm, v12):
    triU, strictL, negmT, bd, offm, offT, strictT, ones32, negones = cm
    I32, I16 = C["ident32"], C["ident16"]
    ts = slice(n * 128, (n + 1) * 128)
    sl = lambda q: slice(q * 128, (q + 1) * 128)
    tw_ = kb.rwtw[i]
    vl = kb.rwvl[i]
    tk, ar, bk = kb.rwtok[i][j], kb.rwar[i][j], kb.rwbk[i][j]
    v32, gcol, bs, bcol = kb.rwv32[i], kb.rwgc[i], kb.rwbs[i], kb.rwbc[i]
    ch = sl(j)
    rT, kT_, vT = rwt[:, sl(j)], rwt[:, sl(2 + j)], rwt[:, sl(4 + j)]
    psw = H.pn()
    H.MM(psw, psw[:, 0:128], [(lora[0:32, ch], tw_[0:32, :])], [lora, tw_])
    psa = H.pn()
    H.MM(psa, psa[:, 0:128], [(lora[32:64, ch], rwt[32:64, sl(6)])], [lora, rwt])
    kk0 = H.t32()
    H.TS("act", kk0, kk0[:], kT_, rwc[:, 4 + j:5 + j], ALU.mult, [rwt, rwc])
    yield
    sgd = H.t32()
    H.ACT(sgd, sgd[:], psw[:, 0:128], AF.Sigmoid, [psw, rwc], bias=rwc[:, j:j + 1])
    sq = H.t32()
    H.TT("dve", sq, sq[:], kk0[:], kk0[:], ALU.mult, [kk0])
    yield
    aa = H.t32()
    H.ACT(aa, aa[:], psa[:, 0:128], AF.Sigmoid, [psa, rwc], bias=rwc[:, 2 + j:3 + j])
    yield
    pst = H.pn()
    kb.tr(pst, pst[:, 0:128], sgd[:], I32[:], r=[sgd, I32])
    yield
    sgtok = H.t32()
    H.CP("act", sgtok, sgtok[:], pst[:, 0:128], [pst])
    psq = H.pn()
    H.MM(psq, psq[:, 0:128], [(bd[:], sq[:])], [bd, sq])
    yield
    rs = H.t32()
    H.ACT(rs, rs[:], psq[:, 0:128], AF.Ln, [psq], bias=kb.epsc[1e-6][:])
    yield
    psc = H.pn()
    H.MM(psc, psc[:, 0:128], [(sgtok[:], triU[:])], [sgtok, triU])
    H.ACT(rs, rs[:], rs[:], AF.Exp, [rs], scale=-0.5)
    yield
    kk = H.t32()
    H.TT("dve", kk, kk[:], kk0[:], rs[:], ALU.mult, [kk0, rs])
    tka = H.t32()
    H.TS("dve", tka, tka[:], aa[:], rwc[:, 6 + j:7 + j], ALU.mult, [aa, rwc, omka], s2=omka[:, j:j + 1], op1=ALU.add)
    yield
    Scp = H.t32()
    H.CP("dve", Scp, Scp[:], psc[:, 0:128], [psc])
    yield
    E1, Einv, E1x, Ehat, Sx = H.t32(), H.t32(), H.t32(), H.t32(), H.t32()
    H.ACT(E1, E1[:], psc[:, 0:128], AF.Exp, [psc], scale=WSC)
    H.ACT(Einv, Einv[:], psc[:, 0:128], AF.Exp, [psc], scale=-WSC)
    H.TT("dve", Sx, Sx[:], Scp[:], sgd[:], ALU.subtract, [Scp, sgd])
    H.TS("dve", bcol, bcol[:, j:j + 1], Scp[:, 127:128], WSC, ALU.mult, [Scp])
    yield
    kmod = H.t32()
    H.TT("pool", kmod, kmod[:], kT_, tka[:], ALU.mult, [rwt, tka])
    bp = H.t32()
    H.TT("pool", bp, bp[:], kk[:], aa[:], ALU.mult, [kk, aa])
    H.ACT(E1x, E1x[:], Sx[:], AF.Exp, [Sx], scale=WSC)
    H.ACT(Ehat, Ehat[:], Scp[:], AF.Exp, [Scp, bcol], scale=-WSC, bias=bcol[:, j:j + 1])
    yield
    if l == 1:
        v1, v2 = v12
        psv2 = H.pn()
        H.MM(psv2, psv2[:, 0:128], [(v2[0:32, ch], vl[0:32, :])], [v2, vl])
        yield
        sgv = H.t32()
        H.ACT(sgv, sgv[:], psv2[:, 0:128], AF.Sigmoid, [psv2, rwc], bias=rwc[:, 10 + j:11 + j])
        dv = H.t32()
        H.TT("dve", dv, dv[:], vft[:, ch], vT, ALU.subtract, [vft, rwt])
        yield
        H.TT("pool", dv, dv[:], dv[:], sgv[:], ALU.mult, [dv, sgv])
        yield
        vu = H.t32()
        H.TT("pool", vu, vu[:], dv[:], vT, ALU.add, [dv, rwt])
        yield
        vuse, vub = vu[:], vu
    else:
        kb.dma("sp", A["vfT"][ch, ts], vT, r=[rwt])
        vuse, vub = vT, rwt
    H.CP("act", gcol, gcol[:, j:j + 1], E1[:, 127:128], [E1])
    H.STT("dve", ar, ar[:, 0:128], kk[:], -1.0, E1x[:], ALU.mult, ALU.mult, [kk, E1x])
    H.TT("pool", ar, ar[:, 128:256], rT, E1[:], ALU.mult, [rwt, E1])
    yield
    bhT, khT, v16T = H.t16(), H.t16(), H.t16()
    H.TT("dve", bk, bk[:, 0:128], bp[:], Einv[:], ALU.mult, [bp, Einv])
    H.TT("pool", bk, bk[:, 128:256], kmod[:], Einv[:], ALU.mult, [kmod, Einv])
    yield
    H.TT("dve", bhT, bhT[:], bp[:], Ehat[:], ALU.mult, [bp, Ehat])
    H.TT("pool", khT, khT[:], kmod[:], Ehat[:], ALU.mult, [kmod, Ehat])
    yield
    H.CP("act", v16T, v16T[:], vuse, [vub])
    rk_ = H.t32()
    H.STT("dve", rk_, rk_[:], rT, rwc[:, 8 + j:9 + j], kmod[:], ALU.mult, ALU.mult, [rwt, rwc, kmod])
    yield
    ps4 = H.pn()
    psb4 = ps4[:].bitcast(BF16)
    srcs = [bhT, khT, v16T]

    def trf(e):
        ins = None
        for q, s_ in enumerate(srcs):
            ins = e.transpose(psb4[:, q * 128:(q + 1) * 128], s_[:], I16[:])
        ins = e.transpose(psb4[:, 384:512], ar[:, 0:128], I16[:])
        return ins
    kb.op("pe", trf, r=srcs + [ar, I16], w=[ps4])
    yield
    H.CP("act", tk, tk[:], psb4[:, 0:512], [ps4])
    ps5 = H.pn()
    kb.tr(ps5, ps5[:, 0:128], vuse, I32[:], r=[vub, I32])
    yield
    H.CP("dve", v32, v32[:, ch], ps5[:, 0:128], [ps5])
    yield
    ps6 = H.pn()
    H.MM(ps6, ps6[:, 0:2], [(rk_[:], sel[:])], [rk_, sel])
    yield
    H.CP("dve", bs, bs[:, 2 * j:2 * j + 2], ps6[:, 0:2], [ps6])
    yield


def rwkv_head(kb, H, LoffTn, I32, i, j, hh, cm):
    triU, strictL, negmT, bd, offm, offT, strictT, ones32, negones = cm
    tk, ar, bk = kb.rwtok[i][j], kb.rwar[i][j], kb.rwbk[i][j]
    PaT, Qv = kb.rwpa[i][j], kb.rwqv[i][j]
    pb = hh * 64
    hg = 2 * j + hh
    At = kb.rwA[i][hg]
    btT = bk[pb:pb + 64, 0:128]
    ktT = bk[pb:pb + 64, 128:256]
    psA3 = H.pn()
    H.MM(psA3, psA3[:, 0:128], [(ar[pb:pb + 64, 0:128], btT)], [bk, ar])
    yield
    Xf = H.t32()
    H.TT("dve", Xf, Xf[:], psA3[:, 0:128], strictL[:], ALU.mult, [psA3, strictL])
    yield
    psA1 = H.pn()
    H.MM(psA1, psA1[:, 128:384], [(btT, ar[pb:pb + 64, 0:256])], [bk, ar])
    ev0 = H.tev()
    H.TT("dve", ev0, ev0[:, 0:128], Xf[:], bd[:], ALU.mult, [Xf, bd])
    yield
    XfT = H.t32()
    H.TT("dve", XfT, XfT[:], psA1[:, 128:256], strictT[:], ALU.mult, [psA1, strictT])
    H.TT("dve", At, At[:, 0:128], psA1[:, 256:384], triU[:], ALU.mult, [psA1, triU])
    yield
    psA2 = H.pn()
    H.MM(psA2, psA2[:, 0:256], [(ktT, ar[pb:pb + 64, 0:256])], [bk, ar])
    H.TT("dve", ev0, ev0[:, 256:384], XfT[:], bd[:], ALU.mult, [XfT, bd])
    H.TT("pool", LoffTn, LoffTn[:], XfT[:], offT[:], ALU.mult, [XfT, offT])
    yield
    H.TT("dve", At, At[:, 128:256], psA2[:, 128:256], triU[:], ALU.mult, [psA2, triU])
    AakT = H.t16()
    H.TT("dve", AakT, AakT[:], psA2[:, 0:128], strictT[:], ALU.mult, [psA2, strictT])
    yield
    TTt = yield from inverse(H, ev0, LoffTn, I32)
    psAV = H.pn()
    H.MM(psAV, psAV[:, 0:64], [(AakT[:], tk[:, 256 + pb:256 + pb + 64])], [AakT, tk])
    yield
    AVs = H.t32()
    H.CP("act", AVs, AVs[:, 0:64], psAV[:, 0:64], [psAV])
    yield
    TT16 = H.t16()
    H.CP("act", TT16, TT16[:], TTt[:], [TTt])
    psQv = H.pn()
    H.MM(psQv, psQv[:, 64:128], [(TTt[:], AVs[:, 0:64])], [TTt, AVs])
    yield
    H.CP("dve", Qv, Qv[:, pb:pb + 64], psQv[:, 64:128], [psQv])
    yield
    psP = H.pn()
    H.MM(psP, psP[:, 128:256], [(tk[:, 384:512], TT16[:])], [tk, TT16])
    yield
    H.CP("act", PaT, PaT[pb:pb + 64, :], psP[pb:pb + 64, 128:256], [psP])
    yield


def rwkv_tail(kb, H, A, i, ts, bd, W32, W16, ysb, lnw, lnb, mx):
    tokm, PaT, Qv, arT, AT, gcol = kb.rwtok[i], kb.rwpa[i], kb.rwqv[i], kb.rwar[i], kb.rwA[i], kb.rwgc[i]
    B = H.pn()
    jc = lambda j: slice(j * 128, (j + 1) * 128)

    def fU(e):
        ins = None
        for j in range(2):
            ins = e.matmul(B[:, jc(j)], lhsT=PaT[j][:], rhs=W16[j][:], start=True, stop=True)
        return ins
    kb.op("pe", fU, r=PaT + W16, w=[B])
    yield
    U16 = []
    for j in range(2):
        u_ = H.t16()
        H.TT("dve", u_, u_[:], B[:, jc(j)], Qv[j][:], ALU.add, [B, Qv[j]])
        U16.append(u_)
    yield

    def fY(e):
        ins = None
        for j in range(2):
            tk = tokm[j]
            for hh in range(2):
                hg = 2 * j + hh
                es = slice(hh * 64, (hh + 1) * 64)
                o = B[:, 256 + hg * 64:256 + (hg + 1) * 64]
                e.matmul(o, lhsT=arT[j][:, 128:256], rhs=W16[j][:, es], start=True, stop=False)
                e.matmul(o, lhsT=AT[hg][:, 0:128], rhs=U16[j][:, es], start=False, stop=False)
                ins = e.matmul(o, lhsT=AT[hg][:, 128:256], rhs=tk[:, 256 + hh * 64:256 + (hh + 1) * 64], start=False, stop=True)
        return ins
    kb.op("pe", fY, r=arT + W16 + AT + U16 + tokm, w=[B])
    yield
    H.CP("act", ysb, ysb[:], B[:, 256:512], [B])
    yield

    def w_update():
        def fH(e):
            ins = None
            for j in range(2):
                tk = tokm[j]
                e.matmul(B[:, jc(j)], lhsT=tk[:, 0:128], rhs=U16[j][:], start=True, stop=False)
                ins = e.matmul(B[:, jc(j)], lhsT=tk[:, 128:256], rhs=tk[:, 256:384], start=False, stop=True)
            return ins
        kb.op("pe", fH, r=tokm + U16, w=[B])
        yield
        tmps = []
        for j in range(2):
            tmpH = H.t32()
            H.TT("dve", tmpH, tmpH[:], B[:, jc(j)], bd[:], ALU.mult, [B, bd])
            tmps.append(tmpH)
        yield
        for j in range(2):
            H.STT("dve", W32[j], W32[j][:], W32[j][:], gcol[:, j:j + 1], tmps[j][:], ALU.mult, ALU.add, [W32[j], gcol, tmps[j]])
        yield
        for j in range(2):
            H.CP("act", W16[j], W16[j][:], W32[j][:], [W32[j]])
        yield

    def out_chain():
        st, v32, bs, gtok = kb.wst[i], kb.rwv32[i], kb.rwbs[i], kb.rwg[i]
        kb.op("dve", lambda e: e.tensor_reduce(out=st[:, 0:4], in_=ysb[:].rearrange("p (h d) -> p h d", d=64), axis=AX.X, op=ALU.add), r=[ysb], w=[st])
        sq2 = H.tw()
        H.TT("pool", sq2, sq2[:], ysb[:], ysb[:], ALU.mult, [ysb])
        yield
        kb.op("dve", lambda e: e.tensor_reduce(out=st[:, 4:8], in_=sq2[:].rearrange("p (h d) -> p h d", d=64), axis=AX.X, op=ALU.add), r=[sq2], w=[st])
        yield
        H.TS("dve", st, st[:, 0:4], st[:, 0:4], 1.0 / 64, ALU.mult, [st])
        yield
        H.TT("dve", st, st[:, 8:12], st[:, 0:4], st[:, 0:4], ALU.mult, [st])
        yield
        H.STT("dve", st, st[:, 4:8], st[:, 4:8], 1.0 / 64, st[:, 8:12], ALU.mult, ALU.subtract, [st])
        yield
        H.ACT(st, st[:, 4:8], st[:, 4:8], AF.Ln, [st], bias=kb.epsc[64e-5][:])
        yield
        H.ACT(st, st[:, 4:8], st[:, 4:8], AF.Exp, [st], scale=-0.5)
        yield
        H.STT("dve", st, st[:, 8:12], st[:, 0:4], -1.0, st[:, 4:8], ALU.mult, ALU.mult, [st])
        yield
        yn = H.tw()
        for h in range(4):
            hs = slice(h * 64, (h + 1) * 64)
            H.TS("dve", yn, yn[:, hs], ysb[:, hs], st[:, 4 + h:5 + h], ALU.mult, [ysb, st], s2=st[:, 8 + h:9 + h], op1=ALU.add)
        yield
        H.TT("pool", yn, yn[:], yn[:], lnw[:], ALU.mult, [yn, lnw])
        yield
        H.TT("pool", yn, yn[:], yn[:], lnb[:], ALU.add, [yn, lnb])
        yield
        for h in range(4):
            hs = slice(h * 64, (h + 1) * 64)
            H.STT("dve", yn, yn[:, hs], v32[:, hs], bs[:, h:h + 1], yn[:, hs], ALU.mult, ALU.add, [v32, bs, yn])
        yield
        H.TT("dve", mx, mx[:, 768:1024], yn[:], gtok[:], ALU.mult, [yn, gtok])
        yield
        kb.dma("sp", A["mix"][ts, :], mx[:], r=[mx])

    yield from zip_gens(w_update(), out_chain())


def phase3a(kb, A, l, xsrc):
    kb.psum_pool(8)
    make_consts(kb, A, [1e-6])
    ident16 = kb.sb([128, 128], BF16)
    kb.dma("pool", ident16[:], A["ident"], w=[ident16])
    wout = kb.sb([128, 8 * 1024], BF16)
    for k in range(8):
        kb.dma("pool", wout[:, k * 1024:(k + 1) * 1024], A["wout"][l, :, k * 1024:(k + 1) * 1024], w=[wout])
    bc = {}
    for nm, i in (("g1", 2), ("w2", 3), ("s2", 4)):
        bc[nm] = kb.sb([128, 1024], F32, nm)
        kb.dma("sp", bc[nm][:], A["modrow"][l, 0:1, i * 1024:(i + 1) * 1024].partition_broadcast(128), w=[bc[nm]])
    for k in range(8):
        kb.op("dve" if k % 2 == 0 else "pool", lambda e, k=k: e.tensor_tensor(out=wout[:, k * 1024:(k + 1) * 1024], in0=wout[:, k * 1024:(k + 1) * 1024],
                                                                            in1=bc["g1"][:], op=ALU.mult), r=[wout, bc["g1"]], w=[wout])
    junk = kb.sb([128, 1024], BF16)
    slots = [dict(b0=2 * s, b1=2 * s + 1, mx=kb.sb([128, 1024], BF16, "mx"), xt=kb.sb([128, 1024], F32, "xt"), mT=kb.sb([128, 1024], BF16, "mT"),
                  t1=kb.sb([128, 1024], F32, "t1"), xm=kb.sb([128, 1024], F32, "xm"), ss=kb.sb([128, 1], F32), rstd=kb.sb([128, 1], F32),
                  xn=kb.sb([128, 1024], BF16, "xn"), hT=kb.sb([128, 1024], BF16, "hT")) for s in range(4)]

    def tile(S, n):
        mx, xt, mT, t1, xm, ss, rstd, xn, hT = (S[k] for k in ("mx", "xt", "mT", "t1", "xm", "ss", "rstd", "xn", "hT"))
        pa, pb = kb.pstiles[S["b0"]], kb.pstiles[S["b1"]]
        t0 = n * 128
        kb.dma("sp", mx[:], A["mix"][t0:t0 + 128, :], w=[mx])
        kb.dma("sp", xt[:], xsrc[t0:t0 + 128, :], w=[xt])
        yield
        psb = pa[:].bitcast(BF16)

        def trf(e):
            ins = None
            for k in range(8):
                ins = e.transpose(psb[:, k * 128:(k + 1) * 128], mx[:, k * 128:(k + 1) * 128], ident16[:])
            return ins
        kb.op("pe", trf, r=[mx, ident16], w=[pa])
        yield
        kb.op("act", lambda e: e.copy(out=mT[:], in_=psb), r=[pa], w=[mT])
        yield
        for hf, ps2 in ((0, pb), (1, pa)):
            hs = slice(hf * 512, (hf + 1) * 512)
            kb.mm(ps2, ps2[:, 0:512], [(mT[:, k * 128:(k + 1) * 128], wout[:, k * 1024 + hf * 512:k * 1024 + (hf + 1) * 512]) for k in range(8)], r=[mT, wout])
            yield
            kb.op("dve", lambda e, ps2=ps2, hs=hs: e.tensor_tensor(out=xm[:, hs], in0=ps2[:, 0:512], in1=xt[:, hs], op=ALU.add), r=[ps2, xt], w=[xm])
            yield
        kb.dma("sp", A["xmid"][t0:t0 + 128, :], xm[:], r=[xm])
        kb.op("act", lambda e: e.activation(out=junk[:], in_=xm[:], func=AF.Square, accum_out=ss[:]), r=[xm], w=[junk, ss])
        yield
        kb.op("act", lambda e: e.activation(out=rstd[:], in_=ss[:], func=AF.Sqrt, scale=1.0 / 1024, bias=kb.epsc[1e-6][:]), r=[ss], w=[rstd])
        yield
        kb.op("dve", lambda e: e.reciprocal(out=rstd[:], in_=rstd[:]), r=[rstd], w=[rstd])
        yield
        kb.op("dve", lambda e: e.scalar_tensor_tensor(out=t1[:], in0=xm[:], scalar=rstd[:, 0:1], in1=bc["w2"][:], op0=ALU.mult, op1=ALU.mult),
              r=[xm, rstd, bc["w2"]], w=[t1])
        yield
        kb.op("pool", lambda e: e.tensor_tensor(out=xn[:], in0=t1[:], in1=bc["s2"][:], op=ALU.add), r=[t1, bc["s2"]], w=[xn])
        yield
        psb3 = pb[:].bitcast(BF16)

        def trf3(e):
            ins = None
            for k in range(8):
                ins = e.transpose(psb3[:, k * 128:(k + 1) * 128], xn[:, k * 128:(k + 1) * 128], ident16[:])
            return ins
        kb.op("pe", trf3, r=[xn, ident16], w=[pb])
        yield
        kb.op("act", lambda e: e.copy(out=hT[:], in_=psb3), r=[pb], w=[hT])
        yield
        kb.dma("sp", A["h2T"][:, :, t0:t0 + 128].rearrange("k p t -> p k t"), hT[:].rearrange("p (k t) -> p k t", k=8), r=[hT])

    Pipe({"s": slots}).run([("s", (lambda S, n=n: tile(S, n))) for n in range(NCH)])
    kb.end_phase()


def phase3b(kb, A, l, xdst, final):
    kb.psum_pool(8)
    make_consts(kb, A, [1e-6])
    wup = [kb.sb([128, 4096], BF16, "wup") for _ in range(8)]
    for k in range(8):
        kb.dma("pool", wup[k][:], A["wup"][l, :, k * 4096:(k + 1) * 4096], w=[wup[k]])
    wdn = [kb.sb([128, 4096], BF16, "wdn") for _ in range(8)]
    for k in range(8):
        kb.dma("pool", wdn[k][:], A["wdn"][l, :, k * 4096:(k + 1) * 4096], w=[wdn[k]])
    g2 = kb.sb([128, 1024], F32)
    kb.dma("sp", g2[:], A["modrow"][l, 0:1, 5 * 1024:6 * 1024].partition_broadcast(128), w=[g2])
    if final:
        fnw = kb.sb([128, 1024], F32)
        kb.dma("sp", fnw[:], A["fnw"][0:1, :].partition_broadcast(128), w=[fnw])
    TB = 512
    hT = kb.sb([128, 8 * TB], BF16, "hT")
    uT = kb.sb([128, 32 * TB], BF16, "uT")
    rl = [kb.sb([128, TB], F32, "rl") for _ in range(2)]
    xm = [kb.sb([128, 1024], F32, "xm") for _ in range(2)]
    t1 = [kb.sb([128, 512], F32, "t1") for _ in range(2)]
    xo = [kb.sb([128, 1024], F32, "xo") for _ in range(2)]
    junk = kb.sb([128, 1024], BF16)
    ss = [kb.sb([128, 1], F32) for _ in range(2)]
    rstd = [kb.sb([128, 1], F32) for _ in range(2)]
    ri = 0
    ti = 0
    h, u = hT, uT
    NBLK = T // TB

    def load_h(blk):
        kb.dma("sp", h[:].rearrange("p (k t) -> p k t", k=8), A["h2T"][:, :, blk * TB:(blk + 1) * TB].rearrange("k p t -> p k t"), w=[h])
    load_h(0)
    for blk in range(NBLK):
        t0 = blk * TB
        for f in range(32):
            ps = kb.pn()
            kb.mm(ps, ps[:, 0:TB], [(wup[k][:, f * 128:(f + 1) * 128], h[:, k * TB:(k + 1) * TB]) for k in range(8)], r=[h] + wup)
            r_ = rl[ri % 2]
            ri += 1
            kb.op("act", lambda e, ps=ps, r_=r_: e.activation(out=r_[:], in_=ps[:, 0:TB], func=AF.Relu), r=[ps], w=[r_])
            eng = "pool" if f % 3 == 0 else "dve"
            kb.op(eng, lambda e, r_=r_, f=f: e.tensor_tensor(out=u[:, f * TB:(f + 1) * TB], in0=r_[:], in1=r_[:], op=ALU.mult), r=[r_], w=[u])
        if blk + 1 < NBLK:
            load_h(blk + 1)
        for j in range(TB // 128):
            i = ti % 2
            ti += 1
            tt = t0 + j * 128
            kb.dma("sp", xm[i][:], A["xmid"][tt:tt + 128, :], w=[xm[i]])
            for hf in range(2):
                ps2 = kb.pn()
                kb.mm(ps2, ps2[:, 0:512], [(u[:, f * TB + j * 128:f * TB + (j + 1) * 128], wdn[f // 4][:, (f % 4) * 1024 + hf * 512:(f % 4) * 1024 + (hf + 1) * 512]) for f in range(32)], r=[u] + wdn)
                kb.op("dve", lambda e, i=i, ps2=ps2, hf=hf: e.tensor_tensor(out=t1[i][:], in0=ps2[:, 0:512], in1=g2[:, hf * 512:(hf + 1) * 512], op=ALU.mult), r=[ps2, g2], w=[t1[i]])
                kb.op("pool", lambda e, i=i, hf=hf: e.tensor_tensor(out=xo[i][:, hf * 512:(hf + 1) * 512], in0=t1[i][:], in1=xm[i][:, hf * 512:(hf + 1) * 512], op=ALU.add), r=[t1[i], xm[i]], w=[xo[i]])
            if not final:
                kb.dma("sp", xdst[tt:tt + 128, :], xo[i][:], r=[xo[i]])
            else:
                rms_rstd(kb, xo[i], junk, ss[i], rstd[i], 1e-6, 1024)
                kb.op("dve", lambda e, i=i: e.scalar_tensor_tensor(out=xm[i][:], in0=xo[i][:], scalar=rstd[i][:, 0:1], in1=fnw[:], op0=ALU.mult, op1=ALU.mult),
                      r=[xo[i], rstd[i], fnw], w=[xm[i]])
                kb.dma("sp", xdst[tt:tt + 128, :], xm[i][:], r=[xm[i]])
    kb.end_phase()


SHARED = ["adaw", "adab", "n1w", "n2w", "fnw", "win", "convw", "alog", "dtb", "gnw", "mu", "lora", "rwcol",
          "lnw", "lnb", "v1", "v2", "wout", "wup", "wdn", "ident", "rope", "cmask", "retc"]


def declare(nc, dbg=()):
    A = {}

    def inp(name, shape, dt=F32):
        A[name] = nc.dram_tensor(name, list(shape), dt, kind="ExternalInput").ap()

    def scr(name, shape, dt=F32):
        kind = "ExternalOutput" if name in dbg else "Internal"
        A[name] = nc.dram_tensor(name, list(shape), dt, kind=kind).ap()
    inp("x", [T, D])
    inp("cT", [128, 8])
    inp("adaw", [2, 12, 128, 4096])
    inp("adab", [2, 1, 6144])
    inp("n1w", [2, 1, 1024])
    inp("n2w", [2, 1, 1024])
    inp("fnw", [1, 1024])
    inp("win", [2, 128, 8 * DIN])
    inp("convw", [2, 128, 48])
    inp("alog", [2, 1, 4])
    inp("dtb", [2, 1, 4])
    inp("gnw", [2, 1, 128])
    inp("mu", [2, 128, 7])
    inp("lora", [2, 128, 256])
    inp("rwcol", [2, 128, 12])
    inp("lnw", [2, 1, 256])
    inp("lnb", [2, 1, 256])
    inp("v1", [128, 64])
    inp("v2", [32, 256])
    inp("wout", [2, 128, 8 * 1024])
    inp("wup", [2, 128, 8 * 4096])
    inp("wdn", [2, 128, 32 * 1024])
    inp("ident", [128, 128])
    inp("rope", [T, 64])
    inp("cmask", [10, 128, 128])
    inp("retc", [128, 1026])
    A["out"] = nc.dram_tensor("out", [T, D], F32, kind="ExternalOutput").ap()
    scr("modrow", [2, 1, 6144])
    scr("qT", [2, 4, 128, T], BF16)
    scr("kT", [2, 4, 128, T], BF16)
    scr("ktok", [2, T, 512])
    scr("vtok", [2, T, 512])
    scr("rwT", [2, 896, T])
    scr("tm", [2, T, 1544])
    scr("vfT", [256, T])
    scr("mix", [T, 1024], BF16)
    scr("xmid", [T, D])
    scr("h2T", [8, 128, T], BF16)
    scr("xres", [T, D])
    return A


def build(upto=99, dbg=()):
    nc = bass.Bass("TRN2", target_bir_lowering=False)
    A = declare(nc, dbg)
    kb = KB(nc)
    step = 0

    def go():
        nonlocal step
        step += 1
        return step <= upto
    if go():
        phase0(kb, A)
    for l in range(2):
        xsrc = A["x"] if l == 0 else A["xres"]
        if go():
            phase1(kb, A, l, xsrc)
        if go():
            phase2(kb, A, l)
        if go():
            phase3a(kb, A, l, xsrc)
        if go():
            phase3b(kb, A, l, A["xres"] if l == 0 else A["out"], l == 1)
    kb.P.emit()
    return nc


def fm(v, nj):
    return np.ascontiguousarray(v.reshape(nj, 128).T)


def prep_shared(I):
    f = np.float32
    S = {}
    aw = I["ada_w"].reshape(2, 8, 128, 12, 512)
    S["adaw"] = np.ascontiguousarray(aw.transpose(0, 3, 2, 1, 4)).reshape(2, 12, 128, 4096)
    S["adab"] = I["ada_b"].reshape(2, 1, 6144)
    S["n1w"] = I["norm1_w"].reshape(2, 1, 1024)
    S["n2w"] = I["norm2_w"].reshape(2, 1, 1024)
    S["fnw"] = I["final_norm_w"].reshape(1, 1024)

    def kmaj(w, nk):
        L, _, N = w.shape
        return np.ascontiguousarray(w.reshape(L, nk, 128, N).transpose(0, 2, 1, 3)).reshape(L, 128, nk * N)
    S["win"] = kmaj(I["w_in"], 8)
    S["wout"] = kmaj(I["w_out"], 8)
    S["wup"] = kmaj(I["w_up"], 8)
    S["wdn"] = kmaj(I["w_down"], 32)
    cw = I["gdn_conv_w"]
    S["convw"] = np.ascontiguousarray(cw.reshape(2, 4, 12, 128).transpose(0, 3, 2, 1)).reshape(2, 128, 48)
    S["alog"] = I["gdn_a_log"].reshape(2, 1, 4)
    S["dtb"] = I["gdn_dt_bias"].reshape(2, 1, 4)
    S["gnw"] = I["gdn_norm_w"].reshape(2, 1, 128)
    S["mu"] = np.stack([fm(I["rwkv_mu"][l], 7) for l in range(2)])
    S["lora"] = np.ascontiguousarray(np.concatenate([I["rwkv_w2"], I["rwkv_a2"], I["rwkv_g2"]], axis=1))
    v0 = np.concatenate([np.zeros((1, 256), f), I["rwkv_v0"]], axis=0)
    S["rwcol"] = np.stack([np.concatenate([fm(I["rwkv_w0"][l], 2), fm(I["rwkv_a0"][l], 2), fm(I["rwkv_k_k"][l], 2),
                                           fm(I["rwkv_k_a"][l], 2), fm(I["rwkv_r_k"][l].reshape(256), 2), fm(v0[l], 2)], axis=1)
                           for l in range(2)])
    S["lnw"] = I["rwkv_ln_w"].reshape(2, 1, 256)
    S["lnb"] = I["rwkv_ln_b"].reshape(2, 1, 256)
    S["v1"] = np.ascontiguousarray(I["rwkv_v1"][0].reshape(2, 128, 32).transpose(1, 0, 2)).reshape(128, 64)
    S["v2"] = np.ascontiguousarray(I["rwkv_v2"][0])
    S["ident"] = np.eye(128, dtype=f)
    pos = np.arange(T, dtype=f)
    inv_freq = (np.float32(10000.0) ** (-np.arange(32, dtype=f) / np.float32(32))).astype(f)
    ang = (pos[:, None] * inv_freq[None, :]).astype(f)
    S["rope"] = np.concatenate([np.cos(ang), np.sin(ang)], axis=1).astype(f)
    i = np.arange(128)
    cm = np.zeros((10, 128, 128), f)
    cm[0] = (i[:, None] <= i[None, :])
    cm[1] = (i[:, None] > i[None, :])
    cm[2] = np.where(i[None, :] >= i[:, None], 0.0, NEGM)
    cm[3] = ((i[:, None] // 64) == (i[None, :] // 64))
    cm[4] = ((i[:, None] >= 64) & (i[None, :] < 64))
    cm[5] = cm[4].T
    cm[6] = (i[:, None] < i[None, :])
    cm[7] = 1.0
    cm[8] = -1.0
    S["cmask"] = cm
    rc = np.zeros((128, 1026), np.float64)
    for h in range(4):
        d = i[None, :] - i[:, None]
        rc[:, h * 128:(h + 1) * 128] = np.where(d >= 0, np.exp(np.maximum(d, 0) * LG[h]), 0.0)
    for pr in range(2):
        for p in range(128):
            hh = pr * 2 + p // 64
            rc[p, 512 + pr * 128:512 + (pr + 1) * 128] = np.exp((i + 1.0) * LG[hh])
    for h in range(4):
        rc[:, 768 + h * 64:768 + (h + 1) * 64] = (0.125 * np.exp((127.0 - i) * LG[h]))[:, None]
    for pr in range(2):
        for p in range(128):
            rc[p, 1024 + pr] = np.exp(128.0 * LG[pr * 2 + p // 64])
    S["retc"] = rc.astype(f)
    return {k: np.ascontiguousarray(v, dtype=f) for k, v in S.items()}


_NC = None


def kernel(**inputs):
    global _NC
    I = {k: np.asarray(v) for k, v in inputs.items()}
    S = prep_shared(I)
    if _NC is None:
        _NC = build()
    in_maps = []
    for b in range(8):
        m = dict(S)
        m["x"] = np.ascontiguousarray(I["x"][b])
        m["cT"] = fm(I["c"][b], 8)
        in_maps.append(m)
    res = run_bass_kernel_spmd(_NC, in_maps, core_ids=list(range(8)))
    return np.stack([np.asarray(r["out"]) for r in res.results]).astype(np.float32)
```
